# Optimizing a Trainium2 kernel written in Bass

```python
import math
import jax
import jax.numpy as jnp
from jax import lax
import numpy as np

D_MODEL = 2048
BATCH = 2
SEQ = 16384
DEPTH = 4

GRID_W = 64
CTX_LEN = 256

D_MIX = D_MODEL
W_LRU = D_MIX // 4
W_POOL = D_MIX // 4
W_FFT = D_MIX // 4
W_ATT = D_MIX // 4
LRU_BLOCKS = 8
LRU_BW = W_LRU // LRU_BLOCKS
LRU_CONV = 4
LRU_PAD = (2, 1)
LRU_C = 8.0
POOL_GROUPS = 4
POOL_GW = W_POOL // POOL_GROUPS
POOL_HALF = (1, 2, 4, 8)
FFT_HEADS = 4
FFT_HW = W_FFT // FFT_HEADS
ATT_HEADS = 4
ATT_DV = W_ATT // ATT_HEADS
ATT_DH = ATT_DV // 2
N_FREQ = ATT_DH // 4
ROPE_BASE = 10000.0
Q_BLOCK = 128
D_FF = 5632
FFN_CONV = 3
FFN_PAD = (1, 1)
N_MOD = 6
EPS = 1e-6
D_IN = 2 * W_LRU + W_POOL + W_FFT + 3 * W_ATT
SPLITS = (W_LRU, 2 * W_LRU, 2 * W_LRU + W_POOL, 2 * W_LRU + W_POOL + W_FFT,
          2 * W_LRU + W_POOL + W_FFT + W_ATT, 2 * W_LRU + W_POOL + W_FFT + 2 * W_ATT)
OFF_K = D_IN - 2 * W_ATT

kernel_name = 'hybrid_parallel_groups_diffusion_trunk'


def rmsnorm(x, g):
    xf = x.astype(jnp.float32)
    y = xf * lax.rsqrt(jnp.mean(xf * xf, axis=-1, keepdims=True) + EPS)
    return (y * g.astype(jnp.float32)).astype(x.dtype)


def modulate(h, shift, scale):
    return h * (1 + scale[:, None, :]) + shift[:, None, :]


def dwconv(x, w, b, pad):
    y = lax.conv_general_dilated(x, w[:, None, :].astype(x.dtype), window_strides=(1,), padding=[pad],
                                 dimension_numbers=('NWC', 'WIO', 'NWC'), feature_group_count=x.shape[-1])
    return y + b.astype(x.dtype)


def _lin_combine(e1, e2):
    a1, b1 = e1
    a2, b2 = e2
    return a1 * a2, a2 * b1 + b2


def rglru_scan(u, wa, ba, wi, bi, lam, h0, reverse):
    B_, L, _ = u.shape
    ub = u.reshape(B_, L, LRU_BLOCKS, LRU_BW)
    r = jax.nn.sigmoid(jnp.einsum('blnd,nde->blne', ub, wa).reshape(B_, L, W_LRU) + ba)
    i = jax.nn.sigmoid(jnp.einsum('blnd,nde->blne', ub, wi).reshape(B_, L, W_LRU) + bi)
    log_a = -LRU_C * r * jax.nn.softplus(-lam.astype(jnp.float32))
    a = jnp.exp(log_a)
    b = jnp.sqrt(-jnp.expm1(2.0 * log_a)) * (i * u)
    if h0 is not None:
        first = L - 1 if reverse else 0
        b = b.at[:, first].add(a[:, first] * h0)
    _, h = lax.associative_scan(_lin_combine, (a, b), axis=1, reverse=reverse)
    return h


def bidir_lru(ux, conv_w, conv_b, wa, ba, wi, bi, lam, h0f, h0b):
    u = dwconv(ux, conv_w, conv_b, LRU_PAD).astype(jnp.float32)
    hf = rglru_scan(u, wa[0], ba[0], wi[0], bi[0], lam[0], h0f, False)
    hb = rglru_scan(u, wa[1], ba[1], wi[1], bi[1], lam[1], h0b, True)
    return hf, hb


def pool_mixer(u, w_p, s_p):
    B_, L, _ = u.shape
    ug = u.astype(jnp.float32).reshape(B_, L, POOL_GROUPS, POOL_GW)
    cs = jnp.concatenate([jnp.zeros((B_, 1, POOL_GROUPS, POOL_GW), jnp.float32), jnp.cumsum(ug, axis=1)], axis=1)
    t = jnp.arange(L)[:, None]
    half = jnp.array(POOL_HALF, dtype=jnp.int32)[None, :]
    lo = jnp.clip(t - half, 0, L - 1)
    hi = jnp.clip(t + half - 1, 0, L - 1)

    def window_sum(cs_g, lo_g, hi_g):
        return cs_g[:, hi_g + 1] - cs_g[:, lo_g]

    s = jax.vmap(window_sum, in_axes=(2, 1, 1), out_axes=2)(cs, lo, hi)
    cnt = (hi - lo + 1).astype(jnp.float32)[None, :, :, None]
    d = s / cnt - ug
    y = jnp.einsum('blgc,gce->blge', d, w_p) * s_p.reshape(POOL_GROUPS, POOL_GW)
    return y.reshape(B_, L, W_POOL)


def fourier_mixer(u, w_f):
    B_, L, _ = u.shape
    uf = u.astype(jnp.float32).reshape(B_, L, FFT_HEADS, FFT_HW)
    y = jnp.fft.fftn(uf, axes=(1, 3), norm='ortho').real.reshape(B_, L, W_FFT)
    return y @ w_f


def qk_heads(t):
    return t.reshape(t.shape[0], t.shape[1], ATT_HEADS, 2, ATT_DH).astype(jnp.float32)


def v_heads(t):
    return t.reshape(t.shape[0], t.shape[1], ATT_HEADS, ATT_DV).astype(jnp.float32)


def _rot(x, ang):
    x1, x2 = jnp.split(x, 2, axis=-1)
    cos = jnp.cos(ang)[None, :, None, None, :]
    sin = jnp.sin(ang)[None, :, None, None, :]
    return jnp.concatenate([x1 * cos - x2 * sin, x2 * cos + x1 * sin], axis=-1)


def axial_rope(x, ang_r, ang_c):
    half = ATT_DH // 2
    return jnp.concatenate([_rot(x[..., :half], ang_r), _rot(x[..., half:], ang_c)], axis=-1)


def diff_attention(q, k, v, lam):
    B_, Lq = q.shape[0], q.shape[1]
    nb = Lq // Q_BLOCK
    qb = jnp.moveaxis(q.reshape(B_, nb, Q_BLOCK, ATT_HEADS, 2, ATT_DH), 1, 0)

    def block(qi):
        s = jnp.einsum('bqhjd,bkhjd->bhjqk', qi, k) * (ATT_DH ** -0.5)
        p = jax.nn.softmax(s, axis=-1)
        w = p[:, :, 0] - lam * p[:, :, 1]
        return jnp.einsum('bhqk,bkhe->bqhe', w, v)

    o = lax.map(block, qb)
    return jnp.moveaxis(o, 0, 1).reshape(B_, Lq, ATT_HEADS, ATT_DV)


def merge_groups(y_lru, y_pool, y_fft, o_att, g_sub, lam_init, w_o, dt):
    B_, L = y_pool.shape[0], y_pool.shape[1]
    y_att = (rmsnorm(o_att, g_sub) * (1.0 - lam_init)).reshape(B_, L, W_ATT)
    y = jnp.concatenate([y_lru.astype(dt), y_pool.astype(dt), y_fft.astype(dt), y_att.astype(dt)], axis=-1)
    return y @ w_o


def conv_ffn(h, w_up, conv_w, conv_b, w_down):
    g, v = jnp.split(h @ w_up, 2, axis=-1)
    g = dwconv(g, conv_w, conv_b, FFN_PAD)
    return (jax.nn.gelu(g) * v) @ w_down


def setup_inputs(seed: int = 0) -> dict:
    key = jax.random.key(seed)
    ks = jax.random.split(key, 26)
    f32 = jnp.float32

    def nrm(k, shape, fan_in, gain=1.0):
        return gain * (fan_in ** -0.5) * jax.random.normal(k, shape, f32)

    def small(k, shape, s=0.02):
        return s * jax.random.normal(k, shape, f32)

    a0 = jax.random.uniform(ks[13], (DEPTH, 2, W_LRU), f32, minval=0.9 ** (1.0 / LRU_C), maxval=0.999 ** (1.0 / LRU_C))
    return {
        'x': jax.random.normal(ks[0], (BATCH, SEQ, D_MODEL), f32),
        'c': jax.random.normal(ks[1], (BATCH, D_MODEL), f32),
        'ctx': jax.random.normal(ks[2], (BATCH, CTX_LEN, D_MODEL), f32),
        'c_ctx': jax.random.normal(ks[3], (D_MODEL,), f32),
        'ada_w': nrm(ks[4], (DEPTH, D_MODEL, N_MOD * D_MODEL), D_MODEL, 0.5),
        'ada_b': small(ks[5], (DEPTH, N_MOD * D_MODEL)),
        'norm_g': 1.0 + small(ks[6], (DEPTH, 4, D_MODEL), 0.05),
        'w_in': nrm(ks[7], (DEPTH, D_MODEL, D_IN), D_MODEL),
        'lru_conv_w': nrm(ks[8], (DEPTH, LRU_CONV, W_LRU), LRU_CONV),
        'lru_conv_b': small(ks[9], (DEPTH, W_LRU)),
        'lru_wa': nrm(ks[10], (DEPTH, 2, LRU_BLOCKS, LRU_BW, LRU_BW), LRU_BW),
        'lru_ba': small(ks[11], (DEPTH, 2, W_LRU)),
        'lru_wi': nrm(ks[12], (DEPTH, 2, LRU_BLOCKS, LRU_BW, LRU_BW), LRU_BW),
        'lru_bi': small(ks[14], (DEPTH, 2, W_LRU)),
        'lru_lam': jnp.log(a0) - jnp.log1p(-a0),
        'pool_w': nrm(ks[15], (DEPTH, POOL_GROUPS, POOL_GW, POOL_GW), POOL_GW),
        'pool_scale': 1.0 + small(ks[16], (DEPTH, W_POOL), 0.1),
        'fourier_w': nrm(ks[17], (DEPTH, W_FFT, W_FFT), W_FFT),
        'diff_lam': small(ks[18], (DEPTH, 4, ATT_DH), 0.1),
        'diff_subln_g': 1.0 + small(ks[19], (DEPTH, ATT_DV), 0.05),
        'w_out': nrm(ks[20], (DEPTH, D_MIX, D_MODEL), D_MIX),
        'ffn_w_up': nrm(ks[21], (DEPTH, D_MODEL, 2 * D_FF), D_MODEL),
        'ffn_conv_w': nrm(ks[22], (DEPTH, FFN_CONV, D_FF), FFN_CONV),
        'ffn_conv_b': small(ks[23], (DEPTH, D_FF)),
        'ffn_w_down': nrm(ks[24], (DEPTH, D_FF, D_MODEL), D_FF),
    }


def reference(x, c, ctx, c_ctx, ada_w, ada_b, norm_g, w_in, lru_conv_w, lru_conv_b, lru_wa, lru_ba, lru_wi,
              lru_bi, lru_lam, pool_w, pool_scale, fourier_w, diff_lam, diff_subln_g, w_out, ffn_w_up,
              ffn_conv_w, ffn_conv_b, ffn_w_down):
    f32 = jnp.float32
    dt = x.dtype
    B_, L, _ = x.shape
    rows = L // GRID_W
    t = jnp.arange(rows * GRID_W)
    row = (t // GRID_W).astype(f32)
    col = (t % GRID_W).astype(f32)
    inv = ROPE_BASE ** (-jnp.arange(N_FREQ, dtype=f32) / N_FREQ)
    ang_r = row[:, None] * inv
    ang_c = col[:, None] * inv

    silu_c = jax.nn.silu(c.astype(f32))
    silu_cc = jax.nn.silu(c_ctx.astype(f32))[None, :]
    xc = ctx
    for l in range(DEPTH):
        last = l == DEPTH - 1
        lam_init = 0.8 - 0.6 * math.exp(-0.3 * l)
        lam = (jnp.exp(jnp.sum(diff_lam[l, 0] * diff_lam[l, 1]).astype(f32))
               - jnp.exp(jnp.sum(diff_lam[l, 2] * diff_lam[l, 3]).astype(f32)) + lam_init)
        mod = (silu_c @ ada_w[l] + ada_b[l]).astype(dt)
        modc = (silu_cc @ ada_w[l] + ada_b[l]).astype(dt)
        sh1, sc1, g1, sh2, sc2, g2 = jnp.split(mod, N_MOD, axis=-1)
        csh1, csc1, cg1, csh2, csc2, cg2 = jnp.split(modc, N_MOD, axis=-1)
        lru_p = (lru_wa[l], lru_ba[l], lru_wi[l], lru_bi[l], lru_lam[l])

        hc = modulate(rmsnorm(xc, norm_g[l, 0]), csh1, csc1)
        if last:
            cx = hc @ w_in[l, :, :W_LRU]
            ck, cv = jnp.split(hc @ w_in[l, :, OFF_K:], 2, axis=-1)
        else:
            cx, cgate, cpool, cfft, cq, ck, cv = jnp.split(hc @ w_in[l], SPLITS, axis=-1)
        chf, chb = bidir_lru(cx, lru_conv_w[l], lru_conv_b[l], *lru_p, None, None)
        ck = qk_heads(ck)
        cv = v_heads(cv)

        h = modulate(rmsnorm(x, norm_g[l, 0]), sh1, sc1)
        ux, ugate, upool, ufft, q, k, v = jnp.split(h @ w_in[l], SPLITS, axis=-1)
        hf, hb = bidir_lru(ux, lru_conv_w[l], lru_conv_b[l], *lru_p, chf[:, -1], chb[:, 0])
        y_lru = jax.nn.gelu(ugate.astype(f32)) * (hf + hb)
        qr = axial_rope(qk_heads(q), ang_r, ang_c)
        kr = axial_rope(qk_heads(k), ang_r, ang_c)
        o = diff_attention(qr, jnp.concatenate([ck, kr], axis=1), jnp.concatenate([cv, v_heads(v)], axis=1), lam)
        y = merge_groups(y_lru, pool_mixer(upool, pool_w[l], pool_scale[l]), fourier_mixer(ufft, fourier_w[l]),
                         o, diff_subln_g[l], lam_init, w_out[l], dt)
        x = x + g1[:, None, :] * rmsnorm(y, norm_g[l, 1])

        if not last:
            yc = merge_groups(jax.nn.gelu(cgate.astype(f32)) * (chf + chb),
                              pool_mixer(cpool, pool_w[l], pool_scale[l]), fourier_mixer(cfft, fourier_w[l]),
                              diff_attention(qk_heads(cq), ck, cv, lam), diff_subln_g[l], lam_init, w_out[l], dt)
            xc = xc + cg1[:, None, :] * rmsnorm(yc, norm_g[l, 1])

        f = conv_ffn(modulate(rmsnorm(x, norm_g[l, 2]), sh2, sc2), ffn_w_up[l], ffn_conv_w[l], ffn_conv_b[l], ffn_w_down[l])
        x = x + g2[:, None, :] * rmsnorm(f, norm_g[l, 3])
        if not last:
            fc = conv_ffn(modulate(rmsnorm(xc, norm_g[l, 2]), csh2, csc2), ffn_w_up[l], ffn_conv_w[l], ffn_conv_b[l], ffn_w_down[l])
            xc = xc + cg2[:, None, :] * rmsnorm(fc, norm_g[l, 3])
    return x
```

```python
import contextlib
import math
import numpy as np
import ml_dtypes
import concourse.bass as bass
import concourse.mybir as mybir
from concourse.bass_utils import run_bass_kernel_spmd

F32 = mybir.dt.float32
BF16 = mybir.dt.bfloat16
AF = mybir.ActivationFunctionType
ALU = mybir.AluOpType
NPBF = ml_dtypes.bfloat16

D = 2048
DC = D // 128
DFF = 5632
FC = DFF // 128
DIN = 3584
EPS = 1e-6
NCORES = 8


class Tk:
    def __init__(self, t, name):
        self.t = t
        self.name = name
        self.w = None
        self.r = {}
        self.dsem = {}
        self.dcnt = {}
        self.psum = False

    def __getitem__(self, idx):
        return self.t[idx]


class K:
    def __init__(self, nc):
        self.nc = nc
        self.es = contextlib.ExitStack()
        self.root = self.es
        self.E = {'pe': nc.tensor, 'act': nc.scalar, 'dve': nc.vector, 'pool': nc.gpsimd, 'sp': nc.sync}
        self.sem = {e: self.es.enter_context(nc.semaphore('s_' + e)) for e in ('pe', 'act', 'dve', 'pool')}
        self.cnt = dict.fromkeys(self.sem, 0)
        self.known = {e: {} for e in self.E}
        self.final = []
        self.nd = 0

    def sb(self, name, shape, dt):
        t = Tk(self.es.enter_context(self.nc.sbuf_tensor(name, shape, dt)), name)
        if getattr(self, 'scope_tiles', None) is not None:
            self.scope_tiles.append(t)
        return t

    def push_scope(self):
        self.saved_es = self.es
        self.es = contextlib.ExitStack()
        self.scope_tiles = []

    def pop_scope(self):
        deps = [(self.sem[e], self.cnt[e]) for e in self.sem if self.cnt[e]]
        for t in self.scope_tiles:
            if t.w:
                deps.append(t.w)
            deps.extend(t.r.values())
        for e in self.E:
            self._wait(e, deps)
        self.es.close()
        self.es = self.saved_es
        self.scope_tiles = None

    def ps(self, name, shape, dt=F32):
        t = Tk(self.es.enter_context(self.nc.psum_tensor(name, shape, dt)), name)
        t.psum = True
        return t

    def dram(self, name, shape, dt, kind):
        return Tk(self.nc.dram_tensor(name, shape, dt, kind=kind).ap(), name)

    def _wait(self, e, deps):
        kn = self.known[e]
        for sem, val in deps:
            if kn.get(id(sem), 0) < val:
                self.E[e].wait_ge(sem, val)
                kn[id(sem)] = val

    @staticmethod
    def _deps(reads, writes):
        d = []
        for t in reads:
            if t.w:
                d.append(t.w)
        for t in writes:
            if t.w:
                d.append(t.w)
            d.extend(t.r.values())
        return d

    @staticmethod
    def _mark(tok, reads, writes):
        for t in reads:
            t.r[id(tok[0])] = tok
        for t in writes:
            t.w = tok
            t.r = {}

    def op(self, e, reads, writes, fn):
        writes = list(writes) + [t for t in reads if t.psum]
        self._wait(e, self._deps(reads, writes))
        ins = fn(self.E[e])
        self.cnt[e] += 1
        ins.then_inc(self.sem[e], 1)
        self._mark((self.sem[e], self.cnt[e]), reads, writes)

    def mm(self, out, out_ap, pairs, reads, **kw):
        self._wait('pe', self._deps(reads, [out]))
        n = len(pairs)
        for i, (l, r) in enumerate(pairs):
            ins = self.nc.tensor.matmul(out_ap, l, r, start=(i == 0), stop=(i == n - 1), **kw)
        self.cnt['pe'] += 1
        ins.then_inc(self.sem['pe'], 1)
        self._mark((self.sem['pe'], self.cnt['pe']), reads, [out])

    def dma(self, q, out_tk, out_ap, in_tk, in_ap, final=False, **kw):
        reads = [in_tk] if in_tk is not None else []
        writes = [out_tk] if out_tk is not None else []
        self._wait(q, self._deps(reads, writes))
        own = out_tk if out_tk is not None else in_tk
        if q not in own.dsem:
            self.nd += 1
            own.dsem[q] = self.root.enter_context(self.nc.semaphore('d%d' % self.nd))
            own.dcnt[q] = 0
        ins = self.E[q].dma_start(out=out_ap, in_=in_ap, **kw)
        own.dcnt[q] += 16
        ins.then_inc(own.dsem[q], 16)
        tok = (own.dsem[q], own.dcnt[q])
        self._mark(tok, reads, writes)
        if final:
            self.final.append(tok)

    def finish(self):
        self._wait('sp', self.final)
        self._wait('sp', [(self.sem[e], self.cnt[e]) for e in self.sem if self.cnt[e]])
        self.es.close()


def _col(t, i):
    return t[:, i:i + 1]


def build_T(NT, do_C, do_A, CT=256, TT=512, dbg=0):
    nc = bass.Bass("TRN2", target_bir_lowering=False)
    k = K(nc)
    ntile = NT // TT
    NH = 2 * ntile if do_C else 0
    NTOT = NH + NT + CT
    NOUT = NT + CT

    def din(name, shape, dt=F32):
        return nc.dram_tensor(name, shape, dt, kind="ExternalInput").ap()

    xT = din("xT", [D, NTOT]).rearrange("(c p) n -> p c n", p=128)
    if do_C:
        yT = din("yT", [D, NTOT], BF16).rearrange("(c p) n -> p c n", p=128)
        hmask = din("hmask", [128, NH])
        modC = din("modC", [128, 2 * 4 * DC])
        ngC = din("ngC", [128, 3 * DC])
        cw = din("cw", [128, 3 * FC])
        cb = din("cb", [128, FC])
        wf_in = din("wf", [1, 128, 4 * 512])
        wo_in = din("wo", [8, 128, DC * 256])
        wu_in = din("wu", [FC, 128, DC * 256])
        wd_in = din("wd", [DC, 128, FC * 128])
        xo = nc.dram_tensor("xo", [D, NOUT], F32, kind="ExternalOutput").ap().rearrange("(c p) n -> p c n", p=128)
    if do_A:
        modA = din("modA", [128, 2 * 2 * DC])
        ngA = din("ngA", [128, DC])
        wi_in = din("wi", [14, 128, DC * 256])
        po = nc.dram_tensor("po", [DIN, NOUT], BF16, kind="ExternalOutput").ap()

    def scratch(name, src):
        G, _, Fw = src.shape
        tk = k.dram(name + "_bf", [G, 128, Fw], BF16, "Internal")
        step = max(1, (1 << 20) // (128 * Fw))
        mld = max(d for d in range(1, 2049) if Fw % d == 0)
        for g0 in range(0, G, step):
            g1 = min(G, g0 + step)
            k.dma('pool', tk, tk.t[g0:g1], None, src[g0:g1], max_dma_last_dim=mld)
        return tk

    WB = 3
    wbuf = [k.sb("wbuf%d" % i, [128, FC * 128], BF16) for i in range(WB)]
    wctr = [0]

    def wload(tk, g, width):
        b = wbuf[wctr[0] % WB]
        wctr[0] += 1
        k.dma('sp', b, b.t[:, 0:width], tk, tk.t[g])
        return b

    if do_C:
        wf_s = scratch("wf", wf_in)
        wo_s = scratch("wo", wo_in)
        wu_s = scratch("wu", wu_in)
        wd_s = scratch("wd", wd_in)
    if do_A:
        wi_s = scratch("wi", wi_in)

    ones = k.sb("ones", [128, 128], BF16)
    k.op('dve', [], [ones], lambda e: e.memset(ones.t[:], 1.0))
    epsc = k.sb("epsc", [128, 1], F32)
    k.op('dve', [], [epsc], lambda e: e.memset(epsc.t[:], EPS))
    prm = k.sb("prm", [128, 16 * DC + 4 * FC + NH], F32)
    cst = k.sb("cst", [128, 12 * DC], F32)
    o_modC, o_ngC, o_modA, o_ngA, o_cw, o_cb, o_hm = 0, 8 * DC, 11 * DC, 15 * DC, 16 * DC, 16 * DC + 3 * FC, 16 * DC + 4 * FC
    cG1, cA2, cB2, cG2, cA1, cB1 = 0, 2 * DC, 4 * DC, 6 * DC, 8 * DC, 10 * DC
    if do_C:
        k.dma('sp', prm, prm.t[:, o_modC:o_modC + 8 * DC], None, modC)
        k.dma('sp', prm, prm.t[:, o_ngC:o_ngC + 3 * DC], None, ngC)
        k.dma('sp', prm, prm.t[:, o_cw:o_cw + 3 * FC], None, cw)
        k.dma('sp', prm, prm.t[:, o_cb:o_cb + FC], None, cb)
        k.dma('sp', prm, prm.t[:, o_hm:o_hm + NH], None, hmask)
        for m in range(2):
            mo = o_modC + m * 4 * DC
            k.op('dve', [prm], [cst], lambda e, m=m, mo=mo: e.tensor_tensor(
                cst.t[:, cG1 + m * DC:cG1 + (m + 1) * DC], prm.t[:, mo:mo + DC], prm.t[:, o_ngC:o_ngC + DC], ALU.mult))
            k.op('dve', [prm], [cst], lambda e, m=m, mo=mo: e.scalar_tensor_tensor(
                cst.t[:, cA2 + m * DC:cA2 + (m + 1) * DC], prm.t[:, mo + 2 * DC:mo + 3 * DC], 1.0,
                prm.t[:, o_ngC + DC:o_ngC + 2 * DC], ALU.add, ALU.mult))
            k.op('dve', [prm], [cst], lambda e, m=m, mo=mo: e.tensor_copy(
                cst.t[:, cB2 + m * DC:cB2 + (m + 1) * DC], prm.t[:, mo + DC:mo + 2 * DC]))
            k.op('dve', [prm], [cst], lambda e, m=m, mo=mo: e.tensor_tensor(
                cst.t[:, cG2 + m * DC:cG2 + (m + 1) * DC], prm.t[:, mo + 3 * DC:mo + 4 * DC],
                prm.t[:, o_ngC + 2 * DC:o_ngC + 3 * DC], ALU.mult))
    if do_A:
        k.dma('sp', prm, prm.t[:, o_modA:o_modA + 4 * DC], None, modA)
        k.dma('sp', prm, prm.t[:, o_ngA:o_ngA + DC], None, ngA)
        for m in range(2):
            mo = o_modA + m * 2 * DC
            k.op('dve', [prm], [cst], lambda e, m=m, mo=mo: e.scalar_tensor_tensor(
                cst.t[:, cA1 + m * DC:cA1 + (m + 1) * DC], prm.t[:, mo + DC:mo + 2 * DC], 1.0,
                prm.t[:, o_ngA:o_ngA + DC], ALU.add, ALU.mult))
            k.op('dve', [prm], [cst], lambda e, m=m, mo=mo: e.tensor_copy(
                cst.t[:, cB1 + m * DC:cB1 + (m + 1) * DC], prm.t[:, mo:mo + DC]))

    xs = k.sb("xs", [128, DC, TT], F32)
    of = k.sb("of", [128, DC, TT], F32)
    hb = k.sb("hb", [128, DC, TT], BF16)
    zb = k.sb("zb", [128, 4, TT], BF16)
    ab = k.sb("ab", [128, FC, TT], BF16)
    ghalo = k.sb("ghalo", [128, FC, max(NH, 2)], F32)
    rstd = k.sb("rstd", [128, TT], F32)
    NR = 3
    sqb = [k.sb("sqb%d" % i, [128, TT], BF16) for i in range(NR)]
    tmp = [k.sb("tmp%d" % i, [128, TT], F32) for i in range(NR)]
    gbuf = [k.sb("gbuf%d" % i, [128, TT + 2], F32) for i in range(NR)]
    cva = [k.sb("cva%d" % i, [128, TT], F32) for i in range(NR)]
    cvb = [k.sb("cvb%d" % i, [128, TT], F32) for i in range(NR)]
    geb = [k.sb("geb%d" % i, [128, TT], F32) for i in range(NR)]
    pbo = [k.sb("pbo%d" % i, [128, TT], BF16) for i in range(NR)]
    pg = [k.ps("pg%d" % i, [128, TT]) for i in range(2)]
    pv = [k.ps("pv%d" % i, [128, TT]) for i in range(2)]
    pm = [k.ps("pm%d" % i, [128, TT]) for i in range(2)]
    pst = k.ps("pst", [128, TT])
    rot = {'sq': 0, 'tmp': 0, 'g': 0, 'pg': 0, 'pm': 0, 'pb': 0}

    def nxt(key, n):
        v = rot[key] % n
        rot[key] += 1
        return v

    def sumsq_chunk(src_tk, src_ap, c, n):
        s = sqb[nxt('sq', NR)]
        k.op('act', [src_tk], [s], lambda e: e.activation(s.t[:, :n], src_ap, AF.Square))
        k._wait('pe', k._deps([s, ones], [pst]))
        ins = nc.tensor.matmul(pst.t[:, :n], ones.t[:, :], s.t[:, :n], start=(c == 0), stop=(c == DC - 1))
        k.cnt['pe'] += 1
        ins.then_inc(k.sem['pe'], 1)
        k._mark((k.sem['pe'], k.cnt['pe']), [s, ones], [pst])

    def make_rstd(n):
        t = tmp[nxt('tmp', NR)]
        k.op('act', [pst, epsc], [t], lambda e: e.activation(t.t[:, :n], pst.t[:, :n], AF.Sqrt, bias=epsc.t[:, 0:1], scale=1.0 / D))
        k.op('dve', [t], [rstd], lambda e: e.reciprocal(rstd.t[:, :n], t.t[:, :n]))

    def resid_update(gcol0, n):
        for c in range(DC):
            t = tmp[nxt('tmp', NR)]
            k.op('dve', [of, cst, rstd], [t], lambda e, c=c, t=t: e.scalar_tensor_tensor(
                t.t[:, :n], of.t[:, c, :n], _col(cst.t, gcol0 + c), rstd.t[:, :n], ALU.mult, ALU.mult))
            k.op('pool', [t, xs], [xs], lambda e, c=c, t=t: e.tensor_tensor(xs.t[:, c, :n], xs.t[:, c, :n], t.t[:, :n], ALU.add))

    def norm_mod(acol0, bcol0, n):
        for c in range(DC):
            sumsq_chunk(xs, xs.t[:, c, :n], c, n)
        make_rstd(n)
        for c in range(DC):
            t = tmp[nxt('tmp', NR)]
            k.op('dve', [xs, cst, rstd], [t], lambda e, c=c, t=t: e.scalar_tensor_tensor(
                t.t[:, :n], xs.t[:, c, :n], _col(cst.t, acol0 + c), rstd.t[:, :n], ALU.mult, ALU.mult))
            k.op('act', [t, cst], [hb], lambda e, c=c, t=t: e.activation(
                hb.t[:, c, :n], t.t[:, :n], AF.Identity, bias=_col(cst.t, bcol0 + c), scale=1.0))

    segs = []
    if do_C and dbg < 2:
        segs.append(('halo', 0, NH, 0, -1))
    for i in range(ntile):
        segs.append(('lat', NH + i * TT, TT, 0, i))
    segs.append(('ctx', NH + NT, CT, 1, -1))

    for kind, c0, n, m, ti in segs:
        k.dma('sp', xs, xs.t[:, :, :n], None, xT[:, :, c0:c0 + n])
        if do_C:
            k.dma('sp', hb, hb.t[:, :, :n], None, yT[:, :, c0:c0 + n])
            wt = wload(wf_s, 0, 4 * 512)
            for mj in range(4 if dbg not in (3,) else 0):
                p = pm[nxt('pm', 2)]
                k.mm(p, p.t[:, :n], [(wt.t[:, kc * 512 + mj * 128:kc * 512 + (mj + 1) * 128], hb.t[:, 8 + kc, :n]) for kc in range(4)], [wt, hb])
                k.op('act', [p], [zb], lambda e, p=p, mj=mj: e.activation(zb.t[:, mj, :n], p.t[:, :n], AF.Identity))
            for g in range(8 if dbg not in (3, 4) else 0):
                wt = wload(wo_s, g, DC * 256)
                for j in range(2):
                    mi = 2 * g + j
                    p = pm[nxt('pm', 2)]
                    prs = []
                    for kc in range(DC):
                        rhs = zb.t[:, kc - 8, :n] if 8 <= kc < 12 else hb.t[:, kc, :n]
                        prs.append((wt.t[:, kc * 256 + j * 128:kc * 256 + (j + 1) * 128], rhs))
                    k.mm(p, p.t[:, :n], prs, [wt, hb, zb])
                    k.op('dve', [p], [of], lambda e, p=p, mi=mi: e.tensor_copy(of.t[:, mi, :n], p.t[:, :n]))
                    sumsq_chunk(of, of.t[:, mi, :n], mi, n)
            if dbg not in (3, 4, 5):
                make_rstd(n)
                resid_update(cG1 + m * DC, n)
            if dbg == 0:
                norm_mod(cA2 + m * DC, cB2 + m * DC, n)
            for mi in range(FC if dbg == 0 else 0):
                wt = wload(wu_s, mi, DC * 256)
                pgi = nxt('pg', 2)
                pgt, pvt = pg[pgi], pv[pgi]
                k.mm(pgt, pgt.t[:, :n], [(wt.t[:, kc * 256:kc * 256 + 128], hb.t[:, kc, :n]) for kc in range(DC)], [wt, hb])
                if kind == 'halo':
                    k.op('dve', [pgt, prm], [ghalo], lambda e, pgt=pgt, mi=mi: e.tensor_tensor(
                        ghalo.t[:, mi, :], pgt.t[:, :n], prm.t[:, o_hm:o_hm + NH], ALU.mult))
                    continue
                k.mm(pvt, pvt.t[:, :n], [(wt.t[:, kc * 256 + 128:kc * 256 + 256], hb.t[:, kc, :n]) for kc in range(DC)], [wt, hb])
                gi = nxt('g', NR)
                gb, ca, cbb, ge = gbuf[gi], cva[gi], cvb[gi], geb[gi]
                k.op('act', [pgt], [gb], lambda e, gb=gb, pgt=pgt: e.activation(gb.t[:, 1:n + 1], pgt.t[:, :n], AF.Identity))
                if kind == 'lat':
                    k.op('pool', [ghalo], [gb], lambda e, gb=gb, mi=mi: e.tensor_copy(gb.t[:, 0:n + 2:n + 1], ghalo.t[:, mi, 2 * ti:2 * ti + 2]))
                else:
                    k.op('pool', [], [gb], lambda e, gb=gb: e.memset(gb.t[:, 0:n + 2:n + 1], 0.0))
                k.op('dve', [gb, prm], [ca], lambda e, gb=gb, ca=ca, mi=mi: e.tensor_scalar(
                    ca.t[:, :n], gb.t[:, 0:n], _col(prm.t, o_cw + mi), _col(prm.t, o_cb + mi), ALU.mult, ALU.add))
                k.op('dve', [gb, prm, ca], [cbb], lambda e, gb=gb, ca=ca, cbb=cbb, mi=mi: e.scalar_tensor_tensor(
                    cbb.t[:, :n], gb.t[:, 1:n + 1], _col(prm.t, o_cw + FC + mi), ca.t[:, :n], ALU.mult, ALU.add))
                k.op('dve', [gb, prm, cbb], [ca], lambda e, gb=gb, ca=ca, cbb=cbb, mi=mi: e.scalar_tensor_tensor(
                    ca.t[:, :n], gb.t[:, 2:n + 2], _col(prm.t, o_cw + 2 * FC + mi), cbb.t[:, :n], ALU.mult, ALU.add))
                k.op('act', [ca], [ge], lambda e, ca=ca, ge=ge: e.activation(ge.t[:, :n], ca.t[:, :n], AF.Gelu))
                k.op('dve', [ge, pvt], [ab], lambda e, ge=ge, pvt=pvt, mi=mi: e.tensor_tensor(ab.t[:, mi, :n], ge.t[:, :n], pvt.t[:, :n], ALU.mult))
            if kind == 'halo':
                continue
            for g in range(DC if dbg == 0 else 0):
                wt = wload(wd_s, g, FC * 128)
                p = pm[nxt('pm', 2)]
                k.mm(p, p.t[:, :n], [(wt.t[:, mi * 128:(mi + 1) * 128], ab.t[:, mi, :n]) for mi in range(FC)], [wt, ab])
                k.op('dve', [p], [of], lambda e, p=p, g=g: e.tensor_copy(of.t[:, g, :n], p.t[:, :n]))
                sumsq_chunk(of, of.t[:, g, :n], g, n)
            if dbg == 0:
                make_rstd(n)
                resid_update(cG2 + m * DC, n)
            k.dma('pool', None, xo[:, :, c0 - NH:c0 - NH + n], xs, xs.t[:, :, :n], final=True)
        if do_A:
            norm_mod(cA1 + m * DC, cB1 + m * DC, n)
            for g in range(14):
                wt = wload(wi_s, g, DC * 256)
                for j in range(2):
                    mi = 2 * g + j
                    p = pm[nxt('pm', 2)]
                    k.mm(p, p.t[:, :n], [(wt.t[:, kc * 256 + j * 128:kc * 256 + (j + 1) * 128], hb.t[:, kc, :n]) for kc in range(DC)], [wt, hb])
                    pb = pbo[nxt('pb', NR)]
                    k.op('act', [p], [pb], lambda e, p=p, pb=pb: e.activation(pb.t[:, :n], p.t[:, :n], AF.Identity))
                    k.dma('pool', None, po[mi * 128:(mi + 1) * 128, c0 - NH:c0 - NH + n], pb, pb.t[:, :n], final=True)
    k.finish()
    return nc


def blk(W, cols):
    Kd, M = W.shape
    KC, G = Kd // 128, M // cols
    return np.ascontiguousarray(W.reshape(KC, 128, G, cols).transpose(2, 1, 0, 3).reshape(G, 128, KC * cols))


def blk_up(W):
    Wg = W[:, :DFF].reshape(DC, 128, FC, 1, 128)
    Wv = W[:, DFF:].reshape(DC, 128, FC, 1, 128)
    Wc = np.concatenate([Wg, Wv], axis=3)
    return np.ascontiguousarray(Wc.transpose(2, 1, 0, 3, 4).reshape(FC, 128, DC * 256))


def colvec(v):
    return np.ascontiguousarray(v.reshape(-1, 128).T)


def cols(*vs):
    return np.ascontiguousarray(np.concatenate([colvec(v) for v in vs], axis=1).astype(np.float32))


def build_M(L, CT=256, TT=512):
    nc = bass.Bass("TRN2", target_bir_lowering=False)
    k = K(nc)
    SEQ = [('c', CT), ('l', L)]

    def din(name, shape, dt=F32):
        return nc.dram_tensor(name, shape, dt, kind="ExternalInput").ap()

    I = {}
    for s, Ls in SEQ:
        I['ux' + s] = din('ux' + s, [128, Ls + 3], BF16)
        I['ug' + s] = din('ug' + s, [128, Ls], BF16)
        I['up' + s] = din('up' + s, [128, Ls + 16], BF16)
        I['uf' + s] = din('uf' + s, [128, Ls], BF16)
        I['q' + s] = din('q' + s, [128, Ls], BF16)
        I['k' + s] = din('k' + s, [128, Ls], BF16)
        I['v' + s] = din('v' + s, [Ls, 128], BF16)
        I['icnt' + s] = din('icnt' + s, [128, Ls])
        L1 = Ls // 128
        I['F1' + s] = din('F1' + s, [L1, 4 * L1])
        I['Tw' + s] = din('Tw' + s, [128, 2 * L1])
        I['yo' + s] = nc.dram_tensor('yo' + s, [4, 128, Ls], BF16, kind="ExternalOutput").ap()
    lruw_in = din('lruw', [128, 512]); lrup_in = din('lrup', [128, 11])
    poolw_in = din('poolw', [128, 128]); poolp_in = din('poolp', [128, 5])
    f128_in = din('f128', [128, 384]); rm_in = din('rm', [128, 128])
    cs_in = din('cossin', [2, 128, L]); dlam_in = din('dlam', [128, 256]); attp_in = din('attp', [128, 3])

    wq = k.sb('wq', [128, 512 + 128 + 384 + 128], BF16)
    oLW, oPW, oF, oRM = 0, 512, 640, 1024
    k.dma('pool', wq, wq.t[:, oLW:oLW + 512], None, lruw_in)
    k.dma('pool', wq, wq.t[:, oPW:oPW + 128], None, poolw_in)
    k.dma('pool', wq, wq.t[:, oF:oF + 384], None, f128_in)
    k.dma('pool', wq, wq.t[:, oRM:oRM + 128], None, rm_in)
    pp = k.sb('pp', [128, 11 + 5 + 3 + 8], F32)
    oLP, oPP, oAP, oDV = 0, 11, 16, 19
    k.dma('sp', pp, pp.t[:, oLP:oLP + 11], None, lrup_in)
    k.dma('sp', pp, pp.t[:, oPP:oPP + 5], None, poolp_in)
    k.dma('sp', pp, pp.t[:, oAP:oAP + 3], None, attp_in)
    ones = k.sb("ones", [128, 128], BF16)
    k.op('dve', [], [ones], lambda e: e.memset(ones.t[:], 1.0))
    onec = k.sb("onec", [128, 2], F32)
    k.op('dve', [], [onec], lambda e: e.memset(onec.t[:, 0:1], 1.0))
    k.op('dve', [onec], [onec], lambda e: e.memset(onec.t[:, 1:2], EPS))
    sc0 = k.sb('sc0', [128, 256], F32)
    k.op('act', [pp], [sc0], lambda e: e.activation(sc0.t[:, 0:2], pp.t[:, oLP + 9:oLP + 11], AF.Exp, scale=-1.0))
    k.op('act', [sc0, onec], [sc0], lambda e: e.activation(sc0.t[:, 2:4], sc0.t[:, 0:2], AF.Ln, bias=onec.t[:, 0:1], scale=1.0))
    k.op('dve', [sc0], [pp], lambda e: e.tensor_scalar(pp.t[:, oDV:oDV + 2], sc0.t[:, 2:4], -8.0, None, ALU.mult))
    k.op('dve', [sc0], [pp], lambda e: e.tensor_scalar(pp.t[:, oDV + 2:oDV + 4], sc0.t[:, 2:4], -16.0, None, ALU.mult))
    dl = k.sb('dl', [128, 256], F32)
    k.dma('sp', dl, dl.t[:, :], None, dlam_in)
    k.op('dve', [dl], [sc0], lambda e: e.tensor_tensor(sc0.t[:, 0:64], dl.t[:, 0:64], dl.t[:, 64:128], ALU.mult))
    k.op('dve', [dl, sc0], [sc0], lambda e: e.tensor_tensor(sc0.t[:, 64:128], dl.t[:, 128:192], dl.t[:, 192:256], ALU.mult))
    k.op('dve', [sc0], [sc0], lambda e: e.reduce_sum(sc0.t[:, 128:130], sc0.t[:, 0:128].rearrange("p (a b) -> p a b", a=2), mybir.AxisListType.X))
    k.op('act', [sc0], [sc0], lambda e: e.activation(sc0.t[:, 130:132], sc0.t[:, 128:130], AF.Exp))
    k.op('dve', [sc0], [sc0], lambda e: e.tensor_tensor(sc0.t[:, 132:133], sc0.t[:, 131:132], sc0.t[:, 130:131], ALU.subtract))
    k.op('dve', [sc0, pp], [pp], lambda e: e.tensor_tensor(pp.t[:, oDV + 4:oDV + 5], sc0.t[:, 132:133], pp.t[:, oAP + 1:oAP + 2], ALU.subtract))

    k.op('dve', [pp], [pp], lambda e: e.tensor_tensor(pp.t[:, oDV + 5:oDV + 6], pp.t[:, oAP:oAP + 1], pp.t[:, oAP + 2:oAP + 3], ALU.mult))
    NR = 3
    rot = {}

    def nxt(key, n=NR):
        v = rot.get(key, 0)
        rot[key] = v + 1
        return v % n

    ps = [k.ps("ps%d" % i, [128, 512]) for i in range(8)]
    f32t = [k.sb("f32t%d" % i, [128, TT + 16], F32) for i in range(12)]
    bft = [k.sb("bft%d" % i, [128, TT + 16], BF16) for i in range(8)]
    outb = [k.sb("outb%d" % i, [128, TT], BF16) for i in range(NR)]

    def F():
        return f32t[nxt('f', 12)]

    def B():
        return bft[nxt('b', 8)]

    def P():
        return ps[nxt('p', 4)]

    def store(y_ap, src_tk, src_ap):
        k.dma('pool', None, y_ap, src_tk, src_ap, final=True)

    hstate = k.sb('hstate', [128, 4], F32)
    k.push_scope()
    hf_all = k.sb('hf_all', [128, L], F32)

    def lru(s, Ls):
        yo = I['yo' + s]
        nt = (Ls + TT - 1) // TT
        tiles = [(i * TT, min(TT, Ls - i * TT)) for i in range(nt)]

        def gates(t0, n, d):
            xt = B()
            k.dma('sp', xt, xt.t[:, :n + 3], None, I['ux' + s][:, t0:t0 + n + 3])
            u = F(); u2 = F()
            k.op('dve', [xt, pp], [u], lambda e: e.tensor_scalar(u.t[:, :n], xt.t[:, 0:n], _col(pp.t, oLP + 0), _col(pp.t, oLP + 4), ALU.mult, ALU.add))
            k.op('dve', [xt, pp, u], [u2], lambda e: e.scalar_tensor_tensor(u2.t[:, :n], xt.t[:, 1:n + 1], _col(pp.t, oLP + 1), u.t[:, :n], ALU.mult, ALU.add))
            k.op('dve', [xt, pp, u2], [u], lambda e: e.scalar_tensor_tensor(u.t[:, :n], xt.t[:, 2:n + 2], _col(pp.t, oLP + 2), u2.t[:, :n], ALU.mult, ALU.add))
            k.op('dve', [xt, pp, u], [u2], lambda e: e.scalar_tensor_tensor(u2.t[:, :n], xt.t[:, 3:n + 3], _col(pp.t, oLP + 3), u.t[:, :n], ALU.mult, ALU.add))
            ub = B()
            k.op('act', [u2], [ub], lambda e: e.activation(ub.t[:, :n], u2.t[:, :n], AF.Identity))
            pr, pi = P(), P()
            k.mm(pr, pr.t[:, :n], [(wq.t[:, oLW + (2 * d) * 128:oLW + (2 * d + 1) * 128], ub.t[:, :n])], [wq, ub])
            k.mm(pi, pi.t[:, :n], [(wq.t[:, oLW + (2 * d + 1) * 128:oLW + (2 * d + 2) * 128], ub.t[:, :n])], [wq, ub])
            r = F(); ig = F(); a = F(); sq = F(); bb = F()
            k.op('act', [pr, pp], [r], lambda e: e.activation(r.t[:, :n], pr.t[:, :n], AF.Sigmoid, bias=_col(pp.t, oLP + 5 + d), scale=1.0))
            k.op('act', [pi, pp], [ig], lambda e: e.activation(ig.t[:, :n], pi.t[:, :n], AF.Sigmoid, bias=_col(pp.t, oLP + 7 + d), scale=1.0))
            k.op('act', [r, pp], [a], lambda e: e.activation(a.t[:, :n], r.t[:, :n], AF.Exp, scale=_col(pp.t, oDV + d)))
            k.op('act', [r, pp], [sq], lambda e: e.activation(sq.t[:, :n], r.t[:, :n], AF.Exp, scale=_col(pp.t, oDV + 2 + d)))
            k.op('act', [sq, onec], [sq], lambda e: e.activation(sq.t[:, :n], sq.t[:, :n], AF.Sqrt, bias=onec.t[:, 0:1], scale=-1.0))
            k.op('dve', [sq, ig], [bb], lambda e: e.tensor_tensor(bb.t[:, :n], sq.t[:, :n], ig.t[:, :n], ALU.mult))
            k.op('dve', [bb, u2], [bb], lambda e: e.tensor_tensor(bb.t[:, :n], bb.t[:, :n], u2.t[:, :n], ALU.mult))
            return a, bb

        for ti, (t0, n) in enumerate(tiles):
            a, bb = gates(t0, n, 0)
            if ti == 0:
                if s == 'c':
                    k.op('dve', [a, bb], [hf_all], lambda e: e.tensor_tensor_scan(hf_all.t[:, t0:t0 + n], a.t[:, :n], bb.t[:, :n], 0.0, ALU.mult, ALU.add))
                else:
                    k.op('dve', [a, bb, hstate], [hf_all], lambda e: e.tensor_tensor_scan(hf_all.t[:, t0:t0 + n], a.t[:, :n], bb.t[:, :n], hstate.t[:, 0:1], ALU.mult, ALU.add))
            else:
                k.op('dve', [a, bb, hf_all], [hf_all], lambda e: e.tensor_tensor_scan(hf_all.t[:, t0:t0 + n], a.t[:, :n], bb.t[:, :n], hf_all.t[:, t0 - 1:t0], ALU.mult, ALU.add))
        if s == 'c':
            k.op('dve', [hf_all], [hstate], lambda e: e.tensor_copy(hstate.t[:, 0:1], hf_all.t[:, Ls - 1:Ls]))
        prev = None
        for ti in range(nt - 1, -1, -1):
            t0, n = tiles[ti]
            a, bb = gates(t0, n, 1)
            hb_ = F()
            if prev is None:
                init = 0.0 if s == 'c' else hstate.t[:, 1:2]
                rd = [a, bb] if s == 'c' else [a, bb, hstate]
            else:
                init = prev.t[:, 0:1]
                rd = [a, bb, prev]
            k.op('dve', rd, [hb_], lambda e: e.tensor_tensor_scan(hb_.t[:, n - 1::-1] if False else hb_.t[:, 0:n][:, ::-1], a.t[:, 0:n][:, ::-1], bb.t[:, 0:n][:, ::-1], init, ALU.mult, ALU.add))
            prev = hb_
            gt = B(); gg = F(); hs = F(); ob = outb[nxt('o')]
            k.dma('sp', gt, gt.t[:, :n], None, I['ug' + s][:, t0:t0 + n])
            k.op('act', [gt], [gg], lambda e: e.activation(gg.t[:, :n], gt.t[:, :n], AF.Gelu))
            k.op('pool', [hb_, hf_all], [hs], lambda e: e.tensor_tensor(hs.t[:, :n], hb_.t[:, :n], hf_all.t[:, t0:t0 + n], ALU.add))
            k.op('dve', [hs, gg], [ob], lambda e: e.tensor_tensor(ob.t[:, :n], hs.t[:, :n], gg.t[:, :n], ALU.mult))
            store(yo[0, :, t0:t0 + n], ob, ob.t[:, :n])
        if s == 'c':
            k.op('dve', [prev], [hstate], lambda e: e.tensor_copy(hstate.t[:, 1:2], prev.t[:, 0:1]))

    def pool(s, Ls):
        yo = I['yo' + s]
        for t0 in range(0, Ls, TT):
            n = min(TT, Ls - t0)
            ut = B(); ic = F()
            k.dma('sp', ut, ut.t[:, :n + 16], None, I['up' + s][:, t0:t0 + n + 16])
            k.dma('sp', ic, ic.t[:, :n], None, I['icnt' + s][:, t0:t0 + n])
            w1 = F(); w2 = F(); w4 = F(); w8 = F(); ws = F(); ws2 = F()
            m = n + 16
            k.op('pool', [ut], [w1], lambda e: e.tensor_tensor(w1.t[:, 1:m], ut.t[:, 0:m - 1], ut.t[:, 1:m], ALU.add))
            k.op('pool', [w1], [w2], lambda e: e.tensor_tensor(w2.t[:, 2:m - 1], w1.t[:, 1:m - 2], w1.t[:, 3:m], ALU.add))
            k.op('pool', [w2], [w4], lambda e: e.tensor_tensor(w4.t[:, 4:m - 3], w2.t[:, 2:m - 5], w2.t[:, 6:m - 1], ALU.add))
            k.op('pool', [w4], [w8], lambda e: e.tensor_tensor(w8.t[:, 8:m - 7], w4.t[:, 4:m - 11], w4.t[:, 12:m - 3], ALU.add))
            k.op('dve', [w1, pp], [ws], lambda e: e.tensor_scalar(ws.t[:, :n], w1.t[:, 8:8 + n], _col(pp.t, oPP + 1), None, ALU.mult))
            k.op('dve', [w2, pp, ws], [ws2], lambda e: e.scalar_tensor_tensor(ws2.t[:, :n], w2.t[:, 8:8 + n], _col(pp.t, oPP + 2), ws.t[:, :n], ALU.mult, ALU.add))
            k.op('dve', [w4, pp, ws2], [ws], lambda e: e.scalar_tensor_tensor(ws.t[:, :n], w4.t[:, 8:8 + n], _col(pp.t, oPP + 3), ws2.t[:, :n], ALU.mult, ALU.add))
            k.op('dve', [w8, pp, ws], [ws2], lambda e: e.scalar_tensor_tensor(ws2.t[:, :n], w8.t[:, 8:8 + n], _col(pp.t, oPP + 4), ws.t[:, :n], ALU.mult, ALU.add))
            k.op('dve', [ws2, ic], [ws], lambda e: e.tensor_tensor(ws.t[:, :n], ws2.t[:, :n], ic.t[:, :n], ALU.mult))
            db = B()
            k.op('dve', [ws, ut], [db], lambda e: e.tensor_tensor(db.t[:, :n], ws.t[:, :n], ut.t[:, 8:8 + n], ALU.subtract))
            p = P(); ob = outb[nxt('o')]
            k.mm(p, p.t[:, :n], [(wq.t[:, oPW:oPW + 128], db.t[:, :n])], [wq, db])
            k.op('act', [p, pp], [ob], lambda e: e.activation(ob.t[:, :n], p.t[:, :n], AF.Identity, scale=_col(pp.t, oPP + 0)))
            store(yo[1, :, t0:t0 + n], ob, ob.t[:, :n])

    for s, Ls in SEQ:
        lru(s, Ls)
    k.pop_scope()
    for s, Ls in SEQ:
        pool(s, Ls)

    def fft(s, Ls):
        yo = I['yo' + s]
        L1 = Ls // 128
        scale = 1.0 / math.sqrt(Ls * 128.0)
        k.push_scope()
        ufs = k.sb('ufs' + s, [128, Ls], BF16)
        k.dma('sp', ufs, ufs.t[:, :], None, I['uf' + s])
        f1 = k.sb('f1' + s, [128, 4 * L1], BF16)
        k.dma('pool', f1, f1.t[0:L1, :], None, I['F1' + s])
        tw = k.sb('tw' + s, [128, 2 * L1], F32)
        k.dma('sp', tw, tw.t[:, :], None, I['Tw' + s])
        Ab = k.sb('Ab' + s, [128, 2 * 64 * 128], BF16)
        Bp = k.sb('Bp' + s, [128, 2 * 64 * L1], BF16)
        Yh = k.sb('Yh' + s, [128, 128 * L1], BF16)
        Av = Ab.t[:, :].rearrange("p (r c l) -> p r c l", r=2, c=64)
        Bv = Bp.t[:, :].rearrange("p (r c l) -> p r c l", r=2, c=64)
        Yv = Yh.t[:, :].rearrange("p (a b) -> p a b", b=L1)
        for h in range(2):
            for l2 in range(128):
                p = P()
                k.mm(p, p.t[0:L1, 0:64], [(ufs.t[:, l2 * L1:(l2 + 1) * L1], wq.t[:, oF + h * 64:oF + h * 64 + 64])], [ufs, wq])
                k.mm(p, p.t[0:L1, 64:128], [(ufs.t[:, l2 * L1:(l2 + 1) * L1], wq.t[:, oF + 128 + h * 64:oF + 128 + h * 64 + 64])], [ufs, wq])
                k.op('act' if l2 % 2 else 'dve', [p], [Ab], (lambda e, p=p, l2=l2: e.activation(Av[0:L1, :, :, l2], p.t[0:L1, 0:128].rearrange("p (r c) -> p r c", r=2), AF.Identity)) if l2 % 2 else
                     (lambda e, p=p, l2=l2: e.tensor_copy(Av[0:L1, :, :, l2], p.t[0:L1, 0:128].rearrange("p (r c) -> p r c", r=2))))
            for c in range(64):
                p = P()
                k.mm(p, p.t[:, 0:2 * L1], [(Av[0:L1, 0, c, :], f1.t[0:L1, 0:2 * L1]), (Av[0:L1, 1, c, :], f1.t[0:L1, 2 * L1:4 * L1])], [Ab, f1])
                br, bi = p.t[:, 0:L1], p.t[:, L1:2 * L1]
                tc, ts = tw.t[:, 0:L1], tw.t[:, L1:2 * L1]
                t1 = F(); t2 = F(); t3 = F(); t4 = F()
                k.op('dve', [p, tw], [t1], lambda e, t1=t1, br=br, tc=tc: e.tensor_tensor(t1.t[:, :L1], br, tc, ALU.mult))
                k.op('dve', [p, tw], [t2], lambda e, t2=t2, bi=bi, ts=ts: e.tensor_tensor(t2.t[:, :L1], bi, ts, ALU.mult))
                k.op('dve', [p, tw], [t3], lambda e, t3=t3, bi=bi, tc=tc: e.tensor_tensor(t3.t[:, :L1], bi, tc, ALU.mult))
                k.op('dve', [p, tw], [t4], lambda e, t4=t4, br=br, ts=ts: e.tensor_tensor(t4.t[:, :L1], br, ts, ALU.mult))
                k.op('pool', [t1, t2], [Bp], lambda e, t1=t1, t2=t2, c=c: e.tensor_tensor(Bv[:, 0, c, :], t1.t[:, :L1], t2.t[:, :L1], ALU.add))
                k.op('pool', [t3, t4], [Bp], lambda e, t3=t3, t4=t4, c=c: e.tensor_tensor(Bv[:, 1, c, :], t3.t[:, :L1], t4.t[:, :L1], ALU.subtract))
            for l1 in range(L1):
                p = P()
                k.mm(p, p.t[0:64, 0:128], [(Bv[:, 0, :, l1], wq.t[:, oF:oF + 128]), (Bv[:, 1, :, l1], wq.t[:, oF + 256:oF + 384])], [Bp, wq])
                k.op('act', [p], [Yh], lambda e, p=p, l1=l1: e.activation(Yv[h * 64:h * 64 + 64, :, l1], p.t[0:64, 0:128], AF.Identity, scale=scale))
        store(yo[2, :, :].rearrange("p (a b) -> p a b", b=L1), Yh, Yv)
        k.pop_scope()

    for s, Ls in SEQ:
        fft(s, Ls)

    k.push_scope()
    LK = CT + L
    KT = k.sb('KT', [128, LK], BF16)
    VV = k.sb('VV', [128, LK], BF16)
    QT = k.sb('QT', [128, L], BF16)
    QC = k.sb('QC', [128, CT], BF16)
    k.dma('sp', KT, KT.t[:, 0:CT], None, I['kc'])
    k.dma('sp', QC, QC.t[:, :], None, I['qc'])
    Vv = VV.t[:, :].rearrange("p (c e) -> p c e", e=128)
    k.dma('sp', VV, Vv[:, 0:CT // 128, :], None, I['vc'].rearrange("(c p) e -> p c e", p=128))
    k.dma('sp', VV, Vv[:, CT // 128:, :], None, I['vl'].rearrange("(c p) e -> p c e", p=128))
    for nm, dst, off in (('ql', QT, 0), ('kl', KT, CT)):
        for t0 in range(0, L, TT):
            n = min(TT, L - t0)
            xt = B(); ct = F(); st = F()
            k.dma('sp', xt, xt.t[:, :n], None, I[nm][:, t0:t0 + n])
            k.dma('sp', ct, ct.t[:, :n], None, cs_in[0, :, t0:t0 + n])
            k.dma('sp', st, st.t[:, :n], None, cs_in[1, :, t0:t0 + n])
            p = P()
            k.mm(p, p.t[:, :n], [(wq.t[:, oRM:oRM + 128], xt.t[:, :n])], [wq, xt])
            t1 = F(); t2 = F()
            k.op('dve', [xt, ct], [t1], lambda e, t1=t1, xt=xt, ct=ct: e.tensor_tensor(t1.t[:, :n], xt.t[:, :n], ct.t[:, :n], ALU.mult))
            k.op('dve', [p, st], [t2], lambda e, t2=t2, p=p, st=st: e.tensor_tensor(t2.t[:, :n], p.t[:, :n], st.t[:, :n], ALU.mult))
            k.op('pool', [t1, t2], [dst], lambda e, t1=t1, t2=t2, dst=dst, a=off + t0: e.tensor_tensor(dst.t[:, a:a + n], t1.t[:, :n], t2.t[:, :n], ALU.add))
    pT = [k.sb('pT%d' % i, [128, 2 * TT], BF16) for i in range(3)]

    def attend(s, Ls, Qtile, nk):
        yo = I['yo' + s]
        nch = nk // 128
        for q0 in range(0, Ls, TT):
            n = min(TT, Ls - q0)
            po0, po1, pz0, pz1 = ps[4], ps[5], ps[6], ps[7]
            for c in range(nch):
                pt = pT[nxt('pt', 3)]
                for j in range(2):
                    p = P()
                    k.mm(p, p.t[:, :n], [(KT.t[64 * j:64 * j + 64, c * 128:(c + 1) * 128], Qtile.t[64 * j:64 * j + 64, q0:q0 + n])], [KT, Qtile])
                    k.op('act', [p], [pt], lambda e, p=p, pt=pt, j=j: e.activation(pt.t[:, j * TT:j * TT + n], p.t[:, :n], AF.Exp, scale=0.125))
                for j, (po_, pz_) in enumerate(((po0, pz0), (po1, pz1))):
                    for tgt, lhs in ((po_, Vv[:, c, :]), (pz_, ones.t[:, :])):
                        k._wait('pe', k._deps([pt, VV, ones], [tgt]))
                        ins = nc.tensor.matmul(tgt.t[:, :n], lhs, pt.t[:, j * TT:j * TT + n], start=(c == 0), stop=(c == nch - 1))
                        k.cnt['pe'] += 1
                        ins.then_inc(k.sem['pe'], 1)
                        k._mark((k.sem['pe'], k.cnt['pe']), [pt, VV, ones], [tgt])
            r0 = F(); r1 = F(); o0 = F(); o1 = F(); oo = F()
            k.op('dve', [pz0], [r0], lambda e: e.reciprocal(r0.t[:, :n], pz0.t[:, :n]))
            k.op('dve', [pz1], [r1], lambda e: e.reciprocal(r1.t[:, :n], pz1.t[:, :n]))
            k.op('dve', [po0, r0], [o0], lambda e: e.tensor_tensor(o0.t[:, :n], po0.t[:, :n], r0.t[:, :n], ALU.mult))
            k.op('dve', [po1, r1], [o1], lambda e: e.tensor_tensor(o1.t[:, :n], po1.t[:, :n], r1.t[:, :n], ALU.mult))
            k.op('dve', [o0, o1, pp], [oo], lambda e: e.scalar_tensor_tensor(oo.t[:, :n], o1.t[:, :n], _col(pp.t, oDV + 4), o0.t[:, :n], ALU.mult, ALU.add))
            sq = B(); p = P(); rs = F(); t3 = F(); ob = outb[nxt('o')]
            k.op('act', [oo], [sq], lambda e: e.activation(sq.t[:, :n], oo.t[:, :n], AF.Square))
            k.mm(p, p.t[:, :n], [(ones.t[:, :], sq.t[:, :n])], [ones, sq])
            k.op('act', [p, onec], [rs], lambda e: e.activation(rs.t[:, :n], p.t[:, :n], AF.Sqrt, bias=onec.t[:, 1:2], scale=1.0 / 128))
            k.op('dve', [rs], [t3], lambda e: e.reciprocal(t3.t[:, :n], rs.t[:, :n]))
            k.op('dve', [oo, t3, pp], [ob], lambda e: e.scalar_tensor_tensor(ob.t[:, :n], oo.t[:, :n], _col(pp.t, oDV + 5), t3.t[:, :n], ALU.mult, ALU.mult))
            store(yo[3, :, q0:q0 + n], ob, ob.t[:, :n])

    attend('c', CT, QC, CT)
    attend('l', L, QT, LK)
    k.pop_scope()
    k.finish()
    return nc


POOL_HALF = (1, 2, 4, 8)


def _dft(n):
    i = np.arange(n)
    ang = 2.0 * np.pi * np.outer(i, i) / n
    return np.cos(ang), np.sin(ang)


def m_consts(L, CT, j, lam_init):
    c = {}
    C, S = _dft(128)
    c['f128'] = np.concatenate([C, -S, S], 1).astype(np.float32)
    rm = np.zeros((128, 128), np.float32)
    for d in range(128):
        if d % 32 < 16:
            rm[d + 16, d] = -1.0
        else:
            rm[d - 16, d] = 1.0
    c['rm'] = rm
    t = np.arange(L)
    inv = (10000.0 ** (-np.arange(16, dtype=np.float32) / 16)).astype(np.float32)
    ang_r = (t // 64).astype(np.float32)[:, None] * inv
    ang_c = (t % 64).astype(np.float32)[:, None] * inv
    ang = np.zeros((128, L), np.float32)
    for p in range(128):
        d = p % 64
        ang[p] = ang_r[:, d % 16] if d < 32 else ang_c[:, (d - 32) % 16]
    c['cossin'] = np.stack([np.cos(ang), np.sin(ang)]).astype(np.float32)
    for s, Ls in (('c', CT), ('l', L)):
        L1 = Ls // 128
        C1, S1 = _dft(L1)
        c['F1' + s] = np.concatenate([C1, -S1, S1, C1], 1).astype(np.float32)
        a = 2.0 * np.pi * np.outer(np.arange(128), np.arange(L1)) / Ls
        c['Tw' + s] = np.concatenate([np.cos(a), np.sin(a)], 1).astype(np.float32)
        tt = np.arange(Ls)
        half = POOL_HALF[j]
        lo = np.clip(tt - half, 0, Ls - 1)
        hi = np.clip(tt + half - 1, 0, Ls - 1)
        c['icnt' + s] = np.ascontiguousarray(np.broadcast_to((1.0 / (hi - lo + 1)).astype(np.float32)[None], (128, Ls)))
    sel = np.zeros((128, 4), np.float32)
    sel[:, j] = 1.0
    c['sel'] = sel
    c['lam_init'] = np.full((128, 1), lam_init, np.float32)
    c['one_m_lam_init'] = np.full((128, 1), 1.0 - lam_init, np.float32)
    return c


def m_inputs(pT, pcT, j, L, CT, P, consts):
    inp = {}
    for s, Ls, src in (('c', CT, pcT), ('l', L, pT)):
        sl = lambda base: src[base + 128 * j: base + 128 * (j + 1)]
        z = lambda n: np.zeros((128, n), src.dtype)
        inp['ux' + s] = np.concatenate([z(2), sl(0), z(1)], 1)
        inp['ug' + s] = np.ascontiguousarray(sl(512))
        inp['up' + s] = np.concatenate([z(8), sl(1024), z(8)], 1)
        L1 = Ls // 128
        inp['uf' + s] = np.ascontiguousarray(sl(1536).reshape(128, L1, 128).transpose(0, 2, 1).reshape(128, Ls))
        inp['q' + s] = np.ascontiguousarray(sl(2048))
        inp['k' + s] = np.ascontiguousarray(sl(2560))
        inp['v' + s] = np.ascontiguousarray(sl(3072).T)
        for nm in ('icnt', 'F1', 'Tw'):
            inp[nm + s] = consts[nm + s]
    lw = np.zeros((128, 512), np.float32)
    for d in range(2):
        for gi, W in enumerate((P['lru_wa'], P['lru_wi'])):
            for a in range(2):
                lw[64 * a:64 * a + 64, (2 * d + gi) * 128 + 64 * a:(2 * d + gi) * 128 + 64 * a + 64] = W[d, 2 * j + a]
    inp['lruw'] = lw
    cs = slice(128 * j, 128 * (j + 1))
    inp['lrup'] = np.ascontiguousarray(np.stack(
        [P['lru_conv_w'][0, cs], P['lru_conv_w'][1, cs], P['lru_conv_w'][2, cs], P['lru_conv_w'][3, cs], P['lru_conv_b'][cs],
         P['lru_ba'][0, cs], P['lru_ba'][1, cs], P['lru_bi'][0, cs], P['lru_bi'][1, cs], P['lru_lam'][0, cs], P['lru_lam'][1, cs]], 1).astype(np.float32))
    inp['poolw'] = np.ascontiguousarray(P['pool_w'][j])
    inp['poolp'] = np.ascontiguousarray(np.concatenate([P['pool_scale'][cs][:, None], consts['sel']], 1).astype(np.float32))
    inp['f128'] = consts['f128']
    inp['rm'] = consts['rm']
    inp['cossin'] = consts['cossin']
    inp['dlam'] = np.ascontiguousarray(np.broadcast_to(P['diff_lam'].reshape(1, 256), (128, 256)))
    inp['attp'] = np.ascontiguousarray(np.concatenate([P['diff_subln_g'][:, None], consts['lam_init'], consts['one_m_lam_init']], 1).astype(np.float32))
    return inp


ACOLS = 6 * D // NCORES


def build_ADA(depth=4):
    nc = bass.Bass("TRN2", target_bir_lowering=False)
    k = K(nc)
    cT = nc.dram_tensor("cT", [128, DC * 3], F32, kind="ExternalInput").ap()
    aw = nc.dram_tensor("aw", [depth * DC, 128, ACOLS], F32, kind="ExternalInput").ap()
    ab_in = nc.dram_tensor("ab", [3, depth * ACOLS], F32, kind="ExternalInput").ap()
    mo = nc.dram_tensor("mo", [3, depth * ACOLS], F32, kind="ExternalOutput").ap()
    ct = k.sb("ct", [128, DC * 3], F32)
    sc = k.sb("sc", [128, DC * 3], F32)
    bias = k.sb("bias", [3, depth * ACOLS], F32)
    res = k.sb("res", [3, depth * ACOLS], F32)
    k.dma('sp', ct, ct.t[:, :], None, cT)
    k.dma('sp', bias, bias.t[:, :], None, ab_in)
    k.op('act', [ct], [sc], lambda e: e.activation(sc.t[:, :], ct.t[:, :], AF.Silu))
    wb = [k.sb("awb%d" % i, [128, ACOLS], F32) for i in range(4)]
    pss = [k.ps("aps%d" % i, [128, 512]) for i in range(3)]
    for l in range(depth):
        for kc in range(DC):
            w = wb[(l * DC + kc) % 4]
            k.dma('sp', w, w.t[:, :], None, aw[l * DC + kc])
            for g in range(3):
                k._wait('pe', k._deps([w, sc], [pss[g]]))
                ins = nc.tensor.matmul(pss[g].t[0:3, :], sc.t[:, kc * 3:kc * 3 + 3], w.t[:, g * 512:(g + 1) * 512], start=(kc == 0), stop=(kc == DC - 1))
                k.cnt['pe'] += 1
                ins.then_inc(k.sem['pe'], 1)
                k._mark((k.sem['pe'], k.cnt['pe']), [w, sc], [pss[g]])
        for g in range(3):
            o = l * ACOLS + g * 512
            k.op('dve', [pss[g], bias], [res], lambda e, g=g, o=o: e.tensor_tensor(res.t[0:3, o:o + 512], pss[g].t[0:3, :], bias.t[0:3, o:o + 512], ALU.add))
    k.dma('pool', None, mo, res, res.t[:, :], final=True)
    k.finish()
    return nc


_CACHE = {}


def _prog(key, fn):
    if key not in _CACHE:
        _CACHE[key] = fn()
    return _CACHE[key]


def kernel(x, c, ctx, c_ctx, ada_w, ada_b, norm_g, w_in, lru_conv_w, lru_conv_b, lru_wa, lru_ba, lru_wi, lru_bi,
           lru_lam, pool_w, pool_scale, fourier_w, diff_lam, diff_subln_g, w_out, ffn_w_up, ffn_conv_w, ffn_conv_b,
           ffn_w_down):
    A = lambda a: np.asarray(a)
    x, c, ctx, c_ctx = A(x), A(c), A(ctx), A(c_ctx)
    ada_w, ada_b, norm_g, w_in = A(ada_w), A(ada_b), A(norm_g), A(w_in)
    Bn, L, _ = x.shape
    CT = ctx.shape[1]
    depth = w_in.shape[0]
    NS = NCORES // Bn
    NT = L // NS
    TT = 512
    ntile = NT // TT
    cores = list(range(NCORES))
    PL = [dict(lru_conv_w=A(lru_conv_w)[l], lru_conv_b=A(lru_conv_b)[l], lru_wa=A(lru_wa)[l], lru_ba=A(lru_ba)[l],
               lru_wi=A(lru_wi)[l], lru_bi=A(lru_bi)[l], lru_lam=A(lru_lam)[l], pool_w=A(pool_w)[l],
               pool_scale=A(pool_scale)[l], diff_lam=A(diff_lam)[l], diff_subln_g=A(diff_subln_g)[l]) for l in range(depth)]

    c3 = np.concatenate([c, c_ctx[None]], 0).astype(np.float32)
    cT = np.ascontiguousarray(c3.reshape(3, DC, 128).transpose(2, 1, 0).reshape(128, DC * 3))
    maps = []
    for r in cores:
        cs = slice(r * ACOLS, (r + 1) * ACOLS)
        aw = np.ascontiguousarray(ada_w[:, :, cs].reshape(depth * DC, 128, ACOLS))
        ab = np.ascontiguousarray(np.broadcast_to(ada_b[:, cs].reshape(1, depth * ACOLS), (3, depth * ACOLS)))
        maps.append(dict(cT=cT, aw=aw, ab=ab))
    res = run_bass_kernel_spmd(_prog('ada', lambda: build_ADA(depth)), maps, core_ids=cores)
    mod = np.concatenate([res.results[r]['mo'].reshape(3, depth, ACOLS) for r in cores], axis=2)
    mod = mod.reshape(3, depth, 6, D)
    del maps, res

    xT = [np.ascontiguousarray(x[b].T) for b in range(Bn)]
    xcT = [np.ascontiguousarray(ctx[b].T) for b in range(Bn)]

    def t_launch(lC, lA, yT, ycT):
        do_C, do_A = lC is not None, lA is not None
        NH = 2 * ntile if do_C else 0
        common = {}
        if do_C:
            common.update(ngC=cols(norm_g[lC, 1], norm_g[lC, 2], norm_g[lC, 3]),
                          cw=np.ascontiguousarray(A(ffn_conv_w)[lC].reshape(3, FC, 128).transpose(2, 0, 1).reshape(128, 3 * FC)),
                          cb=colvec(A(ffn_conv_b)[lC]), wf=blk(A(fourier_w)[lC], 512), wo=blk(A(w_out)[lC], 256),
                          wu=blk_up(A(ffn_w_up)[lC]), wd=blk(A(ffn_w_down)[lC], 128))
        if do_A:
            common.update(ngA=cols(norm_g[lA, 0]), wi=blk(w_in[lA], 256))
        maps = []
        for r in cores:
            b, s = divmod(r, NS)
            t0 = s * NT
            m = dict(common)
            hm = np.zeros(max(NH, 1), np.float32)
            hidx = []
            for i in range(ntile if do_C else 0):
                for hi, tpos in enumerate((t0 + i * TT - 1, t0 + (i + 1) * TT)):
                    ok = 0 <= tpos < L
                    hm[2 * i + hi] = 1.0 if ok else 0.0
                    hidx.append(tpos if ok else 0)
            xh = xT[b][:, hidx] if do_C else np.zeros((D, 0), np.float32)
            m['xT'] = np.ascontiguousarray(np.concatenate([xh, xT[b][:, t0:t0 + NT], xcT[b]], 1))
            if do_C:
                m['yT'] = np.ascontiguousarray(np.concatenate([yT[b][:, hidx], yT[b][:, t0:t0 + NT], ycT[b]], 1))
                m['hmask'] = np.ascontiguousarray(np.broadcast_to(hm[None, :NH], (128, NH)))
                m['modC'] = cols(*[mod[row, lC, i] for row in (b, 2) for i in (2, 3, 4, 5)])
            if do_A:
                m['modA'] = cols(*[mod[row, lA, i] for row in (b, 2) for i in (0, 1)])
            maps.append(m)
        res = run_bass_kernel_spmd(_prog(('T', NT, do_C, do_A, CT), lambda: build_T(NT, do_C, do_A, CT, TT)), maps, core_ids=cores)
        pT = pcT = None
        if do_C:
            for b in range(Bn):
                xT[b] = np.ascontiguousarray(np.concatenate([res.results[b * NS + s]['xo'][:, :NT] for s in range(NS)], 1))
                xcT[b] = np.ascontiguousarray(res.results[b * NS]['xo'][:, NT:])
        if do_A:
            pT = [np.concatenate([res.results[b * NS + s]['po'][:, :NT] for s in range(NS)], 1) for b in range(Bn)]
            pcT = [res.results[b * NS]['po'][:, NT:] for b in range(Bn)]
        return pT, pcT

    pT, pcT = t_launch(None, 0, None, None)
    for l in range(depth):
        lam_init = 0.8 - 0.6 * math.exp(-0.3 * l)
        maps = []
        for r in cores:
            b, j = divmod(r, NS)
            maps.append(m_inputs(pT[b], pcT[b], j, L, CT, PL[l], m_consts(L, CT, j, lam_init)))
        res = run_bass_kernel_spmd(_prog(('M', L, CT), lambda: build_M(L, CT, TT)), maps, core_ids=cores)
        yT, ycT = [], []
        for b in range(Bn):
            yl = np.stack([res.results[b * NS + j]['yol'] for j in range(NS)], 1)
            yc = np.stack([res.results[b * NS + j]['yoc'] for j in range(NS)], 1)
            yT.append(yl.reshape(D, L))
            ycT.append(yc.reshape(D, CT))
        del maps, res
        pT, pcT = t_launch(l, l + 1 if l + 1 < depth else None, yT, ycT)
    out = np.stack([xT[b].T for b in range(Bn)], 0).astype(np.float32)
    return np.ascontiguousarray(out)
```

```python
import contextlib
import math
import numpy as np
import ml_dtypes
import concourse.bass as bass
import concourse.mybir as mybir
from concourse.bass_utils import run_bass_kernel_spmd

F32 = mybir.dt.float32
BF16 = mybir.dt.bfloat16
AF = mybir.ActivationFunctionType
ALU = mybir.AluOpType
NPBF = ml_dtypes.bfloat16

D = 2048
DC = D // 128
DFF = 5632
FC = DFF // 128
DIN = 3584
EPS = 1e-6
NCORES = 8


class Tk:
    def __init__(self, t, name):
        self.t = t
        self.name = name
        self.w = None
        self.r = {}
        self.dsem = {}
        self.dcnt = {}
        self.psum = False

    def __getitem__(self, idx):
        return self.t[idx]


class K:
    def __init__(self, nc):
        self.nc = nc
        self.es = contextlib.ExitStack()
        self.root = self.es
        self.E = {'pe': nc.tensor, 'act': nc.scalar, 'dve': nc.vector, 'pool': nc.gpsimd, 'sp': nc.sync}
        self.sem = {e: self.es.enter_context(nc.semaphore('s_' + e)) for e in ('pe', 'act', 'dve', 'pool')}
        self.cnt = dict.fromkeys(self.sem, 0)
        self.known = {e: {} for e in self.E}
        self.final = []
        self.nd = 0

    def sb(self, name, shape, dt):
        t = Tk(self.es.enter_context(self.nc.sbuf_tensor(name, shape, dt)), name)
        if getattr(self, 'scope_tiles', None) is not None:
            self.scope_tiles.append(t)
        return t

    def push_scope(self):
        self.saved_es = self.es
        self.es = contextlib.ExitStack()
        self.scope_tiles = []

    def pop_scope(self):
        deps = [(self.sem[e], self.cnt[e]) for e in self.sem if self.cnt[e]]
        for t in self.scope_tiles:
            if t.w:
                deps.append(t.w)
            deps.extend(t.r.values())
        for e in self.E:
            self._wait(e, deps)
        self.es.close()
        self.es = self.saved_es
        self.scope_tiles = None

    def ps(self, name, shape, dt=F32):
        t = Tk(self.es.enter_context(self.nc.psum_tensor(name, shape, dt)), name)
        t.psum = True
        return t

    def dram(self, name, shape, dt, kind):
        return Tk(self.nc.dram_tensor(name, shape, dt, kind=kind).ap(), name)

    def _wait(self, e, deps):
        kn = self.known[e]
        for sem, val in deps:
            if e == 'pe' and sem is self.sem['pe']:
                continue
            if kn.get(id(sem), 0) < val:
                self.E[e].wait_ge(sem, val)
                kn[id(sem)] = val

    @staticmethod
    def _deps(reads, writes):
        d = []
        for t in reads:
            if t.w:
                d.append(t.w)
        for t in writes:
            if t.w:
                d.append(t.w)
            d.extend(t.r.values())
        return d

    @staticmethod
    def _mark(tok, reads, writes):
        for t in reads:
            t.r[id(tok[0])] = tok
        for t in writes:
            t.w = tok
            t.r = {}

    def op(self, e, reads, writes, fn):
        writes = list(writes) + [t for t in reads if t.psum]
        self._wait(e, self._deps(reads, writes))
        ins = fn(self.E[e])
        self.cnt[e] += 1
        ins.then_inc(self.sem[e], 1)
        self._mark((self.sem[e], self.cnt[e]), reads, writes)

    def mm(self, out, out_ap, pairs, reads, **kw):
        self._wait('pe', self._deps(reads, [out]))
        n = len(pairs)
        for i, (l, r) in enumerate(pairs):
            ins = self.nc.tensor.matmul(out_ap, l, r, start=(i == 0), stop=(i == n - 1), **kw)
        self.cnt['pe'] += 1
        ins.then_inc(self.sem['pe'], 1)
        self._mark((self.sem['pe'], self.cnt['pe']), reads, [out])

    def dma(self, q, out_tk, out_ap, in_tk, in_ap, final=False, **kw):
        reads = [in_tk] if in_tk is not None else []
        writes = [out_tk] if out_tk is not None else []
        self._wait(q, self._deps(reads, writes))
        own = out_tk if out_tk is not None else in_tk
        if q not in own.dsem:
            self.nd += 1
            own.dsem[q] = self.root.enter_context(self.nc.semaphore('d%d' % self.nd))
            own.dcnt[q] = 0
        ins = self.E[q].dma_start(out=out_ap, in_=in_ap, **kw)
        own.dcnt[q] += 16
        ins.then_inc(own.dsem[q], 16)
        tok = (own.dsem[q], own.dcnt[q])
        self._mark(tok, reads, writes)
        if final:
            self.final.append(tok)

    def finish(self):
        self._wait('sp', self.final)
        self._wait('sp', [(self.sem[e], self.cnt[e]) for e in self.sem if self.cnt[e]])
        self.es.close()


def _col(t, i):
    return t[:, i:i + 1]


def build_T(NT, do_C, do_A, CT=256, TT=512, dbg=0):
    nc = bass.Bass("TRN2", target_bir_lowering=False)
    k = K(nc)
    ntile = NT // TT
    NH = 2 * ntile if do_C else 0
    NTOT = NH + NT + CT
    NOUT = NT + CT

    def din(name, shape, dt=F32):
        return nc.dram_tensor(name, shape, dt, kind="ExternalInput").ap()

    xT = din("xT", [D, NTOT]).rearrange("(c p) n -> p c n", p=128)
    if do_C:
        yT = din("yT", [D, NTOT], BF16).rearrange("(c p) n -> p c n", p=128)
        hmask = din("hmask", [128, NH])
        modC = din("modC", [128, 2 * 4 * DC])
        ngC = din("ngC", [128, 3 * DC])
        cw = din("cw", [128, 3 * FC])
        cb = din("cb", [128, FC])
        wf_in = din("wf", [1, 128, 4 * 512])
        wo_in = din("wo", [8, 128, DC * 256])
        wu_in = din("wu", [FC, 128, DC * 256])
        wd_in = din("wd", [DC, 128, FC * 128])
        xo = nc.dram_tensor("xo", [D, NOUT], F32, kind="ExternalOutput").ap().rearrange("(c p) n -> p c n", p=128)
    if do_A:
        modA = din("modA", [128, 2 * 2 * DC])
        ngA = din("ngA", [128, DC])
        wi_in = din("wi", [14, 128, DC * 256])
        po = nc.dram_tensor("po", [DIN, NOUT], BF16, kind="ExternalOutput").ap()

    def scratch(name, src):
        G, _, Fw = src.shape
        tk = k.dram(name + "_bf", [G, 128, Fw], BF16, "Internal")
        step = max(1, (1 << 20) // (128 * Fw))
        mld = max(d for d in range(1, 2049) if Fw % d == 0)
        for g0 in range(0, G, step):
            g1 = min(G, g0 + step)
            k.dma('pool', tk, tk.t[g0:g1], None, src[g0:g1], max_dma_last_dim=mld)
        return tk

    WB = 3
    wbuf = [k.sb("wbuf%d" % i, [128, FC * 128], BF16) for i in range(WB)]
    wctr = [0]

    def wload(tk, g, width):
        b = wbuf[wctr[0] % WB]
        wctr[0] += 1
        k.dma('sp', b, b.t[:, 0:width], tk, tk.t[g])
        return b

    if do_C:
        wf_s = scratch("wf", wf_in)
        wo_s = scratch("wo", wo_in)
        wu_s = scratch("wu", wu_in)
        wd_s = scratch("wd", wd_in)
    if do_A:
        wi_s = scratch("wi", wi_in)

    ones = k.sb("ones", [128, 128], BF16)
    k.op('dve', [], [ones], lambda e: e.memset(ones.t[:], 1.0))
    epsc = k.sb("epsc", [128, 1], F32)
    k.op('dve', [], [epsc], lambda e: e.memset(epsc.t[:], EPS))
    prm = k.sb("prm", [128, 16 * DC + 4 * FC + NH], F32)
    cst = k.sb("cst", [128, 12 * DC], F32)
    o_modC, o_ngC, o_modA, o_ngA, o_cw, o_cb, o_hm = 0, 8 * DC, 11 * DC, 15 * DC, 16 * DC, 16 * DC + 3 * FC, 16 * DC + 4 * FC
    cG1, cA2, cB2, cG2, cA1, cB1 = 0, 2 * DC, 4 * DC, 6 * DC, 8 * DC, 10 * DC
    if do_C:
        k.dma('sp', prm, prm.t[:, o_modC:o_modC + 8 * DC], None, modC)
        k.dma('sp', prm, prm.t[:, o_ngC:o_ngC + 3 * DC], None, ngC)
        k.dma('sp', prm, prm.t[:, o_cw:o_cw + 3 * FC], None, cw)
        k.dma('sp', prm, prm.t[:, o_cb:o_cb + FC], None, cb)
        k.dma('sp', prm, prm.t[:, o_hm:o_hm + NH], None, hmask)
        for m in range(2):
            mo = o_modC + m * 4 * DC
            k.op('dve', [prm], [cst], lambda e, m=m, mo=mo: e.tensor_tensor(
                cst.t[:, cG1 + m * DC:cG1 + (m + 1) * DC], prm.t[:, mo:mo + DC], prm.t[:, o_ngC:o_ngC + DC], ALU.mult))
            k.op('dve', [prm], [cst], lambda e, m=m, mo=mo: e.scalar_tensor_tensor(
                cst.t[:, cA2 + m * DC:cA2 + (m + 1) * DC], prm.t[:, mo + 2 * DC:mo + 3 * DC], 1.0,
                prm.t[:, o_ngC + DC:o_ngC + 2 * DC], ALU.add, ALU.mult))
            k.op('dve', [prm], [cst], lambda e, m=m, mo=mo: e.tensor_copy(
                cst.t[:, cB2 + m * DC:cB2 + (m + 1) * DC], prm.t[:, mo + DC:mo + 2 * DC]))
            k.op('dve', [prm], [cst], lambda e, m=m, mo=mo: e.tensor_tensor(
                cst.t[:, cG2 + m * DC:cG2 + (m + 1) * DC], prm.t[:, mo + 3 * DC:mo + 4 * DC],
                prm.t[:, o_ngC + 2 * DC:o_ngC + 3 * DC], ALU.mult))
    if do_A:
        k.dma('sp', prm, prm.t[:, o_modA:o_modA + 4 * DC], None, modA)
        k.dma('sp', prm, prm.t[:, o_ngA:o_ngA + DC], None, ngA)
        for m in range(2):
            mo = o_modA + m * 2 * DC
            k.op('dve', [prm], [cst], lambda e, m=m, mo=mo: e.scalar_tensor_tensor(
                cst.t[:, cA1 + m * DC:cA1 + (m + 1) * DC], prm.t[:, mo + DC:mo + 2 * DC], 1.0,
                prm.t[:, o_ngA:o_ngA + DC], ALU.add, ALU.mult))
            k.op('dve', [prm], [cst], lambda e, m=m, mo=mo: e.tensor_copy(
                cst.t[:, cB1 + m * DC:cB1 + (m + 1) * DC], prm.t[:, mo:mo + DC]))

    xs = k.sb("xs", [128, DC, TT], F32)
    of = k.sb("of", [128, DC, TT], F32)
    hb = k.sb("hb", [128, DC, TT], BF16)
    zb = k.sb("zb", [128, 4, TT], BF16)
    ab = k.sb("ab", [128, FC, TT], BF16)
    ghalo = k.sb("ghalo", [128, FC, max(NH, 2)], F32)
    rstd = k.sb("rstd", [128, TT], F32)
    NR = 3
    sqb = [k.sb("sqb%d" % i, [128, TT], BF16) for i in range(NR)]
    tmp = [k.sb("tmp%d" % i, [128, TT], F32) for i in range(NR)]
    gbuf = [k.sb("gbuf%d" % i, [128, TT + 2], F32) for i in range(NR)]
    cva = [k.sb("cva%d" % i, [128, TT], F32) for i in range(NR)]
    cvb = [k.sb("cvb%d" % i, [128, TT], F32) for i in range(NR)]
    geb = [k.sb("geb%d" % i, [128, TT], F32) for i in range(NR)]
    pbo = [k.sb("pbo%d" % i, [128, TT], BF16) for i in range(NR)]
    pg = [k.ps("pg%d" % i, [128, TT]) for i in range(2)]
    pv = [k.ps("pv%d" % i, [128, TT]) for i in range(2)]
    pm = [k.ps("pm%d" % i, [128, TT]) for i in range(2)]
    pst = k.ps("pst", [128, TT])
    rot = {'sq': 0, 'tmp': 0, 'g': 0, 'pg': 0, 'pm': 0, 'pb': 0}

    def nxt(key, n):
        v = rot[key] % n
        rot[key] += 1
        return v

    def sumsq_chunk(src_tk, src_ap, c, n):
        s = sqb[nxt('sq', NR)]
        k.op('act', [src_tk], [s], lambda e: e.activation(s.t[:, :n], src_ap, AF.Square))
        k._wait('pe', k._deps([s, ones], [pst]))
        ins = nc.tensor.matmul(pst.t[:, :n], ones.t[:, :], s.t[:, :n], start=(c == 0), stop=(c == DC - 1))
        k.cnt['pe'] += 1
        ins.then_inc(k.sem['pe'], 1)
        k._mark((k.sem['pe'], k.cnt['pe']), [s, ones], [pst])

    def make_rstd(n):
        t = tmp[nxt('tmp', NR)]
        k.op('act', [pst, epsc], [t], lambda e: e.activation(t.t[:, :n], pst.t[:, :n], AF.Sqrt, bias=epsc.t[:, 0:1], scale=1.0 / D))
        k.op('dve', [t], [rstd], lambda e: e.reciprocal(rstd.t[:, :n], t.t[:, :n]))

    def resid_update(gcol0, n):
        for c in range(DC):
            t = tmp[nxt('tmp', NR)]
            k.op('dve', [of, cst, rstd], [t], lambda e, c=c, t=t: e.scalar_tensor_tensor(
                t.t[:, :n], of.t[:, c, :n], _col(cst.t, gcol0 + c), rstd.t[:, :n], ALU.mult, ALU.mult))
            k.op('pool', [t, xs], [xs], lambda e, c=c, t=t: e.tensor_tensor(xs.t[:, c, :n], xs.t[:, c, :n], t.t[:, :n], ALU.add))

    def norm_mod(acol0, bcol0, n):
        for c in range(DC):
            sumsq_chunk(xs, xs.t[:, c, :n], c, n)
        make_rstd(n)
        for c in range(DC):
            t = tmp[nxt('tmp', NR)]
            k.op('dve', [xs, cst, rstd], [t], lambda e, c=c, t=t: e.scalar_tensor_tensor(
                t.t[:, :n], xs.t[:, c, :n], _col(cst.t, acol0 + c), rstd.t[:, :n], ALU.mult, ALU.mult))
            k.op('act', [t, cst], [hb], lambda e, c=c, t=t: e.activation(
                hb.t[:, c, :n], t.t[:, :n], AF.Identity, bias=_col(cst.t, bcol0 + c), scale=1.0))

    segs = []
    if do_C and dbg < 2:
        segs.append(('halo', 0, NH, 0, -1))
    for i in range(ntile):
        segs.append(('lat', NH + i * TT, TT, 0, i))
    segs.append(('ctx', NH + NT, CT, 1, -1))

    for kind, c0, n, m, ti in segs:
        k.dma('sp', xs, xs.t[:, :, :n], None, xT[:, :, c0:c0 + n])
        if do_C:
            k.dma('sp', hb, hb.t[:, :, :n], None, yT[:, :, c0:c0 + n])
            wt = wload(wf_s, 0, 4 * 512)
            for mj in range(4 if dbg not in (3,) else 0):
                p = pm[nxt('pm', 2)]
                k.mm(p, p.t[:, :n], [(wt.t[:, kc * 512 + mj * 128:kc * 512 + (mj + 1) * 128], hb.t[:, 8 + kc, :n]) for kc in range(4)], [wt, hb])
                k.op('act', [p], [zb], lambda e, p=p, mj=mj: e.activation(zb.t[:, mj, :n], p.t[:, :n], AF.Identity))
            for g in range(8 if dbg not in (3, 4) else 0):
                wt = wload(wo_s, g, DC * 256)
                for j in range(2):
                    mi = 2 * g + j
                    p = pm[nxt('pm', 2)]
                    prs = []
                    for kc in range(DC):
                        rhs = zb.t[:, kc - 8, :n] if 8 <= kc < 12 else hb.t[:, kc, :n]
                        prs.append((wt.t[:, kc * 256 + j * 128:kc * 256 + (j + 1) * 128], rhs))
                    k.mm(p, p.t[:, :n], prs, [wt, hb, zb])
                    k.op('dve', [p], [of], lambda e, p=p, mi=mi: e.tensor_copy(of.t[:, mi, :n], p.t[:, :n]))
                    sumsq_chunk(of, of.t[:, mi, :n], mi, n)
            if dbg not in (3, 4, 5):
                make_rstd(n)
                resid_update(cG1 + m * DC, n)
            if dbg == 0:
                norm_mod(cA2 + m * DC, cB2 + m * DC, n)
            for mi in range(FC if dbg == 0 else 0):
                wt = wload(wu_s, mi, DC * 256)
                pgi = nxt('pg', 2)
                pgt, pvt = pg[pgi], pv[pgi]
                k.mm(pgt, pgt.t[:, :n], [(wt.t[:, kc * 256:kc * 256 + 128], hb.t[:, kc, :n]) for kc in range(DC)], [wt, hb])
                if kind == 'halo':
                    k.op('dve', [pgt, prm], [ghalo], lambda e, pgt=pgt, mi=mi: e.tensor_tensor(
                        ghalo.t[:, mi, :], pgt.t[:, :n], prm.t[:, o_hm:o_hm + NH], ALU.mult))
                    continue
                k.mm(pvt, pvt.t[:, :n], [(wt.t[:, kc * 256 + 128:kc * 256 + 256], hb.t[:, kc, :n]) for kc in range(DC)], [wt, hb])
                gi = nxt('g', NR)
                gb, ca, cbb, ge = gbuf[gi], cva[gi], cvb[gi], geb[gi]
                k.op('act', [pgt], [gb], lambda e, gb=gb, pgt=pgt: e.activation(gb.t[:, 1:n + 1], pgt.t[:, :n], AF.Identity))
                if kind == 'lat':
                    k.op('pool', [ghalo], [gb], lambda e, gb=gb, mi=mi: e.tensor_copy(gb.t[:, 0:n + 2:n + 1], ghalo.t[:, mi, 2 * ti:2 * ti + 2]))
                else:
                    k.op('pool', [], [gb], lambda e, gb=gb: e.memset(gb.t[:, 0:n + 2:n + 1], 0.0))
                k.op('dve', [gb, prm], [ca], lambda e, gb=gb, ca=ca, mi=mi: e.tensor_scalar(
                    ca.t[:, :n], gb.t[:, 0:n], _col(prm.t, o_cw + mi), _col(prm.t, o_cb + mi), ALU.mult, ALU.add))
                k.op('dve', [gb, prm, ca], [cbb], lambda e, gb=gb, ca=ca, cbb=cbb, mi=mi: e.scalar_tensor_tensor(
                    cbb.t[:, :n], gb.t[:, 1:n + 1], _col(prm.t, o_cw + FC + mi), ca.t[:, :n], ALU.mult, ALU.add))
                k.op('dve', [gb, prm, cbb], [ca], lambda e, gb=gb, ca=ca, cbb=cbb, mi=mi: e.scalar_tensor_tensor(
                    ca.t[:, :n], gb.t[:, 2:n + 2], _col(prm.t, o_cw + 2 * FC + mi), cbb.t[:, :n], ALU.mult, ALU.add))
                k.op('act', [ca], [ge], lambda e, ca=ca, ge=ge: e.activation(ge.t[:, :n], ca.t[:, :n], AF.Gelu))
                k.op('dve', [ge, pvt], [ab], lambda e, ge=ge, pvt=pvt, mi=mi: e.tensor_tensor(ab.t[:, mi, :n], ge.t[:, :n], pvt.t[:, :n], ALU.mult))
            if kind == 'halo':
                continue
            for g in range(DC if dbg == 0 else 0):
                wt = wload(wd_s, g, FC * 128)
                p = pm[nxt('pm', 2)]
                k.mm(p, p.t[:, :n], [(wt.t[:, mi * 128:(mi + 1) * 128], ab.t[:, mi, :n]) for mi in range(FC)], [wt, ab])
                k.op('dve', [p], [of], lambda e, p=p, g=g: e.tensor_copy(of.t[:, g, :n], p.t[:, :n]))
                sumsq_chunk(of, of.t[:, g, :n], g, n)
            if dbg == 0:
                make_rstd(n)
                resid_update(cG2 + m * DC, n)
            k.dma('pool', None, xo[:, :, c0 - NH:c0 - NH + n], xs, xs.t[:, :, :n], final=True)
        if do_A:
            norm_mod(cA1 + m * DC, cB1 + m * DC, n)
            for g in range(14):
                wt = wload(wi_s, g, DC * 256)
                for j in range(2):
                    mi = 2 * g + j
                    p = pm[nxt('pm', 2)]
                    k.mm(p, p.t[:, :n], [(wt.t[:, kc * 256 + j * 128:kc * 256 + (j + 1) * 128], hb.t[:, kc, :n]) for kc in range(DC)], [wt, hb])
                    pb = pbo[nxt('pb', NR)]
                    k.op('act', [p], [pb], lambda e, p=p, pb=pb: e.activation(pb.t[:, :n], p.t[:, :n], AF.Identity))
                    k.dma('pool', None, po[mi * 128:(mi + 1) * 128, c0 - NH:c0 - NH + n], pb, pb.t[:, :n], final=True)
    k.finish()
    return nc


def blk(W, cols):
    Kd, M = W.shape
    KC, G = Kd // 128, M // cols
    return np.ascontiguousarray(W.reshape(KC, 128, G, cols).transpose(2, 1, 0, 3).reshape(G, 128, KC * cols))


def blk_up(W):
    Wg = W[:, :DFF].reshape(DC, 128, FC, 1, 128)
    Wv = W[:, DFF:].reshape(DC, 128, FC, 1, 128)
    Wc = np.concatenate([Wg, Wv], axis=3)
    return np.ascontiguousarray(Wc.transpose(2, 1, 0, 3, 4).reshape(FC, 128, DC * 256))


def colvec(v):
    return np.ascontiguousarray(v.reshape(-1, 128).T)


def cols(*vs):
    return np.ascontiguousarray(np.concatenate([colvec(v) for v in vs], axis=1).astype(np.float32))


def build_M(L, CT=256, TT=512):
    nc = bass.Bass("TRN2", target_bir_lowering=False)
    k = K(nc)
    SEQ = [('c', CT), ('l', L)]

    def din(name, shape, dt=F32):
        return nc.dram_tensor(name, shape, dt, kind="ExternalInput").ap()

    I = {}
    for s, Ls in SEQ:
        I['ux' + s] = din('ux' + s, [128, Ls + 3], BF16)
        I['ug' + s] = din('ug' + s, [128, Ls], BF16)
        I['up' + s] = din('up' + s, [128, Ls + 16], BF16)
        I['uf' + s] = din('uf' + s, [128, Ls], BF16)
        I['q' + s] = din('q' + s, [128, Ls], BF16)
        I['k' + s] = din('k' + s, [128, Ls], BF16)
        I['v' + s] = din('v' + s, [Ls, 128], BF16)
        I['icnt' + s] = din('icnt' + s, [128, Ls])
        L1 = Ls // 128
        I['F1' + s] = din('F1' + s, [L1, 4 * L1])
        I['Tw' + s] = din('Tw' + s, [128, 2 * L1])
        I['yo' + s] = nc.dram_tensor('yo' + s, [4, 128, Ls], BF16, kind="ExternalOutput").ap()
    lruw_in = din('lruw', [128, 512]); lrup_in = din('lrup', [128, 11])
    poolw_in = din('poolw', [128, 128]); poolp_in = din('poolp', [128, 5])
    f128_in = din('f128', [128, 384]); rm_in = din('rm', [128, 128])
    cs_in = din('cossin', [2, 128, L]); dlam_in = din('dlam', [128, 256]); attp_in = din('attp', [128, 3])

    wq = k.sb('wq', [128, 512 + 128 + 384 + 128], BF16)
    oLW, oPW, oF, oRM = 0, 512, 640, 1024
    k.dma('pool', wq, wq.t[:, oLW:oLW + 512], None, lruw_in)
    k.dma('pool', wq, wq.t[:, oPW:oPW + 128], None, poolw_in)
    k.dma('pool', wq, wq.t[:, oF:oF + 384], None, f128_in)
    k.dma('pool', wq, wq.t[:, oRM:oRM + 128], None, rm_in)
    pp = k.sb('pp', [128, 11 + 5 + 3 + 8], F32)
    oLP, oPP, oAP, oDV = 0, 11, 16, 19
    k.dma('sp', pp, pp.t[:, oLP:oLP + 11], None, lrup_in)
    k.dma('sp', pp, pp.t[:, oPP:oPP + 5], None, poolp_in)
    k.dma('sp', pp, pp.t[:, oAP:oAP + 3], None, attp_in)
    ones = k.sb("ones", [128, 128], BF16)
    k.op('dve', [], [ones], lambda e: e.memset(ones.t[:], 1.0))
    onec = k.sb("onec", [128, 2], F32)
    k.op('dve', [], [onec], lambda e: e.memset(onec.t[:, 0:1], 1.0))
    k.op('dve', [onec], [onec], lambda e: e.memset(onec.t[:, 1:2], EPS))
    sc0 = k.sb('sc0', [128, 256], F32)
    k.op('act', [pp], [sc0], lambda e: e.activation(sc0.t[:, 0:2], pp.t[:, oLP + 9:oLP + 11], AF.Exp, scale=-1.0))
    k.op('act', [sc0, onec], [sc0], lambda e: e.activation(sc0.t[:, 2:4], sc0.t[:, 0:2], AF.Ln, bias=onec.t[:, 0:1], scale=1.0))
    k.op('dve', [sc0], [pp], lambda e: e.tensor_scalar(pp.t[:, oDV:oDV + 2], sc0.t[:, 2:4], -8.0, None, ALU.mult))
    k.op('dve', [sc0], [pp], lambda e: e.tensor_scalar(pp.t[:, oDV + 2:oDV + 4], sc0.t[:, 2:4], -16.0, None, ALU.mult))
    dl = k.sb('dl', [128, 256], F32)
    k.dma('sp', dl, dl.t[:, :], None, dlam_in)
    k.op('dve', [dl], [sc0], lambda e: e.tensor_tensor(sc0.t[:, 0:64], dl.t[:, 0:64], dl.t[:, 64:128], ALU.mult))
    k.op('dve', [dl, sc0], [sc0], lambda e: e.tensor_tensor(sc0.t[:, 64:128], dl.t[:, 128:192], dl.t[:, 192:256], ALU.mult))
    k.op('dve', [sc0], [sc0], lambda e: e.reduce_sum(sc0.t[:, 128:130], sc0.t[:, 0:128].rearrange("p (a b) -> p a b", a=2), mybir.AxisListType.X))
    k.op('act', [sc0], [sc0], lambda e: e.activation(sc0.t[:, 130:132], sc0.t[:, 128:130], AF.Exp))
    k.op('dve', [sc0], [sc0], lambda e: e.tensor_tensor(sc0.t[:, 132:133], sc0.t[:, 131:132], sc0.t[:, 130:131], ALU.subtract))
    k.op('dve', [sc0, pp], [pp], lambda e: e.tensor_tensor(pp.t[:, oDV + 4:oDV + 5], sc0.t[:, 132:133], pp.t[:, oAP + 1:oAP + 2], ALU.subtract))

    k.op('dve', [pp], [pp], lambda e: e.tensor_tensor(pp.t[:, oDV + 5:oDV + 6], pp.t[:, oAP:oAP + 1], pp.t[:, oAP + 2:oAP + 3], ALU.mult))
    NR = 3
    rot = {}

    def nxt(key, n=NR):
        v = rot.get(key, 0)
        rot[key] = v + 1
        return v % n

    ps = [k.ps("ps%d" % i, [128, 512]) if i in (0, 1, 4, 5) else None for i in range(8)]
    f32t = [k.sb("f32t%d" % i, [128, TT + 16], F32) for i in range(12)]
    bft = [k.sb("bft%d" % i, [128, TT + 16], BF16) for i in range(8)]
    outb = [k.sb("outb%d" % i, [128, TT], BF16) for i in range(NR)]

    def F():
        return f32t[nxt('f', 12)]

    def B():
        return bft[nxt('b', 8)]

    def P():
        return ps[nxt('p', 2)]

    def store(y_ap, src_tk, src_ap):
        k.dma('pool', None, y_ap, src_tk, src_ap, final=True)

    hstate = k.sb('hstate', [128, 4], F32)
    k.push_scope()
    hf_all = k.sb('hf_all', [128, L], F32)

    def lru(s, Ls):
        yo = I['yo' + s]
        nt = (Ls + TT - 1) // TT
        tiles = [(i * TT, min(TT, Ls - i * TT)) for i in range(nt)]

        def gates(t0, n, d):
            xt = B()
            k.dma('sp', xt, xt.t[:, :n + 3], None, I['ux' + s][:, t0:t0 + n + 3])
            u = F(); u2 = F()
            k.op('dve', [xt, pp], [u], lambda e: e.tensor_scalar(u.t[:, :n], xt.t[:, 0:n], _col(pp.t, oLP + 0), _col(pp.t, oLP + 4), ALU.mult, ALU.add))
            k.op('dve', [xt, pp, u], [u2], lambda e: e.scalar_tensor_tensor(u2.t[:, :n], xt.t[:, 1:n + 1], _col(pp.t, oLP + 1), u.t[:, :n], ALU.mult, ALU.add))
            k.op('dve', [xt, pp, u2], [u], lambda e: e.scalar_tensor_tensor(u.t[:, :n], xt.t[:, 2:n + 2], _col(pp.t, oLP + 2), u2.t[:, :n], ALU.mult, ALU.add))
            k.op('dve', [xt, pp, u], [u2], lambda e: e.scalar_tensor_tensor(u2.t[:, :n], xt.t[:, 3:n + 3], _col(pp.t, oLP + 3), u.t[:, :n], ALU.mult, ALU.add))
            ub = B()
            k.op('act', [u2], [ub], lambda e: e.activation(ub.t[:, :n], u2.t[:, :n], AF.Identity))
            pr, pi = P(), P()
            k.mm(pr, pr.t[:, :n], [(wq.t[:, oLW + (2 * d) * 128:oLW + (2 * d + 1) * 128], ub.t[:, :n])], [wq, ub])
            k.mm(pi, pi.t[:, :n], [(wq.t[:, oLW + (2 * d + 1) * 128:oLW + (2 * d + 2) * 128], ub.t[:, :n])], [wq, ub])
            r = F(); ig = F(); a = F(); sq = F(); bb = F()
            k.op('act', [pr, pp], [r], lambda e: e.activation(r.t[:, :n], pr.t[:, :n], AF.Sigmoid, bias=_col(pp.t, oLP + 5 + d), scale=1.0))
            k.op('act', [pi, pp], [ig], lambda e: e.activation(ig.t[:, :n], pi.t[:, :n], AF.Sigmoid, bias=_col(pp.t, oLP + 7 + d), scale=1.0))
            k.op('act', [r, pp], [a], lambda e: e.activation(a.t[:, :n], r.t[:, :n], AF.Exp, scale=_col(pp.t, oDV + d)))
            k.op('act', [r, pp], [sq], lambda e: e.activation(sq.t[:, :n], r.t[:, :n], AF.Exp, scale=_col(pp.t, oDV + 2 + d)))
            k.op('act', [sq, onec], [sq], lambda e: e.activation(sq.t[:, :n], sq.t[:, :n], AF.Sqrt, bias=onec.t[:, 0:1], scale=-1.0))
            k.op('dve', [sq, ig], [bb], lambda e: e.tensor_tensor(bb.t[:, :n], sq.t[:, :n], ig.t[:, :n], ALU.mult))
            k.op('dve', [bb, u2], [bb], lambda e: e.tensor_tensor(bb.t[:, :n], bb.t[:, :n], u2.t[:, :n], ALU.mult))
            return a, bb

        for ti, (t0, n) in enumerate(tiles):
            a, bb = gates(t0, n, 0)
            if ti == 0:
                if s == 'c':
                    k.op('dve', [a, bb], [hf_all], lambda e: e.tensor_tensor_scan(hf_all.t[:, t0:t0 + n], a.t[:, :n], bb.t[:, :n], 0.0, ALU.mult, ALU.add))
                else:
                    k.op('dve', [a, bb, hstate], [hf_all], lambda e: e.tensor_tensor_scan(hf_all.t[:, t0:t0 + n], a.t[:, :n], bb.t[:, :n], hstate.t[:, 0:1], ALU.mult, ALU.add))
            else:
                k.op('dve', [a, bb, hf_all], [hf_all], lambda e: e.tensor_tensor_scan(hf_all.t[:, t0:t0 + n], a.t[:, :n], bb.t[:, :n], hf_all.t[:, t0 - 1:t0], ALU.mult, ALU.add))
        if s == 'c':
            k.op('dve', [hf_all], [hstate], lambda e: e.tensor_copy(hstate.t[:, 0:1], hf_all.t[:, Ls - 1:Ls]))
        prev = None
        for ti in range(nt - 1, -1, -1):
            t0, n = tiles[ti]
            a, bb = gates(t0, n, 1)
            hb_ = F()
            if prev is None:
                init = 0.0 if s == 'c' else hstate.t[:, 1:2]
                rd = [a, bb] if s == 'c' else [a, bb, hstate]
            else:
                init = prev.t[:, 0:1]
                rd = [a, bb, prev]
            k.op('dve', rd, [hb_], lambda e: e.tensor_tensor_scan(hb_.t[:, n - 1::-1] if False else hb_.t[:, 0:n][:, ::-1], a.t[:, 0:n][:, ::-1], bb.t[:, 0:n][:, ::-1], init, ALU.mult, ALU.add))
            prev = hb_
            gt = B(); gg = F(); hs = F(); ob = outb[nxt('o')]
            k.dma('sp', gt, gt.t[:, :n], None, I['ug' + s][:, t0:t0 + n])
            k.op('act', [gt], [gg], lambda e: e.activation(gg.t[:, :n], gt.t[:, :n], AF.Gelu))
            k.op('pool', [hb_, hf_all], [hs], lambda e: e.tensor_tensor(hs.t[:, :n], hb_.t[:, :n], hf_all.t[:, t0:t0 + n], ALU.add))
            k.op('dve', [hs, gg], [ob], lambda e: e.tensor_tensor(ob.t[:, :n], hs.t[:, :n], gg.t[:, :n], ALU.mult))
            store(yo[0, :, t0:t0 + n], ob, ob.t[:, :n])
        if s == 'c':
            k.op('dve', [prev], [hstate], lambda e: e.tensor_copy(hstate.t[:, 1:2], prev.t[:, 0:1]))

    def pool(s, Ls):
        yo = I['yo' + s]
        for t0 in range(0, Ls, TT):
            n = min(TT, Ls - t0)
            ut = B(); ic = F()
            k.dma('sp', ut, ut.t[:, :n + 16], None, I['up' + s][:, t0:t0 + n + 16])
            k.dma('sp', ic, ic.t[:, :n], None, I['icnt' + s][:, t0:t0 + n])
            w1 = F(); w2 = F(); w4 = F(); w8 = F(); ws = F(); ws2 = F()
            m = n + 16
            k.op('pool', [ut], [w1], lambda e: e.tensor_tensor(w1.t[:, 1:m], ut.t[:, 0:m - 1], ut.t[:, 1:m], ALU.add))
            k.op('pool', [w1], [w2], lambda e: e.tensor_tensor(w2.t[:, 2:m - 1], w1.t[:, 1:m - 2], w1.t[:, 3:m], ALU.add))
            k.op('pool', [w2], [w4], lambda e: e.tensor_tensor(w4.t[:, 4:m - 3], w2.t[:, 2:m - 5], w2.t[:, 6:m - 1], ALU.add))
            k.op('pool', [w4], [w8], lambda e: e.tensor_tensor(w8.t[:, 8:m - 7], w4.t[:, 4:m - 11], w4.t[:, 12:m - 3], ALU.add))
            k.op('dve', [w1, pp], [ws], lambda e: e.tensor_scalar(ws.t[:, :n], w1.t[:, 8:8 + n], _col(pp.t, oPP + 1), None, ALU.mult))
            k.op('dve', [w2, pp, ws], [ws2], lambda e: e.scalar_tensor_tensor(ws2.t[:, :n], w2.t[:, 8:8 + n], _col(pp.t, oPP + 2), ws.t[:, :n], ALU.mult, ALU.add))
            k.op('dve', [w4, pp, ws2], [ws], lambda e: e.scalar_tensor_tensor(ws.t[:, :n], w4.t[:, 8:8 + n], _col(pp.t, oPP + 3), ws2.t[:, :n], ALU.mult, ALU.add))
            k.op('dve', [w8, pp, ws], [ws2], lambda e: e.scalar_tensor_tensor(ws2.t[:, :n], w8.t[:, 8:8 + n], _col(pp.t, oPP + 4), ws.t[:, :n], ALU.mult, ALU.add))
            k.op('dve', [ws2, ic], [ws], lambda e: e.tensor_tensor(ws.t[:, :n], ws2.t[:, :n], ic.t[:, :n], ALU.mult))
            db = B()
            k.op('dve', [ws, ut], [db], lambda e: e.tensor_tensor(db.t[:, :n], ws.t[:, :n], ut.t[:, 8:8 + n], ALU.subtract))
            p = P(); ob = outb[nxt('o')]
            k.mm(p, p.t[:, :n], [(wq.t[:, oPW:oPW + 128], db.t[:, :n])], [wq, db])
            k.op('act', [p, pp], [ob], lambda e: e.activation(ob.t[:, :n], p.t[:, :n], AF.Identity, scale=_col(pp.t, oPP + 0)))
            store(yo[1, :, t0:t0 + n], ob, ob.t[:, :n])

    for s, Ls in SEQ:
        lru(s, Ls)
    k.pop_scope()
    for s, Ls in SEQ:
        pool(s, Ls)

    def fft(s, Ls):
        yo = I['yo' + s]
        L1 = Ls // 128
        scale = 1.0 / math.sqrt(Ls * 128.0)
        k.push_scope()
        ufs = k.sb('ufs' + s, [128, Ls], BF16)
        k.dma('sp', ufs, ufs.t[:, :], None, I['uf' + s])
        f1 = k.sb('f1' + s, [128, 4 * L1], BF16)
        k.dma('pool', f1, f1.t[0:L1, :], None, I['F1' + s])
        tw = k.sb('tw' + s, [128, 2 * L1], F32)
        k.dma('sp', tw, tw.t[:, :], None, I['Tw' + s])
        Ab = k.sb('Ab' + s, [128, 2 * 64 * 128], BF16)
        Bp = k.sb('Bp' + s, [128, 2 * 64 * L1], BF16)
        Yh = k.sb('Yh' + s, [128, 128 * L1], BF16)
        Av = Ab.t[:, :].rearrange("p (r c l) -> p r c l", r=2, c=64)
        Bv = Bp.t[:, :].rearrange("p (r c l) -> p r c l", r=2, c=64)
        Yv = Yh.t[:, :].rearrange("p (a b) -> p a b", b=L1)
        for h in range(2):
            for l2 in range(128):
                p = P()
                k.mm(p, p.t[0:L1, 0:64], [(ufs.t[:, l2 * L1:(l2 + 1) * L1], wq.t[:, oF + h * 64:oF + h * 64 + 64])], [ufs, wq])
                k.mm(p, p.t[0:L1, 64:128], [(ufs.t[:, l2 * L1:(l2 + 1) * L1], wq.t[:, oF + 128 + h * 64:oF + 128 + h * 64 + 64])], [ufs, wq])
                k.op('act' if l2 % 2 else 'dve', [p], [Ab], (lambda e, p=p, l2=l2: e.activation(Av[0:L1, :, :, l2], p.t[0:L1, 0:128].rearrange("p (r c) -> p r c", r=2), AF.Identity)) if l2 % 2 else
                     (lambda e, p=p, l2=l2: e.tensor_copy(Av[0:L1, :, :, l2], p.t[0:L1, 0:128].rearrange("p (r c) -> p r c", r=2))))
            for c in range(64):
                p = P()
                k.mm(p, p.t[:, 0:2 * L1], [(Av[0:L1, 0, c, :], f1.t[0:L1, 0:2 * L1]), (Av[0:L1, 1, c, :], f1.t[0:L1, 2 * L1:4 * L1])], [Ab, f1])
                br, bi = p.t[:, 0:L1], p.t[:, L1:2 * L1]
                tc, ts = tw.t[:, 0:L1], tw.t[:, L1:2 * L1]
                t1 = F(); t2 = F(); t3 = F(); t4 = F()
                k.op('dve', [p, tw], [t1], lambda e, t1=t1, br=br, tc=tc: e.tensor_tensor(t1.t[:, :L1], br, tc, ALU.mult))
                k.op('dve', [p, tw], [t2], lambda e, t2=t2, bi=bi, ts=ts: e.tensor_tensor(t2.t[:, :L1], bi, ts, ALU.mult))
                k.op('dve', [p, tw], [t3], lambda e, t3=t3, bi=bi, tc=tc: e.tensor_tensor(t3.t[:, :L1], bi, tc, ALU.mult))
                k.op('dve', [p, tw], [t4], lambda e, t4=t4, br=br, ts=ts: e.tensor_tensor(t4.t[:, :L1], br, ts, ALU.mult))
                k.op('pool', [t1, t2], [Bp], lambda e, t1=t1, t2=t2, c=c: e.tensor_tensor(Bv[:, 0, c, :], t1.t[:, :L1], t2.t[:, :L1], ALU.add))
                k.op('pool', [t3, t4], [Bp], lambda e, t3=t3, t4=t4, c=c: e.tensor_tensor(Bv[:, 1, c, :], t3.t[:, :L1], t4.t[:, :L1], ALU.subtract))
            for l1 in range(L1):
                p = P()
                k.mm(p, p.t[0:64, 0:128], [(Bv[:, 0, :, l1], wq.t[:, oF:oF + 128]), (Bv[:, 1, :, l1], wq.t[:, oF + 256:oF + 384])], [Bp, wq])
                k.op('act', [p], [Yh], lambda e, p=p, l1=l1: e.activation(Yv[h * 64:h * 64 + 64, :, l1], p.t[0:64, 0:128], AF.Identity, scale=scale))
        store(yo[2, :, :].rearrange("p (a b) -> p a b", b=L1), Yh, Yv)
        k.pop_scope()

    for s, Ls in SEQ:
        fft(s, Ls)

    k.push_scope()
    LK = CT + L
    KT = k.sb('KT', [128, LK], BF16)
    VV = k.sb('VV', [128, LK], BF16)
    QT = k.sb('QT', [128, L], BF16)
    QC = k.sb('QC', [128, CT], BF16)
    k.dma('sp', KT, KT.t[:, 0:CT], None, I['kc'])
    k.dma('sp', QC, QC.t[:, :], None, I['qc'])
    Vv = VV.t[:, :].rearrange("p (c e) -> p c e", e=128)
    k.dma('sp', VV, Vv[:, 0:CT // 128, :], None, I['vc'].rearrange("(c p) e -> p c e", p=128))
    k.dma('sp', VV, Vv[:, CT // 128:, :], None, I['vl'].rearrange("(c p) e -> p c e", p=128))
    for nm, dst, off in (('ql', QT, 0), ('kl', KT, CT)):
        for t0 in range(0, L, TT):
            n = min(TT, L - t0)
            xt = B(); ct = F(); st = F()
            k.dma('sp', xt, xt.t[:, :n], None, I[nm][:, t0:t0 + n])
            k.dma('sp', ct, ct.t[:, :n], None, cs_in[0, :, t0:t0 + n])
            k.dma('sp', st, st.t[:, :n], None, cs_in[1, :, t0:t0 + n])
            p = P()
            k.mm(p, p.t[:, :n], [(wq.t[:, oRM:oRM + 128], xt.t[:, :n])], [wq, xt])
            t1 = F(); t2 = F()
            k.op('dve', [xt, ct], [t1], lambda e, t1=t1, xt=xt, ct=ct: e.tensor_tensor(t1.t[:, :n], xt.t[:, :n], ct.t[:, :n], ALU.mult))
            k.op('dve', [p, st], [t2], lambda e, t2=t2, p=p, st=st: e.tensor_tensor(t2.t[:, :n], p.t[:, :n], st.t[:, :n], ALU.mult))
            k.op('pool', [t1, t2], [dst], lambda e, t1=t1, t2=t2, dst=dst, a=off + t0: e.tensor_tensor(dst.t[:, a:a + n], t1.t[:, :n], t2.t[:, :n], ALU.add))
    pT = [k.sb('pT%d' % i, [128, 2 * TT], BF16) for i in range(4)]
    pS = [k.ps("pS%d" % i, [128, 2 * TT]) for i in range(2)]
    zacc = [k.sb('zacc%d' % i, [128, TT], F32) for i in range(2)]
    onef = k.sb('onef', [128, 128], F32)
    k.op('dve', [], [onef], lambda e: e.memset(onef.t[:], 1.0))

    def attend(s, Ls, Qtile, nk):
        yo = I['yo' + s]
        nch = nk // 128
        for q0 in range(0, Ls, TT):
            n = min(TT, Ls - q0)
            po0, po1, pz0, pz1 = ps[4], ps[5], ps[0], ps[1]
            pts = {}

            def emit_S(c):
                pt = pT[nxt('pt', 4)]
                p = pS[nxt('pS', 2)]
                for j in range(2):
                    k.mm(p, p.t[:, j * TT:j * TT + n], [(KT.t[64 * j:64 * j + 64, c * 128:(c + 1) * 128], Qtile.t[64 * j:64 * j + 64, q0:q0 + n])], [KT, Qtile])
                if n == TT:
                    k.op('act', [p], [pt], lambda e, p=p, pt=pt: e.activation(pt.t[:, :], p.t[:, :], AF.Exp, scale=0.125))
                else:
                    for j in range(2):
                        k.op('act', [p], [pt], lambda e, p=p, pt=pt, j=j: e.activation(pt.t[:, j * TT:j * TT + n], p.t[:, j * TT:j * TT + n], AF.Exp, scale=0.125))
                pts[c] = pt

            def emit_PV(c):
                pt = pts.pop(c)
                for j, (po_, pz_) in enumerate(((po0, pz0), (po1, pz1))):
                    for tgt, lhs in ((po_, Vv[:, c, :]), (pz_, ones.t[:, :])):
                        k._wait('pe', k._deps([pt, VV, ones], [tgt]))
                        ins = nc.tensor.matmul(tgt.t[:, :n], lhs, pt.t[:, j * TT:j * TT + n], start=(c == 0), stop=(c == nch - 1))
                        k.cnt['pe'] += 1
                        ins.then_inc(k.sem['pe'], 1)
                        k._mark((k.sem['pe'], k.cnt['pe']), [pt, VV, ones], [tgt])

            emit_S(0)
            for c in range(nch):
                if c + 1 < nch:
                    emit_S(c + 1)
                emit_PV(c)
            r0 = F(); r1 = F(); o0 = F(); o1 = F(); oo = F()
            k.op('dve', [pz0], [r0], lambda e: e.reciprocal(r0.t[:, :n], pz0.t[:, :n]))
            k.op('dve', [pz1], [r1], lambda e: e.reciprocal(r1.t[:, :n], pz1.t[:, :n]))
            k.op('dve', [po0, r0], [o0], lambda e: e.tensor_tensor(o0.t[:, :n], po0.t[:, :n], r0.t[:, :n], ALU.mult))
            k.op('dve', [po1, r1], [o1], lambda e: e.tensor_tensor(o1.t[:, :n], po1.t[:, :n], r1.t[:, :n], ALU.mult))
            k.op('dve', [o0, o1, pp], [oo], lambda e: e.scalar_tensor_tensor(oo.t[:, :n], o1.t[:, :n], _col(pp.t, oDV + 4), o0.t[:, :n], ALU.mult, ALU.add))
            sq = B(); p = P(); rs = F(); t3 = F(); ob = outb[nxt('o')]
            k.op('act', [oo], [sq], lambda e: e.activation(sq.t[:, :n], oo.t[:, :n], AF.Square))
            k.mm(p, p.t[:, :n], [(ones.t[:, :], sq.t[:, :n])], [ones, sq])
            k.op('act', [p, onec], [rs], lambda e: e.activation(rs.t[:, :n], p.t[:, :n], AF.Sqrt, bias=onec.t[:, 1:2], scale=1.0 / 128))
            k.op('dve', [rs], [t3], lambda e: e.reciprocal(t3.t[:, :n], rs.t[:, :n]))
            k.op('dve', [oo, t3, pp], [ob], lambda e: e.scalar_tensor_tensor(ob.t[:, :n], oo.t[:, :n], _col(pp.t, oDV + 5), t3.t[:, :n], ALU.mult, ALU.mult))
            store(yo[3, :, q0:q0 + n], ob, ob.t[:, :n])

    attend('c', CT, QC, CT)
    attend('l', L, QT, LK)
    k.pop_scope()
    k.finish()
    return nc


POOL_HALF = (1, 2, 4, 8)


def _dft(n):
    i = np.arange(n)
    ang = 2.0 * np.pi * np.outer(i, i) / n
    return np.cos(ang), np.sin(ang)


def m_consts(L, CT, j, lam_init):
    c = {}
    C, S = _dft(128)
    c['f128'] = np.concatenate([C, -S, S], 1).astype(np.float32)
    rm = np.zeros((128, 128), np.float32)
    for d in range(128):
        if d % 32 < 16:
            rm[d + 16, d] = -1.0
        else:
            rm[d - 16, d] = 1.0
    c['rm'] = rm
    t = np.arange(L)
    inv = (10000.0 ** (-np.arange(16, dtype=np.float32) / 16)).astype(np.float32)
    ang_r = (t // 64).astype(np.float32)[:, None] * inv
    ang_c = (t % 64).astype(np.float32)[:, None] * inv
    ang = np.zeros((128, L), np.float32)
    for p in range(128):
        d = p % 64
        ang[p] = ang_r[:, d % 16] if d < 32 else ang_c[:, (d - 32) % 16]
    c['cossin'] = np.stack([np.cos(ang), np.sin(ang)]).astype(np.float32)
    for s, Ls in (('c', CT), ('l', L)):
        L1 = Ls // 128
        C1, S1 = _dft(L1)
        c['F1' + s] = np.concatenate([C1, -S1, S1, C1], 1).astype(np.float32)
        a = 2.0 * np.pi * np.outer(np.arange(128), np.arange(L1)) / Ls
        c['Tw' + s] = np.concatenate([np.cos(a), np.sin(a)], 1).astype(np.float32)
        tt = np.arange(Ls)
        half = POOL_HALF[j]
        lo = np.clip(tt - half, 0, Ls - 1)
        hi = np.clip(tt + half - 1, 0, Ls - 1)
        c['icnt' + s] = np.ascontiguousarray(np.broadcast_to((1.0 / (hi - lo + 1)).astype(np.float32)[None], (128, Ls)))
    sel = np.zeros((128, 4), np.float32)
    sel[:, j] = 1.0
    c['sel'] = sel
    c['lam_init'] = np.full((128, 1), lam_init, np.float32)
    c['one_m_lam_init'] = np.full((128, 1), 1.0 - lam_init, np.float32)
    return c


def m_inputs(pT, pcT, j, L, CT, P, consts):
    inp = {}
    for s, Ls, src in (('c', CT, pcT), ('l', L, pT)):
        sl = lambda base: src[base + 128 * j: base + 128 * (j + 1)]
        z = lambda n: np.zeros((128, n), src.dtype)
        inp['ux' + s] = np.concatenate([z(2), sl(0), z(1)], 1)
        inp['ug' + s] = np.ascontiguousarray(sl(512))
        inp['up' + s] = np.concatenate([z(8), sl(1024), z(8)], 1)
        L1 = Ls // 128
        inp['uf' + s] = np.ascontiguousarray(sl(1536).reshape(128, L1, 128).transpose(0, 2, 1).reshape(128, Ls))
        inp['q' + s] = np.ascontiguousarray(sl(2048))
        inp['k' + s] = np.ascontiguousarray(sl(2560))
        inp['v' + s] = np.ascontiguousarray(sl(3072).T)
        for nm in ('icnt', 'F1', 'Tw'):
            inp[nm + s] = consts[nm + s]
    lw = np.zeros((128, 512), np.float32)
    for d in range(2):
        for gi, W in enumerate((P['lru_wa'], P['lru_wi'])):
            for a in range(2):
                lw[64 * a:64 * a + 64, (2 * d + gi) * 128 + 64 * a:(2 * d + gi) * 128 + 64 * a + 64] = W[d, 2 * j + a]
    inp['lruw'] = lw
    cs = slice(128 * j, 128 * (j + 1))
    inp['lrup'] = np.ascontiguousarray(np.stack(
        [P['lru_conv_w'][0, cs], P['lru_conv_w'][1, cs], P['lru_conv_w'][2, cs], P['lru_conv_w'][3, cs], P['lru_conv_b'][cs],
         P['lru_ba'][0, cs], P['lru_ba'][1, cs], P['lru_bi'][0, cs], P['lru_bi'][1, cs], P['lru_lam'][0, cs], P['lru_lam'][1, cs]], 1).astype(np.float32))
    inp['poolw'] = np.ascontiguousarray(P['pool_w'][j])
    inp['poolp'] = np.ascontiguousarray(np.concatenate([P['pool_scale'][cs][:, None], consts['sel']], 1).astype(np.float32))
    inp['f128'] = consts['f128']
    inp['rm'] = consts['rm']
    inp['cossin'] = consts['cossin']
    inp['dlam'] = np.ascontiguousarray(np.broadcast_to(P['diff_lam'].reshape(1, 256), (128, 256)))
    inp['attp'] = np.ascontiguousarray(np.concatenate([P['diff_subln_g'][:, None], consts['lam_init'], consts['one_m_lam_init']], 1).astype(np.float32))
    return inp


ACOLS = 6 * D // NCORES


def build_ADA(depth=4):
    nc = bass.Bass("TRN2", target_bir_lowering=False)
    k = K(nc)
    cT = nc.dram_tensor("cT", [128, DC * 3], F32, kind="ExternalInput").ap()
    aw = nc.dram_tensor("aw", [depth * DC, 128, ACOLS], F32, kind="ExternalInput").ap()
    ab_in = nc.dram_tensor("ab", [3, depth * ACOLS], F32, kind="ExternalInput").ap()
    mo = nc.dram_tensor("mo", [3, depth * ACOLS], F32, kind="ExternalOutput").ap()
    ct = k.sb("ct", [128, DC * 3], F32)
    sc = k.sb("sc", [128, DC * 3], F32)
    bias = k.sb("bias", [3, depth * ACOLS], F32)
    res = k.sb("res", [3, depth * ACOLS], F32)
    k.dma('sp', ct, ct.t[:, :], None, cT)
    k.dma('sp', bias, bias.t[:, :], None, ab_in)
    k.op('act', [ct], [sc], lambda e: e.activation(sc.t[:, :], ct.t[:, :], AF.Silu))
    wb = [k.sb("awb%d" % i, [128, ACOLS], F32) for i in range(4)]
    pss = [k.ps("aps%d" % i, [128, 512]) for i in range(3)]
    for l in range(depth):
        for kc in range(DC):
            w = wb[(l * DC + kc) % 4]
            k.dma('sp', w, w.t[:, :], None, aw[l * DC + kc])
            for g in range(3):
                k._wait('pe', k._deps([w, sc], [pss[g]]))
                ins = nc.tensor.matmul(pss[g].t[0:3, :], sc.t[:, kc * 3:kc * 3 + 3], w.t[:, g * 512:(g + 1) * 512], start=(kc == 0), stop=(kc == DC - 1))
                k.cnt['pe'] += 1
                ins.then_inc(k.sem['pe'], 1)
                k._mark((k.sem['pe'], k.cnt['pe']), [w, sc], [pss[g]])
        for g in range(3):
            o = l * ACOLS + g * 512
            k.op('dve', [pss[g], bias], [res], lambda e, g=g, o=o: e.tensor_tensor(res.t[0:3, o:o + 512], pss[g].t[0:3, :], bias.t[0:3, o:o + 512], ALU.add))
    k.dma('pool', None, mo, res, res.t[:, :], final=True)
    k.finish()
    return nc


_CACHE = {}


def _prog(key, fn):
    if key not in _CACHE:
        _CACHE[key] = fn()
    return _CACHE[key]


def kernel(x, c, ctx, c_ctx, ada_w, ada_b, norm_g, w_in, lru_conv_w, lru_conv_b, lru_wa, lru_ba, lru_wi, lru_bi,
           lru_lam, pool_w, pool_scale, fourier_w, diff_lam, diff_subln_g, w_out, ffn_w_up, ffn_conv_w, ffn_conv_b,
           ffn_w_down):
    A = lambda a: np.asarray(a)
    x, c, ctx, c_ctx = A(x), A(c), A(ctx), A(c_ctx)
    ada_w, ada_b, norm_g, w_in = A(ada_w), A(ada_b), A(norm_g), A(w_in)
    Bn, L, _ = x.shape
    CT = ctx.shape[1]
    depth = w_in.shape[0]
    NS = NCORES // Bn
    NT = L // NS
    TT = 512
    ntile = NT // TT
    cores = list(range(NCORES))
    PL = [dict(lru_conv_w=A(lru_conv_w)[l], lru_conv_b=A(lru_conv_b)[l], lru_wa=A(lru_wa)[l], lru_ba=A(lru_ba)[l],
               lru_wi=A(lru_wi)[l], lru_bi=A(lru_bi)[l], lru_lam=A(lru_lam)[l], pool_w=A(pool_w)[l],
               pool_scale=A(pool_scale)[l], diff_lam=A(diff_lam)[l], diff_subln_g=A(diff_subln_g)[l]) for l in range(depth)]

    c3 = np.concatenate([c, c_ctx[None]], 0).astype(np.float32)
    cT = np.ascontiguousarray(c3.reshape(3, DC, 128).transpose(2, 1, 0).reshape(128, DC * 3))
    maps = []
    for r in cores:
        cs = slice(r * ACOLS, (r + 1) * ACOLS)
        aw = np.ascontiguousarray(ada_w[:, :, cs].reshape(depth * DC, 128, ACOLS))
        ab = np.ascontiguousarray(np.broadcast_to(ada_b[:, cs].reshape(1, depth * ACOLS), (3, depth * ACOLS)))
        maps.append(dict(cT=cT, aw=aw, ab=ab))
    res = run_bass_kernel_spmd(_prog('ada', lambda: build_ADA(depth)), maps, core_ids=cores)
    mod = np.concatenate([res.results[r]['mo'].reshape(3, depth, ACOLS) for r in cores], axis=2)
    mod = mod.reshape(3, depth, 6, D)
    del maps, res

    xT = [np.ascontiguousarray(x[b].T) for b in range(Bn)]
    xcT = [np.ascontiguousarray(ctx[b].T) for b in range(Bn)]

    def t_launch(lC, lA, yT, ycT):
        do_C, do_A = lC is not None, lA is not None
        NH = 2 * ntile if do_C else 0
        common = {}
        if do_C:
            common.update(ngC=cols(norm_g[lC, 1], norm_g[lC, 2], norm_g[lC, 3]),
                          cw=np.ascontiguousarray(A(ffn_conv_w)[lC].reshape(3, FC, 128).transpose(2, 0, 1).reshape(128, 3 * FC)),
                          cb=colvec(A(ffn_conv_b)[lC]), wf=blk(A(fourier_w)[lC], 512), wo=blk(A(w_out)[lC], 256),
                          wu=blk_up(A(ffn_w_up)[lC]), wd=blk(A(ffn_w_down)[lC], 128))
        if do_A:
            common.update(ngA=cols(norm_g[lA, 0]), wi=blk(w_in[lA], 256))
        maps = []
        for r in cores:
            b, s = divmod(r, NS)
            t0 = s * NT
            m = dict(common)
            hm = np.zeros(max(NH, 1), np.float32)
            hidx = []
            for i in range(ntile if do_C else 0):
                for hi, tpos in enumerate((t0 + i * TT - 1, t0 + (i + 1) * TT)):
                    ok = 0 <= tpos < L
                    hm[2 * i + hi] = 1.0 if ok else 0.0
                    hidx.append(tpos if ok else 0)
            xh = xT[b][:, hidx] if do_C else np.zeros((D, 0), np.float32)
            m['xT'] = np.ascontiguousarray(np.concatenate([xh, xT[b][:, t0:t0 + NT], xcT[b]], 1))
            if do_C:
                m['yT'] = np.ascontiguousarray(np.concatenate([yT[b][:, hidx], yT[b][:, t0:t0 + NT], ycT[b]], 1))
                m['hmask'] = np.ascontiguousarray(np.broadcast_to(hm[None, :NH], (128, NH)))
                m['modC'] = cols(*[mod[row, lC, i] for row in (b, 2) for i in (2, 3, 4, 5)])
            if do_A:
                m['modA'] = cols(*[mod[row, lA, i] for row in (b, 2) for i in (0, 1)])
            maps.append(m)
        res = run_bass_kernel_spmd(_prog(('T', NT, do_C, do_A, CT), lambda: build_T(NT, do_C, do_A, CT, TT)), maps, core_ids=cores)
        pT = pcT = None
        if do_C:
            for b in range(Bn):
                xT[b] = np.ascontiguousarray(np.concatenate([res.results[b * NS + s]['xo'][:, :NT] for s in range(NS)], 1))
                xcT[b] = np.ascontiguousarray(res.results[b * NS]['xo'][:, NT:])
        if do_A:
            pT = [np.concatenate([res.results[b * NS + s]['po'][:, :NT] for s in range(NS)], 1) for b in range(Bn)]
            pcT = [res.results[b * NS]['po'][:, NT:] for b in range(Bn)]
        return pT, pcT

    pT, pcT = t_launch(None, 0, None, None)
    for l in range(depth):
        lam_init = 0.8 - 0.6 * math.exp(-0.3 * l)
        maps = []
        for r in cores:
            b, j = divmod(r, NS)
            maps.append(m_inputs(pT[b], pcT[b], j, L, CT, PL[l], m_consts(L, CT, j, lam_init)))
        res = run_bass_kernel_spmd(_prog(('M', L, CT), lambda: build_M(L, CT, TT)), maps, core_ids=cores)
        yT, ycT = [], []
        for b in range(Bn):
            yl = np.stack([res.results[b * NS + j]['yol'] for j in range(NS)], 1)
            yc = np.stack([res.results[b * NS + j]['yoc'] for j in range(NS)], 1)
            yT.append(yl.reshape(D, L))
            ycT.append(yc.reshape(D, CT))
        del maps, res
        pT, pcT = t_launch(l, l + 1 if l + 1 < depth else None, yT, ycT)
    out = np.stack([xT[b].T for b in range(Bn)], 0).astype(np.float32)
    return np.ascontiguousarray(out)
```

```python
import contextlib
import math
import numpy as np
import ml_dtypes
import concourse.bass as bass
import concourse.mybir as mybir
from concourse.bass_utils import run_bass_kernel_spmd

F32 = mybir.dt.float32
BF16 = mybir.dt.bfloat16
AF = mybir.ActivationFunctionType
ALU = mybir.AluOpType
NPBF = ml_dtypes.bfloat16

D = 2048
DC = D // 128
DFF = 5632
FC = DFF // 128
DIN = 3584
EPS = 1e-6
NCORES = 8


class Tk:
    def __init__(self, t, name):
        self.t = t
        self.name = name
        self.w = None
        self.r = {}
        self.dsem = {}
        self.dcnt = {}
        self.psum = False

    def __getitem__(self, idx):
        return self.t[idx]


class K:
    def __init__(self, nc):
        self.nc = nc
        self.es = contextlib.ExitStack()
        self.root = self.es
        self.E = {'pe': nc.tensor, 'act': nc.scalar, 'dve': nc.vector, 'pool': nc.gpsimd, 'sp': nc.sync}
        self.sem = {e: self.es.enter_context(nc.semaphore('s_' + e)) for e in ('pe', 'act', 'dve', 'pool')}
        self.cnt = dict.fromkeys(self.sem, 0)
        self.known = {e: {} for e in self.E}
        self.final = []
        self.nd = 0

    def sb(self, name, shape, dt):
        t = Tk(self.es.enter_context(self.nc.sbuf_tensor(name, shape, dt)), name)
        if getattr(self, 'scope_tiles', None) is not None:
            self.scope_tiles.append(t)
        return t

    def push_scope(self):
        self.saved_es = self.es
        self.es = contextlib.ExitStack()
        self.scope_tiles = []

    def pop_scope(self):
        deps = [(self.sem[e], self.cnt[e]) for e in self.sem if self.cnt[e]]
        for t in self.scope_tiles:
            if t.w:
                deps.append(t.w)
            deps.extend(t.r.values())
        for e in self.E:
            self._wait(e, deps)
        self.es.close()
        self.es = self.saved_es
        self.scope_tiles = None

    def ps(self, name, shape, dt=F32):
        t = Tk(self.es.enter_context(self.nc.psum_tensor(name, shape, dt)), name)
        t.psum = True
        return t

    def dram(self, name, shape, dt, kind):
        return Tk(self.nc.dram_tensor(name, shape, dt, kind=kind).ap(), name)

    def _wait(self, e, deps):
        kn = self.known[e]
        for sem, val in deps:
            if e == 'pe' and sem is self.sem['pe']:
                continue
            if kn.get(id(sem), 0) < val:
                self.E[e].wait_ge(sem, val)
                kn[id(sem)] = val

    @staticmethod
    def _deps(reads, writes):
        d = []
        for t in reads:
            if t.w:
                d.append(t.w)
        for t in writes:
            if t.w:
                d.append(t.w)
            d.extend(t.r.values())
        return d

    @staticmethod
    def _mark(tok, reads, writes):
        for t in reads:
            t.r[id(tok[0])] = tok
        for t in writes:
            t.w = tok
            t.r = {}

    def op(self, e, reads, writes, fn):
        writes = list(writes) + [t for t in reads if t.psum]
        self._wait(e, self._deps(reads, writes))
        ins = fn(self.E[e])
        self.cnt[e] += 1
        ins.then_inc(self.sem[e], 1)
        self._mark((self.sem[e], self.cnt[e]), reads, writes)

    def mm(self, out, out_ap, pairs, reads, **kw):
        self._wait('pe', self._deps(reads, [out]))
        n = len(pairs)
        for i, (l, r) in enumerate(pairs):
            ins = self.nc.tensor.matmul(out_ap, l, r, start=(i == 0), stop=(i == n - 1), **kw)
        self.cnt['pe'] += 1
        ins.then_inc(self.sem['pe'], 1)
        self._mark((self.sem['pe'], self.cnt['pe']), reads, [out])

    def dma(self, q, out_tk, out_ap, in_tk, in_ap, final=False, **kw):
        reads = [in_tk] if in_tk is not None else []
        writes = [out_tk] if out_tk is not None else []
        self._wait(q, self._deps(reads, writes))
        own = out_tk if out_tk is not None else in_tk
        if q not in own.dsem:
            self.nd += 1
            own.dsem[q] = self.root.enter_context(self.nc.semaphore('d%d' % self.nd))
            own.dcnt[q] = 0
        ins = self.E[q].dma_start(out=out_ap, in_=in_ap, **kw)
        own.dcnt[q] += 16
        ins.then_inc(own.dsem[q], 16)
        tok = (own.dsem[q], own.dcnt[q])
        self._mark(tok, reads, writes)
        if final:
            self.final.append(tok)

    def finish(self):
        self._wait('sp', self.final)
        self._wait('sp', [(self.sem[e], self.cnt[e]) for e in self.sem if self.cnt[e]])
        self.es.close()


def _col(t, i):
    return t[:, i:i + 1]


def build_T(NT, do_C, do_A, CT=256, TT=512, dbg=0):
    nc = bass.Bass("TRN2", target_bir_lowering=False)
    k = K(nc)
    ntile = NT // TT
    NH = 2 * ntile if do_C else 0
    NTOT = NH + NT + CT
    NOUT = NT + CT

    def din(name, shape, dt=F32):
        return nc.dram_tensor(name, shape, dt, kind="ExternalInput").ap()

    xT = din("xT", [D, NTOT]).rearrange("(c p) n -> p c n", p=128)
    if do_C:
        yT = din("yT", [D, NTOT], BF16).rearrange("(c p) n -> p c n", p=128)
        hmask = din("hmask", [128, NH])
        modC = din("modC", [128, 2 * 4 * DC])
        ngC = din("ngC", [128, 3 * DC])
        cw = din("cw", [128, 3 * FC])
        cb = din("cb", [128, FC])
        wf_in = din("wf", [1, 128, 4 * 512])
        wo_in = din("wo", [8, 128, DC * 256])
        wu_in = din("wu", [FC, 128, DC * 256])
        wd_in = din("wd", [DC, 128, FC * 128])
        xo = nc.dram_tensor("xo", [D, NOUT], F32, kind="ExternalOutput").ap().rearrange("(c p) n -> p c n", p=128)
    if do_A:
        modA = din("modA", [128, 2 * 2 * DC])
        ngA = din("ngA", [128, DC])
        wi_in = din("wi", [14, 128, DC * 256])
        po = nc.dram_tensor("po", [DIN, NOUT], BF16, kind="ExternalOutput").ap()

    def scratch(name, src):
        G, _, Fw = src.shape
        step = max(1, (1 << 20) // (128 * Fw))
        mld = max(d for d in range(1, 2049) if Fw % d == 0)
        pieces = []
        for g0 in range(0, G, step):
            g1 = min(G, g0 + step)
            tk = k.dram("%s_bf%d" % (name, g0), [g1 - g0, 128, Fw], BF16, "Internal")
            k.dma('pool', tk, tk.t[0:g1 - g0], None, src[g0:g1], max_dma_last_dim=mld)
            for g in range(g0, g1):
                pieces.append((tk, g - g0))
        return pieces

    WB = 3
    wbuf = [k.sb("wbuf%d" % i, [128, FC * 128], BF16) for i in range(WB)]
    wctr = [0]

    def wload(pieces, g, width):
        b = wbuf[wctr[0] % WB]
        wctr[0] += 1
        tk, gi = pieces[g]
        k.dma('sp', b, b.t[:, 0:width], tk, tk.t[gi])
        return b

    if do_C:
        wf_s = scratch("wf", wf_in)
        wo_s = scratch("wo", wo_in)
        wu_s = scratch("wu", wu_in)
        wd_s = scratch("wd", wd_in)
    if do_A:
        wi_s = scratch("wi", wi_in)

    ones = k.sb("ones", [128, 128], BF16)
    k.op('dve', [], [ones], lambda e: e.memset(ones.t[:], 1.0))
    epsc = k.sb("epsc", [128, 1], F32)
    k.op('dve', [], [epsc], lambda e: e.memset(epsc.t[:], EPS))
    prm = k.sb("prm", [128, 16 * DC + 4 * FC + NH], F32)
    cst = k.sb("cst", [128, 12 * DC], F32)
    o_modC, o_ngC, o_modA, o_ngA, o_cw, o_cb, o_hm = 0, 8 * DC, 11 * DC, 15 * DC, 16 * DC, 16 * DC + 3 * FC, 16 * DC + 4 * FC
    cG1, cA2, cB2, cG2, cA1, cB1 = 0, 2 * DC, 4 * DC, 6 * DC, 8 * DC, 10 * DC
    if do_C:
        k.dma('sp', prm, prm.t[:, o_modC:o_modC + 8 * DC], None, modC)
        k.dma('sp', prm, prm.t[:, o_ngC:o_ngC + 3 * DC], None, ngC)
        k.dma('sp', prm, prm.t[:, o_cw:o_cw + 3 * FC], None, cw)
        k.dma('sp', prm, prm.t[:, o_cb:o_cb + FC], None, cb)
        k.dma('sp', prm, prm.t[:, o_hm:o_hm + NH], None, hmask)
        for m in range(2):
            mo = o_modC + m * 4 * DC
            k.op('dve', [prm], [cst], lambda e, m=m, mo=mo: e.tensor_tensor(
                cst.t[:, cG1 + m * DC:cG1 + (m + 1) * DC], prm.t[:, mo:mo + DC], prm.t[:, o_ngC:o_ngC + DC], ALU.mult))
            k.op('dve', [prm], [cst], lambda e, m=m, mo=mo: e.scalar_tensor_tensor(
                cst.t[:, cA2 + m * DC:cA2 + (m + 1) * DC], prm.t[:, mo + 2 * DC:mo + 3 * DC], 1.0,
                prm.t[:, o_ngC + DC:o_ngC + 2 * DC], ALU.add, ALU.mult))
            k.op('dve', [prm], [cst], lambda e, m=m, mo=mo: e.tensor_copy(
                cst.t[:, cB2 + m * DC:cB2 + (m + 1) * DC], prm.t[:, mo + DC:mo + 2 * DC]))
            k.op('dve', [prm], [cst], lambda e, m=m, mo=mo: e.tensor_tensor(
                cst.t[:, cG2 + m * DC:cG2 + (m + 1) * DC], prm.t[:, mo + 3 * DC:mo + 4 * DC],
                prm.t[:, o_ngC + 2 * DC:o_ngC + 3 * DC], ALU.mult))
    if do_A:
        k.dma('sp', prm, prm.t[:, o_modA:o_modA + 4 * DC], None, modA)
        k.dma('sp', prm, prm.t[:, o_ngA:o_ngA + DC], None, ngA)
        for m in range(2):
            mo = o_modA + m * 2 * DC
            k.op('dve', [prm], [cst], lambda e, m=m, mo=mo: e.scalar_tensor_tensor(
                cst.t[:, cA1 + m * DC:cA1 + (m + 1) * DC], prm.t[:, mo + DC:mo + 2 * DC], 1.0,
                prm.t[:, o_ngA:o_ngA + DC], ALU.add, ALU.mult))
            k.op('dve', [prm], [cst], lambda e, m=m, mo=mo: e.tensor_copy(
                cst.t[:, cB1 + m * DC:cB1 + (m + 1) * DC], prm.t[:, mo:mo + DC]))

    xs = k.sb("xs", [128, DC, TT], F32)
    of = k.sb("of", [128, DC, TT], F32)
    hb = k.sb("hb", [128, DC, TT], BF16)
    zb = k.sb("zb", [128, 4, TT], BF16)
    ab = k.sb("ab", [128, FC, TT], BF16)
    ghalo = k.sb("ghalo", [128, FC, max(NH, 2)], F32)
    rstd = k.sb("rstd", [128, TT], F32)
    NR = 3
    sqb = [k.sb("sqb%d" % i, [128, TT], BF16) for i in range(NR)]
    tmp = [k.sb("tmp%d" % i, [128, TT], F32) for i in range(NR)]
    gbuf = [k.sb("gbuf%d" % i, [128, TT + 2], F32) for i in range(NR)]
    cva = [k.sb("cva%d" % i, [128, TT], F32) for i in range(NR)]
    cvb = [k.sb("cvb%d" % i, [128, TT], F32) for i in range(NR)]
    geb = [k.sb("geb%d" % i, [128, TT], F32) for i in range(NR)]
    pbo = [k.sb("pbo%d" % i, [128, TT], BF16) for i in range(NR)]
    pg = [k.ps("pg%d" % i, [128, TT]) for i in range(2)]
    pv = [k.ps("pv%d" % i, [128, TT]) for i in range(2)]
    pm = [k.ps("pm%d" % i, [128, TT]) for i in range(2)]
    pst = k.ps("pst", [128, TT])
    rot = {'sq': 0, 'tmp': 0, 'g': 0, 'pg': 0, 'pm': 0, 'pb': 0}

    def nxt(key, n):
        v = rot[key] % n
        rot[key] += 1
        return v

    def sumsq_chunk(src_tk, src_ap, c, n):
        s = sqb[nxt('sq', NR)]
        k.op('act', [src_tk], [s], lambda e: e.activation(s.t[:, :n], src_ap, AF.Square))
        k._wait('pe', k._deps([s, ones], [pst]))
        ins = nc.tensor.matmul(pst.t[:, :n], ones.t[:, :], s.t[:, :n], start=(c == 0), stop=(c == DC - 1))
        k.cnt['pe'] += 1
        ins.then_inc(k.sem['pe'], 1)
        k._mark((k.sem['pe'], k.cnt['pe']), [s, ones], [pst])

    def make_rstd(n):
        t = tmp[nxt('tmp', NR)]
        k.op('act', [pst, epsc], [t], lambda e: e.activation(t.t[:, :n], pst.t[:, :n], AF.Sqrt, bias=epsc.t[:, 0:1], scale=1.0 / D))
        k.op('dve', [t], [rstd], lambda e: e.reciprocal(rstd.t[:, :n], t.t[:, :n]))

    def resid_update(gcol0, n):
        for c in range(DC):
            t = tmp[nxt('tmp', NR)]
            k.op('dve', [of, cst, rstd], [t], lambda e, c=c, t=t: e.scalar_tensor_tensor(
                t.t[:, :n], of.t[:, c, :n], _col(cst.t, gcol0 + c), rstd.t[:, :n], ALU.mult, ALU.mult))
            k.op('pool', [t, xs], [xs], lambda e, c=c, t=t: e.tensor_tensor(xs.t[:, c, :n], xs.t[:, c, :n], t.t[:, :n], ALU.add))

    def norm_mod(acol0, bcol0, n):
        for c in range(DC):
            sumsq_chunk(xs, xs.t[:, c, :n], c, n)
        make_rstd(n)
        for c in range(DC):
            t = tmp[nxt('tmp', NR)]
            k.op('dve', [xs, cst, rstd], [t], lambda e, c=c, t=t: e.scalar_tensor_tensor(
                t.t[:, :n], xs.t[:, c, :n], _col(cst.t, acol0 + c), rstd.t[:, :n], ALU.mult, ALU.mult))
            k.op('act', [t, cst], [hb], lambda e, c=c, t=t: e.activation(
                hb.t[:, c, :n], t.t[:, :n], AF.Identity, bias=_col(cst.t, bcol0 + c), scale=1.0))

    segs = []
    if do_C and dbg < 2:
        segs.append(('halo', 0, NH, 0, -1))
    for i in range(ntile):
        segs.append(('lat', NH + i * TT, TT, 0, i))
    segs.append(('ctx', NH + NT, CT, 1, -1))

    for kind, c0, n, m, ti in segs:
        if not do_C:
            k.dma('sp', xs, xs.t[:, :, :n], None, xT[:, :, c0:c0 + n])
        if do_C:
            k.dma('sp', hb, hb.t[:, :, :n], None, yT[:, :, c0:c0 + n])
            wt = wload(wf_s, 0, 4 * 512)
            for mj in range(4 if dbg not in (3,) else 0):
                p = pm[nxt('pm', 2)]
                k.mm(p, p.t[:, :n], [(wt.t[:, kc * 512 + mj * 128:kc * 512 + (mj + 1) * 128], hb.t[:, 8 + kc, :n]) for kc in range(4)], [wt, hb])
                k.op('act', [p], [zb], lambda e, p=p, mj=mj: e.activation(zb.t[:, mj, :n], p.t[:, :n], AF.Identity))
            for g in range(8 if dbg not in (3, 4) else 0):
                wt = wload(wo_s, g, DC * 256)
                for j in range(2):
                    mi = 2 * g + j
                    p = pm[nxt('pm', 2)]
                    prs = []
                    for kc in range(DC):
                        rhs = zb.t[:, kc - 8, :n] if 8 <= kc < 12 else hb.t[:, kc, :n]
                        prs.append((wt.t[:, kc * 256 + j * 128:kc * 256 + (j + 1) * 128], rhs))
                    k.mm(p, p.t[:, :n], prs, [wt, hb, zb])
                    k.op('dve', [p], [of], lambda e, p=p, mi=mi: e.tensor_copy(of.t[:, mi, :n], p.t[:, :n]))
                    sumsq_chunk(of, of.t[:, mi, :n], mi, n)
            k.dma('sp', xs, xs.t[:, :, :n], None, xT[:, :, c0:c0 + n])
            if dbg not in (3, 4, 5):
                make_rstd(n)
                resid_update(cG1 + m * DC, n)
            if dbg == 0:
                norm_mod(cA2 + m * DC, cB2 + m * DC, n)
            for mi in range(FC if dbg == 0 else 0):
                wt = wload(wu_s, mi, DC * 256)
                pgi = nxt('pg', 2)
                pgt, pvt = pg[pgi], pv[pgi]
                k.mm(pgt, pgt.t[:, :n], [(wt.t[:, kc * 256:kc * 256 + 128], hb.t[:, kc, :n]) for kc in range(DC)], [wt, hb])
                if kind == 'halo':
                    k.op('dve', [pgt, prm], [ghalo], lambda e, pgt=pgt, mi=mi: e.tensor_tensor(
                        ghalo.t[:, mi, :], pgt.t[:, :n], prm.t[:, o_hm:o_hm + NH], ALU.mult))
                    continue
                k.mm(pvt, pvt.t[:, :n], [(wt.t[:, kc * 256 + 128:kc * 256 + 256], hb.t[:, kc, :n]) for kc in range(DC)], [wt, hb])
                gi = nxt('g', NR)
                gb, ca, cbb, ge = gbuf[gi], cva[gi], cvb[gi], geb[gi]
                k.op('act', [pgt], [gb], lambda e, gb=gb, pgt=pgt: e.activation(gb.t[:, 1:n + 1], pgt.t[:, :n], AF.Identity))
                if kind == 'lat':
                    k.op('pool', [ghalo], [gb], lambda e, gb=gb, mi=mi: e.tensor_copy(gb.t[:, 0:n + 2:n + 1], ghalo.t[:, mi, 2 * ti:2 * ti + 2]))
                else:
                    k.op('pool', [], [gb], lambda e, gb=gb: e.memset(gb.t[:, 0:n + 2:n + 1], 0.0))
                k.op('dve', [gb, prm], [ca], lambda e, gb=gb, ca=ca, mi=mi: e.tensor_scalar(
                    ca.t[:, :n], gb.t[:, 0:n], _col(prm.t, o_cw + mi), _col(prm.t, o_cb + mi), ALU.mult, ALU.add))
                k.op('dve', [gb, prm, ca], [cbb], lambda e, gb=gb, ca=ca, cbb=cbb, mi=mi: e.scalar_tensor_tensor(
                    cbb.t[:, :n], gb.t[:, 1:n + 1], _col(prm.t, o_cw + FC + mi), ca.t[:, :n], ALU.mult, ALU.add))
                k.op('dve', [gb, prm, cbb], [ca], lambda e, gb=gb, ca=ca, cbb=cbb, mi=mi: e.scalar_tensor_tensor(
                    ca.t[:, :n], gb.t[:, 2:n + 2], _col(prm.t, o_cw + 2 * FC + mi), cbb.t[:, :n], ALU.mult, ALU.add))
                k.op('act', [ca], [ge], lambda e, ca=ca, ge=ge: e.activation(ge.t[:, :n], ca.t[:, :n], AF.Gelu))
                k.op('dve', [ge, pvt], [ab], lambda e, ge=ge, pvt=pvt, mi=mi: e.tensor_tensor(ab.t[:, mi, :n], ge.t[:, :n], pvt.t[:, :n], ALU.mult))
            if kind == 'halo':
                continue
            for g in range(DC if dbg == 0 else 0):
                wt = wload(wd_s, g, FC * 128)
                p = pm[nxt('pm', 2)]
                k.mm(p, p.t[:, :n], [(wt.t[:, mi * 128:(mi + 1) * 128], ab.t[:, mi, :n]) for mi in range(FC)], [wt, ab])
                k.op('dve', [p], [of], lambda e, p=p, g=g: e.tensor_copy(of.t[:, g, :n], p.t[:, :n]))
                sumsq_chunk(of, of.t[:, g, :n], g, n)
            if dbg == 0:
                make_rstd(n)
                resid_update(cG2 + m * DC, n)
            k.dma('pool', None, xo[:, :, c0 - NH:c0 - NH + n], xs, xs.t[:, :, :n], final=True)
        if do_A:
            norm_mod(cA1 + m * DC, cB1 + m * DC, n)
            for g in range(14):
                wt = wload(wi_s, g, DC * 256)
                for j in range(2):
                    mi = 2 * g + j
                    p = pm[nxt('pm', 2)]
                    k.mm(p, p.t[:, :n], [(wt.t[:, kc * 256 + j * 128:kc * 256 + (j + 1) * 128], hb.t[:, kc, :n]) for kc in range(DC)], [wt, hb])
                    pb = pbo[nxt('pb', NR)]
                    k.op('act', [p], [pb], lambda e, p=p, pb=pb: e.activation(pb.t[:, :n], p.t[:, :n], AF.Identity))
                    k.dma('pool', None, po[mi * 128:(mi + 1) * 128, c0 - NH:c0 - NH + n], pb, pb.t[:, :n], final=True)
    k.finish()
    return nc


def blk(W, cols):
    Kd, M = W.shape
    KC, G = Kd // 128, M // cols
    return np.ascontiguousarray(W.reshape(KC, 128, G, cols).transpose(2, 1, 0, 3).reshape(G, 128, KC * cols))


def blk_up(W):
    Wg = W[:, :DFF].reshape(DC, 128, FC, 1, 128)
    Wv = W[:, DFF:].reshape(DC, 128, FC, 1, 128)
    Wc = np.concatenate([Wg, Wv], axis=3)
    return np.ascontiguousarray(Wc.transpose(2, 1, 0, 3, 4).reshape(FC, 128, DC * 256))


def colvec(v):
    return np.ascontiguousarray(v.reshape(-1, 128).T)


def cols(*vs):
    return np.ascontiguousarray(np.concatenate([colvec(v) for v in vs], axis=1).astype(np.float32))


def build_M(L, CT=256, TT=512):
    nc = bass.Bass("TRN2", target_bir_lowering=False)
    k = K(nc)
    SEQ = [('c', CT), ('l', L)]

    def din(name, shape, dt=F32):
        return nc.dram_tensor(name, shape, dt, kind="ExternalInput").ap()

    I = {}
    for s, Ls in SEQ:
        I['ux' + s] = din('ux' + s, [128, Ls + 3], BF16)
        I['ug' + s] = din('ug' + s, [128, Ls], BF16)
        I['up' + s] = din('up' + s, [128, Ls + 16], BF16)
        I['uf' + s] = din('uf' + s, [128, Ls], BF16)
        I['q' + s] = din('q' + s, [128, Ls], BF16)
        I['k' + s] = din('k' + s, [128, Ls], BF16)
        I['v' + s] = din('v' + s, [Ls, 128], BF16)
        I['icnt' + s] = din('icnt' + s, [128, Ls])
        L1 = Ls // 128
        I['F1' + s] = din('F1' + s, [L1, 4 * L1])
        I['Tw' + s] = din('Tw' + s, [128, 2 * L1])
        I['yo' + s] = nc.dram_tensor('yo' + s, [4, 128, Ls], BF16, kind="ExternalOutput").ap()
    lruw_in = din('lruw', [128, 512]); lrup_in = din('lrup', [128, 11])
    poolw_in = din('poolw', [128, 128]); poolp_in = din('poolp', [128, 5])
    f128_in = din('f128', [128, 384]); rm_in = din('rm', [128, 128])
    cs_in = din('cossin', [2, 128, L]); dlam_in = din('dlam', [128, 256]); attp_in = din('attp', [128, 3])

    wq = k.sb('wq', [128, 512 + 128 + 384 + 128], BF16)
    oLW, oPW, oF, oRM = 0, 512, 640, 1024
    k.dma('pool', wq, wq.t[:, oLW:oLW + 512], None, lruw_in)
    k.dma('pool', wq, wq.t[:, oPW:oPW + 128], None, poolw_in)
    k.dma('pool', wq, wq.t[:, oF:oF + 384], None, f128_in)
    k.dma('pool', wq, wq.t[:, oRM:oRM + 128], None, rm_in)
    pp = k.sb('pp', [128, 11 + 5 + 3 + 8], F32)
    oLP, oPP, oAP, oDV = 0, 11, 16, 19
    k.dma('sp', pp, pp.t[:, oLP:oLP + 11], None, lrup_in)
    k.dma('sp', pp, pp.t[:, oPP:oPP + 5], None, poolp_in)
    k.dma('sp', pp, pp.t[:, oAP:oAP + 3], None, attp_in)
    ones = k.sb("ones", [128, 128], BF16)
    k.op('dve', [], [ones], lambda e: e.memset(ones.t[:], 1.0))
    onec = k.sb("onec", [128, 2], F32)
    k.op('dve', [], [onec], lambda e: e.memset(onec.t[:, 0:1], 1.0))
    k.op('dve', [onec], [onec], lambda e: e.memset(onec.t[:, 1:2], EPS))
    sc0 = k.sb('sc0', [128, 256], F32)
    k.op('act', [pp], [sc0], lambda e: e.activation(sc0.t[:, 0:2], pp.t[:, oLP + 9:oLP + 11], AF.Exp, scale=-1.0))
    k.op('act', [sc0, onec], [sc0], lambda e: e.activation(sc0.t[:, 2:4], sc0.t[:, 0:2], AF.Ln, bias=onec.t[:, 0:1], scale=1.0))
    k.op('dve', [sc0], [pp], lambda e: e.tensor_scalar(pp.t[:, oDV:oDV + 2], sc0.t[:, 2:4], -8.0, None, ALU.mult))
    k.op('dve', [sc0], [pp], lambda e: e.tensor_scalar(pp.t[:, oDV + 2:oDV + 4], sc0.t[:, 2:4], -16.0, None, ALU.mult))
    dl = k.sb('dl', [128, 256], F32)
    k.dma('sp', dl, dl.t[:, :], None, dlam_in)
    k.op('dve', [dl], [sc0], lambda e: e.tensor_tensor(sc0.t[:, 0:64], dl.t[:, 0:64], dl.t[:, 64:128], ALU.mult))
    k.op('dve', [dl, sc0], [sc0], lambda e: e.tensor_tensor(sc0.t[:, 64:128], dl.t[:, 128:192], dl.t[:, 192:256], ALU.mult))
    k.op('dve', [sc0], [sc0], lambda e: e.reduce_sum(sc0.t[:, 128:130], sc0.t[:, 0:128].rearrange("p (a b) -> p a b", a=2), mybir.AxisListType.X))
    k.op('act', [sc0], [sc0], lambda e: e.activation(sc0.t[:, 130:132], sc0.t[:, 128:130], AF.Exp))
    k.op('dve', [sc0], [sc0], lambda e: e.tensor_tensor(sc0.t[:, 132:133], sc0.t[:, 131:132], sc0.t[:, 130:131], ALU.subtract))
    k.op('dve', [sc0, pp], [pp], lambda e: e.tensor_tensor(pp.t[:, oDV + 4:oDV + 5], sc0.t[:, 132:133], pp.t[:, oAP + 1:oAP + 2], ALU.subtract))

    k.op('dve', [pp], [pp], lambda e: e.tensor_tensor(pp.t[:, oDV + 5:oDV + 6], pp.t[:, oAP:oAP + 1], pp.t[:, oAP + 2:oAP + 3], ALU.mult))
    NR = 3
    rot = {}

    def nxt(key, n=NR):
        v = rot.get(key, 0)
        rot[key] = v + 1
        return v % n

    in_att = [False]
    ps = [k.ps("ps%d" % i, [128, 512]) if i in (0, 1, 4, 5) else None for i in range(8)]
    f32t = [k.sb("f32t%d" % i, [128, TT + 16], F32) for i in range(20)]
    bft = [k.sb("bft%d" % i, [128, TT + 16], BF16) for i in range(12)]
    outb = [k.sb("outb%d" % i, [128, TT], BF16) for i in range(NR)]

    def F():
        return f32t[nxt('f', 20)]

    def B():
        return bft[nxt('b', 12)]

    def P():
        if in_att[0]:
            return ps[nxt('p', 2)]
        return ps[(0, 1, 4, 5)[nxt('p4', 4)]]

    def store(y_ap, src_tk, src_ap):
        k.dma('pool', None, y_ap, src_tk, src_ap, final=True)

    hstate = k.sb('hstate', [128, 4], F32)
    k.push_scope()
    hf_all = k.sb('hf_all', [128, L], F32)

    def lru(s, Ls):
        yo = I['yo' + s]
        nt = (Ls + TT - 1) // TT
        tiles = [(i * TT, min(TT, Ls - i * TT)) for i in range(nt)]

        def gates(t0, n, d):
            xt = B()
            k.dma('sp', xt, xt.t[:, :n + 3], None, I['ux' + s][:, t0:t0 + n + 3])
            u = F(); u2 = F()
            k.op('dve', [xt, pp], [u], lambda e: e.tensor_scalar(u.t[:, :n], xt.t[:, 0:n], _col(pp.t, oLP + 0), _col(pp.t, oLP + 4), ALU.mult, ALU.add))
            k.op('dve', [xt, pp, u], [u2], lambda e: e.scalar_tensor_tensor(u2.t[:, :n], xt.t[:, 1:n + 1], _col(pp.t, oLP + 1), u.t[:, :n], ALU.mult, ALU.add))
            k.op('dve', [xt, pp, u2], [u], lambda e: e.scalar_tensor_tensor(u.t[:, :n], xt.t[:, 2:n + 2], _col(pp.t, oLP + 2), u2.t[:, :n], ALU.mult, ALU.add))
            k.op('dve', [xt, pp, u], [u2], lambda e: e.scalar_tensor_tensor(u2.t[:, :n], xt.t[:, 3:n + 3], _col(pp.t, oLP + 3), u.t[:, :n], ALU.mult, ALU.add))
            ub = B()
            k.op('act', [u2], [ub], lambda e: e.activation(ub.t[:, :n], u2.t[:, :n], AF.Identity))
            pr, pi = P(), P()
            k.mm(pr, pr.t[:, :n], [(wq.t[:, oLW + (2 * d) * 128:oLW + (2 * d + 1) * 128], ub.t[:, :n])], [wq, ub])
            k.mm(pi, pi.t[:, :n], [(wq.t[:, oLW + (2 * d + 1) * 128:oLW + (2 * d + 2) * 128], ub.t[:, :n])], [wq, ub])
            r = F(); ig = F(); a = F(); sq = F(); bb = F()
            k.op('act', [pr, pp], [r], lambda e: e.activation(r.t[:, :n], pr.t[:, :n], AF.Sigmoid, bias=_col(pp.t, oLP + 5 + d), scale=1.0))
            k.op('act', [pi, pp], [ig], lambda e: e.activation(ig.t[:, :n], pi.t[:, :n], AF.Sigmoid, bias=_col(pp.t, oLP + 7 + d), scale=1.0))
            k.op('act', [r, pp], [a], lambda e: e.activation(a.t[:, :n], r.t[:, :n], AF.Exp, scale=_col(pp.t, oDV + d)))
            k.op('act', [r, pp], [sq], lambda e: e.activation(sq.t[:, :n], r.t[:, :n], AF.Exp, scale=_col(pp.t, oDV + 2 + d)))
            k.op('act', [sq, onec], [sq], lambda e: e.activation(sq.t[:, :n], sq.t[:, :n], AF.Sqrt, bias=onec.t[:, 0:1], scale=-1.0))
            k.op('dve', [sq, ig], [bb], lambda e: e.tensor_tensor(bb.t[:, :n], sq.t[:, :n], ig.t[:, :n], ALU.mult))
            k.op('dve', [bb, u2], [bb], lambda e: e.tensor_tensor(bb.t[:, :n], bb.t[:, :n], u2.t[:, :n], ALU.mult))
            return a, bb

        for ti, (t0, n) in enumerate(tiles):
            a, bb = gates(t0, n, 0)
            if ti == 0:
                if s == 'c':
                    k.op('dve', [a, bb], [hf_all], lambda e: e.tensor_tensor_scan(hf_all.t[:, t0:t0 + n], a.t[:, :n], bb.t[:, :n], 0.0, ALU.mult, ALU.add))
                else:
                    k.op('dve', [a, bb, hstate], [hf_all], lambda e: e.tensor_tensor_scan(hf_all.t[:, t0:t0 + n], a.t[:, :n], bb.t[:, :n], hstate.t[:, 0:1], ALU.mult, ALU.add))
            else:
                k.op('dve', [a, bb, hf_all], [hf_all], lambda e: e.tensor_tensor_scan(hf_all.t[:, t0:t0 + n], a.t[:, :n], bb.t[:, :n], hf_all.t[:, t0 - 1:t0], ALU.mult, ALU.add))
        if s == 'c':
            k.op('dve', [hf_all], [hstate], lambda e: e.tensor_copy(hstate.t[:, 0:1], hf_all.t[:, Ls - 1:Ls]))
        prev = None
        for ti in range(nt - 1, -1, -1):
            t0, n = tiles[ti]
            a, bb = gates(t0, n, 1)
            hb_ = F()
            if prev is None:
                init = 0.0 if s == 'c' else hstate.t[:, 1:2]
                rd = [a, bb] if s == 'c' else [a, bb, hstate]
            else:
                init = prev.t[:, 0:1]
                rd = [a, bb, prev]
            k.op('dve', rd, [hb_], lambda e: e.tensor_tensor_scan(hb_.t[:, n - 1::-1] if False else hb_.t[:, 0:n][:, ::-1], a.t[:, 0:n][:, ::-1], bb.t[:, 0:n][:, ::-1], init, ALU.mult, ALU.add))
            prev = hb_
            gt = B(); gg = F(); hs = F(); ob = outb[nxt('o')]
            k.dma('sp', gt, gt.t[:, :n], None, I['ug' + s][:, t0:t0 + n])
            k.op('act', [gt], [gg], lambda e: e.activation(gg.t[:, :n], gt.t[:, :n], AF.Gelu))
            k.op('pool', [hb_, hf_all], [hs], lambda e: e.tensor_tensor(hs.t[:, :n], hb_.t[:, :n], hf_all.t[:, t0:t0 + n], ALU.add))
            k.op('dve', [hs, gg], [ob], lambda e: e.tensor_tensor(ob.t[:, :n], hs.t[:, :n], gg.t[:, :n], ALU.mult))
            store(yo[0, :, t0:t0 + n], ob, ob.t[:, :n])
        if s == 'c':
            k.op('dve', [prev], [hstate], lambda e: e.tensor_copy(hstate.t[:, 1:2], prev.t[:, 0:1]))

    def pool(s, Ls):
        yo = I['yo' + s]
        for t0 in range(0, Ls, TT):
            n = min(TT, Ls - t0)
            ut = B(); ic = F()
            k.dma('sp', ut, ut.t[:, :n + 16], None, I['up' + s][:, t0:t0 + n + 16])
            k.dma('sp', ic, ic.t[:, :n], None, I['icnt' + s][:, t0:t0 + n])
            w1 = F(); w2 = F(); w4 = F(); w8 = F(); ws = F(); ws2 = F()
            m = n + 16
            k.op('pool', [ut], [w1], lambda e: e.tensor_tensor(w1.t[:, 1:m], ut.t[:, 0:m - 1], ut.t[:, 1:m], ALU.add))
            k.op('pool', [w1], [w2], lambda e: e.tensor_tensor(w2.t[:, 2:m - 1], w1.t[:, 1:m - 2], w1.t[:, 3:m], ALU.add))
            k.op('pool', [w2], [w4], lambda e: e.tensor_tensor(w4.t[:, 4:m - 3], w2.t[:, 2:m - 5], w2.t[:, 6:m - 1], ALU.add))
            k.op('pool', [w4], [w8], lambda e: e.tensor_tensor(w8.t[:, 8:m - 7], w4.t[:, 4:m - 11], w4.t[:, 12:m - 3], ALU.add))
            k.op('dve', [w1, pp], [ws], lambda e: e.tensor_scalar(ws.t[:, :n], w1.t[:, 8:8 + n], _col(pp.t, oPP + 1), None, ALU.mult))
            k.op('dve', [w2, pp, ws], [ws2], lambda e: e.scalar_tensor_tensor(ws2.t[:, :n], w2.t[:, 8:8 + n], _col(pp.t, oPP + 2), ws.t[:, :n], ALU.mult, ALU.add))
            k.op('dve', [w4, pp, ws2], [ws], lambda e: e.scalar_tensor_tensor(ws.t[:, :n], w4.t[:, 8:8 + n], _col(pp.t, oPP + 3), ws2.t[:, :n], ALU.mult, ALU.add))
            k.op('dve', [w8, pp, ws], [ws2], lambda e: e.scalar_tensor_tensor(ws2.t[:, :n], w8.t[:, 8:8 + n], _col(pp.t, oPP + 4), ws.t[:, :n], ALU.mult, ALU.add))
            k.op('dve', [ws2, ic], [ws], lambda e: e.tensor_tensor(ws.t[:, :n], ws2.t[:, :n], ic.t[:, :n], ALU.mult))
            db = B()
            k.op('dve', [ws, ut], [db], lambda e: e.tensor_tensor(db.t[:, :n], ws.t[:, :n], ut.t[:, 8:8 + n], ALU.subtract))
            p = P(); ob = outb[nxt('o')]
            k.mm(p, p.t[:, :n], [(wq.t[:, oPW:oPW + 128], db.t[:, :n])], [wq, db])
            k.op('act', [p, pp], [ob], lambda e: e.activation(ob.t[:, :n], p.t[:, :n], AF.Identity, scale=_col(pp.t, oPP + 0)))
            store(yo[1, :, t0:t0 + n], ob, ob.t[:, :n])

    for s, Ls in SEQ:
        lru(s, Ls)
    k.pop_scope()
    for s, Ls in SEQ:
        pool(s, Ls)

    def fft(s, Ls):
        yo = I['yo' + s]
        L1 = Ls // 128
        scale = 1.0 / math.sqrt(Ls * 128.0)
        k.push_scope()
        ufs = k.sb('ufs' + s, [128, Ls], BF16)
        k.dma('sp', ufs, ufs.t[:, :], None, I['uf' + s])
        f1 = k.sb('f1' + s, [128, 4 * L1], BF16)
        k.dma('pool', f1, f1.t[0:L1, :], None, I['F1' + s])
        tw = k.sb('tw' + s, [128, 2 * L1], F32)
        k.dma('sp', tw, tw.t[:, :], None, I['Tw' + s])
        Ab = k.sb('Ab' + s, [128, 2 * 64 * 128], BF16)
        Bp = k.sb('Bp' + s, [128, 2 * 64 * L1], BF16)
        Yh = k.sb('Yh' + s, [128, 128 * L1], BF16)
        Av = Ab.t[:, :].rearrange("p (r c l) -> p r c l", r=2, c=64)
        Bv = Bp.t[:, :].rearrange("p (r c l) -> p r c l", r=2, c=64)
        Yv = Yh.t[:, :].rearrange("p (a b) -> p a b", b=L1)
        for h in range(2):
            for l2 in range(128):
                p = P()
                k.mm(p, p.t[0:L1, 0:64], [(ufs.t[:, l2 * L1:(l2 + 1) * L1], wq.t[:, oF + h * 64:oF + h * 64 + 64])], [ufs, wq])
                k.mm(p, p.t[0:L1, 64:128], [(ufs.t[:, l2 * L1:(l2 + 1) * L1], wq.t[:, oF + 128 + h * 64:oF + 128 + h * 64 + 64])], [ufs, wq])
                k.op('act' if l2 % 2 else 'dve', [p], [Ab], (lambda e, p=p, l2=l2: e.activation(Av[0:L1, :, :, l2], p.t[0:L1, 0:128].rearrange("p (r c) -> p r c", r=2), AF.Identity)) if l2 % 2 else
                     (lambda e, p=p, l2=l2: e.tensor_copy(Av[0:L1, :, :, l2], p.t[0:L1, 0:128].rearrange("p (r c) -> p r c", r=2))))
            for c in range(64):
                p = P()
                k.mm(p, p.t[:, 0:2 * L1], [(Av[0:L1, 0, c, :], f1.t[0:L1, 0:2 * L1]), (Av[0:L1, 1, c, :], f1.t[0:L1, 2 * L1:4 * L1])], [Ab, f1])
                br, bi = p.t[:, 0:L1], p.t[:, L1:2 * L1]
                tc, ts = tw.t[:, 0:L1], tw.t[:, L1:2 * L1]
                t1 = F(); t2 = F(); t3 = F(); t4 = F()
                k.op('dve', [p, tw], [t1], lambda e, t1=t1, br=br, tc=tc: e.tensor_tensor(t1.t[:, :L1], br, tc, ALU.mult))
                k.op('dve', [p, tw], [t2], lambda e, t2=t2, bi=bi, ts=ts: e.tensor_tensor(t2.t[:, :L1], bi, ts, ALU.mult))
                k.op('dve', [p, tw], [t3], lambda e, t3=t3, bi=bi, tc=tc: e.tensor_tensor(t3.t[:, :L1], bi, tc, ALU.mult))
                k.op('dve', [p, tw], [t4], lambda e, t4=t4, br=br, ts=ts: e.tensor_tensor(t4.t[:, :L1], br, ts, ALU.mult))
                k.op('pool', [t1, t2], [Bp], lambda e, t1=t1, t2=t2, c=c: e.tensor_tensor(Bv[:, 0, c, :], t1.t[:, :L1], t2.t[:, :L1], ALU.add))
                k.op('pool', [t3, t4], [Bp], lambda e, t3=t3, t4=t4, c=c: e.tensor_tensor(Bv[:, 1, c, :], t3.t[:, :L1], t4.t[:, :L1], ALU.subtract))
            for l1 in range(L1):
                p = P()
                k.mm(p, p.t[0:64, 0:128], [(Bv[:, 0, :, l1], wq.t[:, oF:oF + 128]), (Bv[:, 1, :, l1], wq.t[:, oF + 256:oF + 384])], [Bp, wq])
                k.op('act', [p], [Yh], lambda e, p=p, l1=l1: e.activation(Yv[h * 64:h * 64 + 64, :, l1], p.t[0:64, 0:128], AF.Identity, scale=scale))
        store(yo[2, :, :].rearrange("p (a b) -> p a b", b=L1), Yh, Yv)
        k.pop_scope()

    for s, Ls in SEQ:
        fft(s, Ls)

    k.push_scope()
    LK = CT + L
    KT = k.sb('KT', [128, LK], BF16)
    VV = k.sb('VV', [128, LK], BF16)
    QT = k.sb('QT', [128, L], BF16)
    QC = k.sb('QC', [128, CT], BF16)
    k.dma('sp', KT, KT.t[:, 0:CT], None, I['kc'])
    k.dma('sp', QC, QC.t[:, :], None, I['qc'])
    Vv = VV.t[:, :].rearrange("p (c e) -> p c e", e=128)
    k.dma('sp', VV, Vv[:, 0:CT // 128, :], None, I['vc'].rearrange("(c p) e -> p c e", p=128))
    k.dma('sp', VV, Vv[:, CT // 128:, :], None, I['vl'].rearrange("(c p) e -> p c e", p=128))
    for nm, dst, off in (('ql', QT, 0), ('kl', KT, CT)):
        for t0 in range(0, L, TT):
            n = min(TT, L - t0)
            xt = B(); ct = F(); st = F()
            k.dma('sp', xt, xt.t[:, :n], None, I[nm][:, t0:t0 + n])
            k.dma('sp', ct, ct.t[:, :n], None, cs_in[0, :, t0:t0 + n])
            k.dma('sp', st, st.t[:, :n], None, cs_in[1, :, t0:t0 + n])
            p = P()
            k.mm(p, p.t[:, :n], [(wq.t[:, oRM:oRM + 128], xt.t[:, :n])], [wq, xt])
            t1 = F(); t2 = F()
            k.op('dve', [xt, ct], [t1], lambda e, t1=t1, xt=xt, ct=ct: e.tensor_tensor(t1.t[:, :n], xt.t[:, :n], ct.t[:, :n], ALU.mult))
            k.op('dve', [p, st], [t2], lambda e, t2=t2, p=p, st=st: e.tensor_tensor(t2.t[:, :n], p.t[:, :n], st.t[:, :n], ALU.mult))
            k.op('pool', [t1, t2], [dst], lambda e, t1=t1, t2=t2, dst=dst, a=off + t0: e.tensor_tensor(dst.t[:, a:a + n], t1.t[:, :n], t2.t[:, :n], ALU.add))
    pT = [k.sb('pT%d' % i, [128, 2 * TT], BF16) for i in range(4)]
    pS = [k.ps("pS%d" % i, [128, 2 * TT]) for i in range(2)]
    zacc = [k.sb('zacc%d' % i, [128, TT], F32) for i in range(2)]
    onef = k.sb('onef', [128, 128], F32)
    k.op('dve', [], [onef], lambda e: e.memset(onef.t[:], 1.0))

    sel = k.sb('sel', [64, 256], F32)
    k.op('dve', [], [sel], lambda e: e.memset(sel.t[:, :], 0.0))
    k.op('dve', [sel], [sel], lambda e: e.memset(sel.t[0:1, 0:128], 1.0))
    k.op('dve', [sel], [sel], lambda e: e.memset(sel.t[32:33, 128:256], 1.0))

    def attend(s, Ls, Qtile, nk):
        yo = I['yo' + s]
        nch = nk // 128
        for q0 in range(0, Ls, TT):
            n = min(TT, Ls - q0)
            po0, po1, pz0, pz1 = ps[4], ps[5], ps[0], ps[1]
            pts = {}

            def emit_S(c):
                pt = pT[nxt('pt', 4)]
                p = pS[nxt('pS', 2)]
                for j in range(2):
                    k.mm(p, p.t[:, j * TT:j * TT + n], [(KT.t[64 * j:64 * j + 64, c * 128:(c + 1) * 128], Qtile.t[64 * j:64 * j + 64, q0:q0 + n])], [KT, Qtile])
                if n == TT:
                    k.op('act', [p], [pt], lambda e, p=p, pt=pt: e.activation(pt.t[:, :], p.t[:, :], AF.Exp, scale=0.125))
                else:
                    for j in range(2):
                        k.op('act', [p], [pt], lambda e, p=p, pt=pt, j=j: e.activation(pt.t[:, j * TT:j * TT + n], p.t[:, j * TT:j * TT + n], AF.Exp, scale=0.125))
                pts[c] = pt

            def emit_PV(c):
                pt = pts.pop(c)
                tgts = [(po0, po0.t[:, :n], Vv[:, c, :], 0), (po1, po1.t[:, :n], Vv[:, c, :], 1),
                        (pz0, pz0.t[0:32, :n], ones.t[:, 0:32], 0), (pz0, pz0.t[32:64, :n], ones.t[:, 0:32], 1)]
                for tgt, oap, lhs, j in tgts:
                    k._wait('pe', k._deps([pt, VV, ones], [tgt]))
                    ins = nc.tensor.matmul(oap, lhs, pt.t[:, j * TT:j * TT + n], start=(c == 0), stop=(c == nch - 1))
                    k.cnt['pe'] += 1
                    ins.then_inc(k.sem['pe'], 1)
                    k._mark((k.sem['pe'], k.cnt['pe']), [pt, VV, ones], [tgt])

            emit_S(0)
            for c in range(nch):
                if c + 1 < nch:
                    emit_S(c + 1)
                emit_PV(c)
            rzs = F(); r0 = F(); r1 = F(); o0 = F(); o1 = F(); oo = F()
            k.op('dve', [pz0], [rzs], lambda e: e.reciprocal(rzs.t[0:64, :n], pz0.t[0:64, :n]))
            for j, rj in enumerate((r0, r1)):
                pb = P()
                k.mm(pb, pb.t[:, :n], [(sel.t[0:64, j * 128:(j + 1) * 128], rzs.t[0:64, :n])], [sel, rzs])
                k.op('act', [pb], [rj], lambda e, pb=pb, rj=rj: e.activation(rj.t[:, :n], pb.t[:, :n], AF.Identity))
            k.op('dve', [po0, r0], [o0], lambda e: e.tensor_tensor(o0.t[:, :n], po0.t[:, :n], r0.t[:, :n], ALU.mult))
            k.op('dve', [po1, r1], [o1], lambda e: e.tensor_tensor(o1.t[:, :n], po1.t[:, :n], r1.t[:, :n], ALU.mult))
            k.op('dve', [o0, o1, pp], [oo], lambda e: e.scalar_tensor_tensor(oo.t[:, :n], o1.t[:, :n], _col(pp.t, oDV + 4), o0.t[:, :n], ALU.mult, ALU.add))
            sq = B(); p = P(); rs = F(); t3 = F(); ob = outb[nxt('o')]
            k.op('act', [oo], [sq], lambda e: e.activation(sq.t[:, :n], oo.t[:, :n], AF.Square))
            k.mm(p, p.t[:, :n], [(ones.t[:, :], sq.t[:, :n])], [ones, sq])
            k.op('act', [p, onec], [rs], lambda e: e.activation(rs.t[:, :n], p.t[:, :n], AF.Sqrt, bias=onec.t[:, 1:2], scale=1.0 / 128))
            k.op('dve', [rs], [t3], lambda e: e.reciprocal(t3.t[:, :n], rs.t[:, :n]))
            k.op('dve', [oo, t3, pp], [ob], lambda e: e.scalar_tensor_tensor(ob.t[:, :n], oo.t[:, :n], _col(pp.t, oDV + 5), t3.t[:, :n], ALU.mult, ALU.mult))
            store(yo[3, :, q0:q0 + n], ob, ob.t[:, :n])

    in_att[0] = True
    attend('c', CT, QC, CT)
    attend('l', L, QT, LK)
    k.pop_scope()
    k.finish()
    return nc


POOL_HALF = (1, 2, 4, 8)


def _dft(n):
    i = np.arange(n)
    ang = 2.0 * np.pi * np.outer(i, i) / n
    return np.cos(ang), np.sin(ang)


def m_consts(L, CT, j, lam_init):
    c = {}
    C, S = _dft(128)
    c['f128'] = np.concatenate([C, -S, S], 1).astype(np.float32)
    rm = np.zeros((128, 128), np.float32)
    for d in range(128):
        if d % 32 < 16:
            rm[d + 16, d] = -1.0
        else:
            rm[d - 16, d] = 1.0
    c['rm'] = rm
    t = np.arange(L)
    inv = (10000.0 ** (-np.arange(16, dtype=np.float32) / 16)).astype(np.float32)
    ang_r = (t // 64).astype(np.float32)[:, None] * inv
    ang_c = (t % 64).astype(np.float32)[:, None] * inv
    ang = np.zeros((128, L), np.float32)
    for p in range(128):
        d = p % 64
        ang[p] = ang_r[:, d % 16] if d < 32 else ang_c[:, (d - 32) % 16]
    c['cossin'] = np.stack([np.cos(ang), np.sin(ang)]).astype(np.float32)
    for s, Ls in (('c', CT), ('l', L)):
        L1 = Ls // 128
        C1, S1 = _dft(L1)
        c['F1' + s] = np.concatenate([C1, -S1, S1, C1], 1).astype(np.float32)
        a = 2.0 * np.pi * np.outer(np.arange(128), np.arange(L1)) / Ls
        c['Tw' + s] = np.concatenate([np.cos(a), np.sin(a)], 1).astype(np.float32)
        tt = np.arange(Ls)
        half = POOL_HALF[j]
        lo = np.clip(tt - half, 0, Ls - 1)
        hi = np.clip(tt + half - 1, 0, Ls - 1)
        c['icnt' + s] = np.ascontiguousarray(np.broadcast_to((1.0 / (hi - lo + 1)).astype(np.float32)[None], (128, Ls)))
    sel = np.zeros((128, 4), np.float32)
    sel[:, j] = 1.0
    c['sel'] = sel
    c['lam_init'] = np.full((128, 1), lam_init, np.float32)
    c['one_m_lam_init'] = np.full((128, 1), 1.0 - lam_init, np.float32)
    return c


def m_inputs(pT, pcT, j, L, CT, P, consts):
    inp = {}
    for s, Ls, src in (('c', CT, pcT), ('l', L, pT)):
        sl = lambda base: src[base + 128 * j: base + 128 * (j + 1)]
        z = lambda n: np.zeros((128, n), src.dtype)
        inp['ux' + s] = np.concatenate([z(2), sl(0), z(1)], 1)
        inp['ug' + s] = np.ascontiguousarray(sl(512))
        inp['up' + s] = np.concatenate([z(8), sl(1024), z(8)], 1)
        L1 = Ls // 128
        inp['uf' + s] = np.ascontiguousarray(sl(1536).reshape(128, L1, 128).transpose(0, 2, 1).reshape(128, Ls))
        inp['q' + s] = np.ascontiguousarray(sl(2048))
        inp['k' + s] = np.ascontiguousarray(sl(2560))
        inp['v' + s] = np.ascontiguousarray(sl(3072).T)
        for nm in ('icnt', 'F1', 'Tw'):
            inp[nm + s] = consts[nm + s]
    lw = np.zeros((128, 512), np.float32)
    for d in range(2):
        for gi, W in enumerate((P['lru_wa'], P['lru_wi'])):
            for a in range(2):
                lw[64 * a:64 * a + 64, (2 * d + gi) * 128 + 64 * a:(2 * d + gi) * 128 + 64 * a + 64] = W[d, 2 * j + a]
    inp['lruw'] = lw
    cs = slice(128 * j, 128 * (j + 1))
    inp['lrup'] = np.ascontiguousarray(np.stack(
        [P['lru_conv_w'][0, cs], P['lru_conv_w'][1, cs], P['lru_conv_w'][2, cs], P['lru_conv_w'][3, cs], P['lru_conv_b'][cs],
         P['lru_ba'][0, cs], P['lru_ba'][1, cs], P['lru_bi'][0, cs], P['lru_bi'][1, cs], P['lru_lam'][0, cs], P['lru_lam'][1, cs]], 1).astype(np.float32))
    inp['poolw'] = np.ascontiguousarray(P['pool_w'][j])
    inp['poolp'] = np.ascontiguousarray(np.concatenate([P['pool_scale'][cs][:, None], consts['sel']], 1).astype(np.float32))
    inp['f128'] = consts['f128']
    inp['rm'] = consts['rm']
    inp['cossin'] = consts['cossin']
    inp['dlam'] = np.ascontiguousarray(np.broadcast_to(P['diff_lam'].reshape(1, 256), (128, 256)))
    inp['attp'] = np.ascontiguousarray(np.concatenate([P['diff_subln_g'][:, None], consts['lam_init'], consts['one_m_lam_init']], 1).astype(np.float32))
    return inp


ACOLS = 6 * D // NCORES


def build_ADA(depth=4):
    nc = bass.Bass("TRN2", target_bir_lowering=False)
    k = K(nc)
    cT = nc.dram_tensor("cT", [128, DC * 3], F32, kind="ExternalInput").ap()
    aw = nc.dram_tensor("aw", [depth * DC, 128, ACOLS], F32, kind="ExternalInput").ap()
    ab_in = nc.dram_tensor("ab", [3, depth * ACOLS], F32, kind="ExternalInput").ap()
    mo = nc.dram_tensor("mo", [3, depth * ACOLS], F32, kind="ExternalOutput").ap()
    ct = k.sb("ct", [128, DC * 3], F32)
    sc = k.sb("sc", [128, DC * 3], F32)
    bias = k.sb("bias", [3, depth * ACOLS], F32)
    res = k.sb("res", [3, depth * ACOLS], F32)
    k.dma('sp', ct, ct.t[:, :], None, cT)
    k.dma('sp', bias, bias.t[:, :], None, ab_in)
    k.op('act', [ct], [sc], lambda e: e.activation(sc.t[:, :], ct.t[:, :], AF.Silu))
    wb = [k.sb("awb%d" % i, [128, ACOLS], F32) for i in range(4)]
    pss = [k.ps("aps%d" % i, [128, 512]) for i in range(3)]
    for l in range(depth):
        for kc in range(DC):
            w = wb[(l * DC + kc) % 4]
            k.dma('sp', w, w.t[:, :], None, aw[l * DC + kc])
            for g in range(3):
                k._wait('pe', k._deps([w, sc], [pss[g]]))
                ins = nc.tensor.matmul(pss[g].t[0:3, :], sc.t[:, kc * 3:kc * 3 + 3], w.t[:, g * 512:(g + 1) * 512], start=(kc == 0), stop=(kc == DC - 1))
                k.cnt['pe'] += 1
                ins.then_inc(k.sem['pe'], 1)
                k._mark((k.sem['pe'], k.cnt['pe']), [w, sc], [pss[g]])
        for g in range(3):
            o = l * ACOLS + g * 512
            k.op('dve', [pss[g], bias], [res], lambda e, g=g, o=o: e.tensor_tensor(res.t[0:3, o:o + 512], pss[g].t[0:3, :], bias.t[0:3, o:o + 512], ALU.add))
    k.dma('pool', None, mo, res, res.t[:, :], final=True)
    k.finish()
    return nc


_CACHE = {}


def _prog(key, fn):
    if key not in _CACHE:
        _CACHE[key] = fn()
    return _CACHE[key]


def kernel(x, c, ctx, c_ctx, ada_w, ada_b, norm_g, w_in, lru_conv_w, lru_conv_b, lru_wa, lru_ba, lru_wi, lru_bi,
           lru_lam, pool_w, pool_scale, fourier_w, diff_lam, diff_subln_g, w_out, ffn_w_up, ffn_conv_w, ffn_conv_b,
           ffn_w_down):
    A = lambda a: np.asarray(a)
    x, c, ctx, c_ctx = A(x), A(c), A(ctx), A(c_ctx)
    ada_w, ada_b, norm_g, w_in = A(ada_w), A(ada_b), A(norm_g), A(w_in)
    Bn, L, _ = x.shape
    CT = ctx.shape[1]
    depth = w_in.shape[0]
    NS = NCORES // Bn
    NT = L // NS
    TT = 512
    ntile = NT // TT
    cores = list(range(NCORES))
    PL = [dict(lru_conv_w=A(lru_conv_w)[l], lru_conv_b=A(lru_conv_b)[l], lru_wa=A(lru_wa)[l], lru_ba=A(lru_ba)[l],
               lru_wi=A(lru_wi)[l], lru_bi=A(lru_bi)[l], lru_lam=A(lru_lam)[l], pool_w=A(pool_w)[l],
               pool_scale=A(pool_scale)[l], diff_lam=A(diff_lam)[l], diff_subln_g=A(diff_subln_g)[l]) for l in range(depth)]

    c3 = np.concatenate([c, c_ctx[None]], 0).astype(np.float32)
    cT = np.ascontiguousarray(c3.reshape(3, DC, 128).transpose(2, 1, 0).reshape(128, DC * 3))
    maps = []
    for r in cores:
        cs = slice(r * ACOLS, (r + 1) * ACOLS)
        aw = np.ascontiguousarray(ada_w[:, :, cs].reshape(depth * DC, 128, ACOLS))
        ab = np.ascontiguousarray(np.broadcast_to(ada_b[:, cs].reshape(1, depth * ACOLS), (3, depth * ACOLS)))
        maps.append(dict(cT=cT, aw=aw, ab=ab))
    res = run_bass_kernel_spmd(_prog('ada', lambda: build_ADA(depth)), maps, core_ids=cores)
    mod = np.concatenate([res.results[r]['mo'].reshape(3, depth, ACOLS) for r in cores], axis=2)
    mod = mod.reshape(3, depth, 6, D)
    del maps, res

    xT = [np.ascontiguousarray(x[b].T) for b in range(Bn)]
    xcT = [np.ascontiguousarray(ctx[b].T) for b in range(Bn)]

    def t_launch(lC, lA, yT, ycT):
        do_C, do_A = lC is not None, lA is not None
        NH = 2 * ntile if do_C else 0
        common = {}
        if do_C:
            common.update(ngC=cols(norm_g[lC, 1], norm_g[lC, 2], norm_g[lC, 3]),
                          cw=np.ascontiguousarray(A(ffn_conv_w)[lC].reshape(3, FC, 128).transpose(2, 0, 1).reshape(128, 3 * FC)),
                          cb=colvec(A(ffn_conv_b)[lC]), wf=blk(A(fourier_w)[lC], 512), wo=blk(A(w_out)[lC], 256),
                          wu=blk_up(A(ffn_w_up)[lC]), wd=blk(A(ffn_w_down)[lC], 128))
        if do_A:
            common.update(ngA=cols(norm_g[lA, 0]), wi=blk(w_in[lA], 256))
        maps = []
        for r in cores:
            b, s = divmod(r, NS)
            t0 = s * NT
            m = dict(common)
            hm = np.zeros(max(NH, 1), np.float32)
            hidx = []
            for i in range(ntile if do_C else 0):
                for hi, tpos in enumerate((t0 + i * TT - 1, t0 + (i + 1) * TT)):
                    ok = 0 <= tpos < L
                    hm[2 * i + hi] = 1.0 if ok else 0.0
                    hidx.append(tpos if ok else 0)
            xh = xT[b][:, hidx] if do_C else np.zeros((D, 0), np.float32)
            m['xT'] = np.ascontiguousarray(np.concatenate([xh, xT[b][:, t0:t0 + NT], xcT[b]], 1))
            if do_C:
                m['yT'] = np.ascontiguousarray(np.concatenate([yT[b][:, hidx], yT[b][:, t0:t0 + NT], ycT[b]], 1))
                m['hmask'] = np.ascontiguousarray(np.broadcast_to(hm[None, :NH], (128, NH)))
                m['modC'] = cols(*[mod[row, lC, i] for row in (b, 2) for i in (2, 3, 4, 5)])
            if do_A:
                m['modA'] = cols(*[mod[row, lA, i] for row in (b, 2) for i in (0, 1)])
            maps.append(m)
        res = run_bass_kernel_spmd(_prog(('T', NT, do_C, do_A, CT), lambda: build_T(NT, do_C, do_A, CT, TT)), maps, core_ids=cores)
        pT = pcT = None
        if do_C:
            for b in range(Bn):
                xT[b] = np.ascontiguousarray(np.concatenate([res.results[b * NS + s]['xo'][:, :NT] for s in range(NS)], 1))
                xcT[b] = np.ascontiguousarray(res.results[b * NS]['xo'][:, NT:])
        if do_A:
            pT = [np.concatenate([res.results[b * NS + s]['po'][:, :NT] for s in range(NS)], 1) for b in range(Bn)]
            pcT = [res.results[b * NS]['po'][:, NT:] for b in range(Bn)]
        return pT, pcT

    pT, pcT = t_launch(None, 0, None, None)
    for l in range(depth):
        lam_init = 0.8 - 0.6 * math.exp(-0.3 * l)
        maps = []
        for r in cores:
            b, j = divmod(r, NS)
            maps.append(m_inputs(pT[b], pcT[b], j, L, CT, PL[l], m_consts(L, CT, j, lam_init)))
        res = run_bass_kernel_spmd(_prog(('M', L, CT), lambda: build_M(L, CT, TT)), maps, core_ids=cores)
        yT, ycT = [], []
        for b in range(Bn):
            yl = np.stack([res.results[b * NS + j]['yol'] for j in range(NS)], 1)
            yc = np.stack([res.results[b * NS + j]['yoc'] for j in range(NS)], 1)
            yT.append(yl.reshape(D, L))
            ycT.append(yc.reshape(D, CT))
        del maps, res
        pT, pcT = t_launch(l, l + 1 if l + 1 < depth else None, yT, ycT)
    out = np.stack([xT[b].T for b in range(Bn)], 0).astype(np.float32)
    return np.ascontiguousarray(out)
```

```python
import contextlib
import math
import numpy as np
import ml_dtypes
import concourse.bass as bass
import concourse.mybir as mybir
from concourse.bass_utils import run_bass_kernel_spmd

F32 = mybir.dt.float32
BF16 = mybir.dt.bfloat16
AF = mybir.ActivationFunctionType
ALU = mybir.AluOpType
NPBF = ml_dtypes.bfloat16

D = 2048
DC = D // 128
DFF = 5632
FC = DFF // 128
DIN = 3584
EPS = 1e-6
NCORES = 8


class Tk:
    def __init__(self, t, name):
        self.t = t
        self.name = name
        self.w = None
        self.r = {}
        self.dsem = {}
        self.dcnt = {}
        self.psum = False

    def __getitem__(self, idx):
        return self.t[idx]


class K:
    def __init__(self, nc):
        self.nc = nc
        self.es = contextlib.ExitStack()
        self.root = self.es
        self.E = {'pe': nc.tensor, 'act': nc.scalar, 'dve': nc.vector, 'pool': nc.gpsimd, 'sp': nc.sync}
        self.sem = {e: self.es.enter_context(nc.semaphore('s_' + e)) for e in ('pe', 'act', 'dve', 'pool')}
        self.cnt = dict.fromkeys(self.sem, 0)
        self.known = {e: {} for e in self.E}
        self.final = []
        self.nd = 0

    def sb(self, name, shape, dt):
        t = Tk(self.es.enter_context(self.nc.sbuf_tensor(name, shape, dt)), name)
        if getattr(self, 'scope_tiles', None) is not None:
            self.scope_tiles.append(t)
        return t

    def push_scope(self):
        self.saved_es = self.es
        self.es = contextlib.ExitStack()
        self.scope_tiles = []

    def pop_scope(self):
        deps = [(self.sem[e], self.cnt[e]) for e in self.sem if self.cnt[e]]
        for t in self.scope_tiles:
            if t.w:
                deps.append(t.w)
            deps.extend(t.r.values())
        for e in self.E:
            self._wait(e, deps)
        self.es.close()
        self.es = self.saved_es
        self.scope_tiles = None

    def ps(self, name, shape, dt=F32):
        t = Tk(self.es.enter_context(self.nc.psum_tensor(name, shape, dt)), name)
        t.psum = True
        return t

    def dram(self, name, shape, dt, kind):
        return Tk(self.nc.dram_tensor(name, shape, dt, kind=kind).ap(), name)

    def _wait(self, e, deps):
        kn = self.known[e]
        for sem, val in deps:
            if e == 'pe' and sem is self.sem['pe']:
                continue
            if kn.get(id(sem), 0) < val:
                self.E[e].wait_ge(sem, val)
                kn[id(sem)] = val

    @staticmethod
    def _deps(reads, writes):
        d = []
        for t in reads:
            if t.w:
                d.append(t.w)
        for t in writes:
            if t.w:
                d.append(t.w)
            d.extend(t.r.values())
        return d

    @staticmethod
    def _mark(tok, reads, writes):
        for t in reads:
            t.r[id(tok[0])] = tok
        for t in writes:
            t.w = tok
            t.r = {}

    def op(self, e, reads, writes, fn):
        writes = list(writes) + [t for t in reads if t.psum]
        self._wait(e, self._deps(reads, writes))
        ins = fn(self.E[e])
        self.cnt[e] += 1
        ins.then_inc(self.sem[e], 1)
        self._mark((self.sem[e], self.cnt[e]), reads, writes)

    def mm(self, out, out_ap, pairs, reads, **kw):
        self._wait('pe', self._deps(reads, [out]))
        n = len(pairs)
        for i, (l, r) in enumerate(pairs):
            ins = self.nc.tensor.matmul(out_ap, l, r, start=(i == 0), stop=(i == n - 1), **kw)
        self.cnt['pe'] += 1
        ins.then_inc(self.sem['pe'], 1)
        self._mark((self.sem['pe'], self.cnt['pe']), reads, [out])

    def dma(self, q, out_tk, out_ap, in_tk, in_ap, final=False, **kw):
        reads = [in_tk] if in_tk is not None else []
        writes = [out_tk] if out_tk is not None else []
        self._wait(q, self._deps(reads, writes))
        own = out_tk if out_tk is not None else in_tk
        if q not in own.dsem:
            self.nd += 1
            own.dsem[q] = self.root.enter_context(self.nc.semaphore('d%d' % self.nd))
            own.dcnt[q] = 0
        ins = self.E[q].dma_start(out=out_ap, in_=in_ap, **kw)
        own.dcnt[q] += 16
        ins.then_inc(own.dsem[q], 16)
        tok = (own.dsem[q], own.dcnt[q])
        self._mark(tok, reads, writes)
        if final:
            self.final.append(tok)

    def finish(self):
        self._wait('sp', self.final)
        self._wait('sp', [(self.sem[e], self.cnt[e]) for e in self.sem if self.cnt[e]])
        self.es.close()


def _col(t, i):
    return t[:, i:i + 1]


def build_T(NT, do_C, do_A, CT=256, TT=512, dbg=0):
    nc = bass.Bass("TRN2", target_bir_lowering=False)
    k = K(nc)
    ntile = NT // TT
    NH = 2 * ntile if do_C else 0
    NTOT = NH + NT + CT
    NOUT = NT + CT

    def din(name, shape, dt=F32):
        return nc.dram_tensor(name, shape, dt, kind="ExternalInput").ap()

    xT = din("xT", [D, NTOT]).rearrange("(c p) n -> p c n", p=128)
    if do_C:
        yT = din("yT", [D, NTOT], BF16).rearrange("(c p) n -> p c n", p=128)
        hmask = din("hmask", [128, NH])
        modC = din("modC", [128, 2 * 4 * DC])
        ngC = din("ngC", [128, 3 * DC])
        cw = din("cw", [128, 3 * FC])
        cb = din("cb", [128, FC])
        wf_in = din("wf", [1, 128, 4 * 512])
        wo_in = din("wo", [8, 128, DC * 256])
        wu_in = din("wu", [FC, 128, DC * 256])
        wd_in = din("wd", [DC, 128, FC * 128])
        xo = nc.dram_tensor("xo", [D, NOUT], F32, kind="ExternalOutput").ap().rearrange("(c p) n -> p c n", p=128)
    if do_A:
        modA = din("modA", [128, 2 * 2 * DC])
        ngA = din("ngA", [128, DC])
        wi_in = din("wi", [14, 128, DC * 256])
        po = nc.dram_tensor("po", [DIN, NOUT], BF16, kind="ExternalOutput").ap()

    def scratch(name, src):
        G, _, Fw = src.shape
        step = max(1, (1 << 20) // (128 * Fw))
        mld = max(d for d in range(1, 2049) if Fw % d == 0)
        pieces = []
        for g0 in range(0, G, step):
            g1 = min(G, g0 + step)
            tk = k.dram("%s_bf%d" % (name, g0), [g1 - g0, 128, Fw], BF16, "Internal")
            k.dma('pool', tk, tk.t[0:g1 - g0], None, src[g0:g1], max_dma_last_dim=mld)
            for g in range(g0, g1):
                pieces.append((tk, g - g0))
        return pieces

    WB = 3
    wbuf = [k.sb("wbuf%d" % i, [128, FC * 128], BF16) for i in range(WB)]
    wctr = [0]

    def wload(pieces, g, width):
        b = wbuf[wctr[0] % WB]
        wctr[0] += 1
        tk, gi = pieces[g]
        k.dma('sp', b, b.t[:, 0:width], tk, tk.t[gi])
        return b

    if do_C:
        wf_s = scratch("wf", wf_in)
        wo_s = scratch("wo", wo_in)
        wu_s = scratch("wu", wu_in)
        wd_s = scratch("wd", wd_in)
    if do_A:
        wi_s = scratch("wi", wi_in)

    ones = k.sb("ones", [128, 128], BF16)
    k.op('dve', [], [ones], lambda e: e.memset(ones.t[:], 1.0))
    epsc = k.sb("epsc", [128, 1], F32)
    k.op('dve', [], [epsc], lambda e: e.memset(epsc.t[:], EPS))
    prm = k.sb("prm", [128, 16 * DC + 4 * FC + NH], F32)
    cst = k.sb("cst", [128, 12 * DC], F32)
    o_modC, o_ngC, o_modA, o_ngA, o_cw, o_cb, o_hm = 0, 8 * DC, 11 * DC, 15 * DC, 16 * DC, 16 * DC + 3 * FC, 16 * DC + 4 * FC
    cG1, cA2, cB2, cG2, cA1, cB1 = 0, 2 * DC, 4 * DC, 6 * DC, 8 * DC, 10 * DC
    if do_C:
        k.dma('sp', prm, prm.t[:, o_modC:o_modC + 8 * DC], None, modC)
        k.dma('sp', prm, prm.t[:, o_ngC:o_ngC + 3 * DC], None, ngC)
        k.dma('sp', prm, prm.t[:, o_cw:o_cw + 3 * FC], None, cw)
        k.dma('sp', prm, prm.t[:, o_cb:o_cb + FC], None, cb)
        k.dma('sp', prm, prm.t[:, o_hm:o_hm + NH], None, hmask)
        for m in range(2):
            mo = o_modC + m * 4 * DC
            k.op('dve', [prm], [cst], lambda e, m=m, mo=mo: e.tensor_tensor(
                cst.t[:, cG1 + m * DC:cG1 + (m + 1) * DC], prm.t[:, mo:mo + DC], prm.t[:, o_ngC:o_ngC + DC], ALU.mult))
            k.op('dve', [prm], [cst], lambda e, m=m, mo=mo: e.scalar_tensor_tensor(
                cst.t[:, cA2 + m * DC:cA2 + (m + 1) * DC], prm.t[:, mo + 2 * DC:mo + 3 * DC], 1.0,
                prm.t[:, o_ngC + DC:o_ngC + 2 * DC], ALU.add, ALU.mult))
            k.op('dve', [prm], [cst], lambda e, m=m, mo=mo: e.tensor_copy(
                cst.t[:, cB2 + m * DC:cB2 + (m + 1) * DC], prm.t[:, mo + DC:mo + 2 * DC]))
            k.op('dve', [prm], [cst], lambda e, m=m, mo=mo: e.tensor_tensor(
                cst.t[:, cG2 + m * DC:cG2 + (m + 1) * DC], prm.t[:, mo + 3 * DC:mo + 4 * DC],
                prm.t[:, o_ngC + 2 * DC:o_ngC + 3 * DC], ALU.mult))
    if do_A:
        k.dma('sp', prm, prm.t[:, o_modA:o_modA + 4 * DC], None, modA)
        k.dma('sp', prm, prm.t[:, o_ngA:o_ngA + DC], None, ngA)
        for m in range(2):
            mo = o_modA + m * 2 * DC
            k.op('dve', [prm], [cst], lambda e, m=m, mo=mo: e.scalar_tensor_tensor(
                cst.t[:, cA1 + m * DC:cA1 + (m + 1) * DC], prm.t[:, mo + DC:mo + 2 * DC], 1.0,
                prm.t[:, o_ngA:o_ngA + DC], ALU.add, ALU.mult))
            k.op('dve', [prm], [cst], lambda e, m=m, mo=mo: e.tensor_copy(
                cst.t[:, cB1 + m * DC:cB1 + (m + 1) * DC], prm.t[:, mo:mo + DC]))

    xs = k.sb("xs", [128, DC, TT], F32)
    of = k.sb("of", [128, DC, TT], F32)
    hb = k.sb("hb", [128, DC, TT], BF16)
    zb = k.sb("zb", [128, 4, TT], BF16)
    ab = k.sb("ab", [128, FC, TT], BF16)
    ghalo = k.sb("ghalo", [128, FC, max(NH, 2)], F32)
    rstd = k.sb("rstd", [128, TT], F32)
    NR = 3
    sqb = [k.sb("sqb%d" % i, [128, TT], BF16) for i in range(NR)]
    tmp = [k.sb("tmp%d" % i, [128, TT], F32) for i in range(NR)]
    gbuf = [k.sb("gbuf%d" % i, [128, TT + 2], F32) for i in range(NR)]
    cva = [k.sb("cva%d" % i, [128, TT], F32) for i in range(NR)]
    cvb = [k.sb("cvb%d" % i, [128, TT], F32) for i in range(NR)]
    geb = [k.sb("geb%d" % i, [128, TT], F32) for i in range(NR)]
    pbo = [k.sb("pbo%d" % i, [128, TT], BF16) for i in range(NR)]
    pg = [k.ps("pg%d" % i, [128, TT]) for i in range(2)]
    pv = [k.ps("pv%d" % i, [128, TT]) for i in range(2)]
    pm = [k.ps("pm%d" % i, [128, TT]) for i in range(2)]
    pst = k.ps("pst", [128, TT])
    rot = {'sq': 0, 'tmp': 0, 'g': 0, 'pg': 0, 'pm': 0, 'pb': 0}

    def nxt(key, n):
        v = rot[key] % n
        rot[key] += 1
        return v

    def sumsq_sq(src_tk, src_ap, n):
        s = sqb[nxt('sq', NR)]
        k.op('act', [src_tk], [s], lambda e: e.activation(s.t[:, :n], src_ap, AF.Square))
        return s

    def sumsq_mm(s, c, n):
        k._wait('pe', k._deps([s, ones], [pst]))
        ins = nc.tensor.matmul(pst.t[:, :n], ones.t[:, :], s.t[:, :n], start=(c == 0), stop=(c == DC - 1))
        k.cnt['pe'] += 1
        ins.then_inc(k.sem['pe'], 1)
        k._mark((k.sem['pe'], k.cnt['pe']), [s, ones], [pst])

    def sumsq_chunk(src_tk, src_ap, c, n):
        sumsq_mm(sumsq_sq(src_tk, src_ap, n), c, n)

    def make_rstd(n):
        t = tmp[nxt('tmp', NR)]
        k.op('act', [pst, epsc], [t], lambda e: e.activation(t.t[:, :n], pst.t[:, :n], AF.Sqrt, bias=epsc.t[:, 0:1], scale=1.0 / D))
        k.op('dve', [t], [rstd], lambda e: e.reciprocal(rstd.t[:, :n], t.t[:, :n]))

    def resid_update(gcol0, n):
        for c in range(DC):
            t = tmp[nxt('tmp', NR)]
            k.op('dve', [of, cst, rstd], [t], lambda e, c=c, t=t: e.scalar_tensor_tensor(
                t.t[:, :n], of.t[:, c, :n], _col(cst.t, gcol0 + c), rstd.t[:, :n], ALU.mult, ALU.mult))
            k.op('pool', [t, xs], [xs], lambda e, c=c, t=t: e.tensor_tensor(xs.t[:, c, :n], xs.t[:, c, :n], t.t[:, :n], ALU.add))

    def norm_mod(acol0, bcol0, n):
        for c in range(DC):
            sumsq_chunk(xs, xs.t[:, c, :n], c, n)
        make_rstd(n)
        for c in range(DC):
            t = tmp[nxt('tmp', NR)]
            k.op('dve', [xs, cst, rstd], [t], lambda e, c=c, t=t: e.scalar_tensor_tensor(
                t.t[:, :n], xs.t[:, c, :n], _col(cst.t, acol0 + c), rstd.t[:, :n], ALU.mult, ALU.mult))
            k.op('act', [t, cst], [hb], lambda e, c=c, t=t: e.activation(
                hb.t[:, c, :n], t.t[:, :n], AF.Identity, bias=_col(cst.t, bcol0 + c), scale=1.0))

    segs = []
    if do_C and dbg < 2:
        segs.append(('halo', 0, NH, 0, -1))
    for i in range(ntile):
        segs.append(('lat', NH + i * TT, TT, 0, i))
    segs.append(('ctx', NH + NT, CT, 1, -1))

    for kind, c0, n, m, ti in segs:
        if not do_C:
            k.dma('sp', xs, xs.t[:, :, :n], None, xT[:, :, c0:c0 + n])
        if do_C:
            k.dma('sp', hb, hb.t[:, :, :n], None, yT[:, :, c0:c0 + n])
            wt = wload(wf_s, 0, 4 * 512)
            for mj in range(4 if dbg not in (3,) else 0):
                p = pm[nxt('pm', 2)]
                k.mm(p, p.t[:, :n], [(wt.t[:, kc * 512 + mj * 128:kc * 512 + (mj + 1) * 128], hb.t[:, 8 + kc, :n]) for kc in range(4)], [wt, hb])
                k.op('act', [p], [zb], lambda e, p=p, mj=mj: e.activation(zb.t[:, mj, :n], p.t[:, :n], AF.Identity))
            pend = None
            for g in range(8 if dbg not in (3, 4) else 0):
                wt = wload(wo_s, g, DC * 256)
                for j in range(2):
                    mi = 2 * g + j
                    p = pm[nxt('pm', 2)]
                    prs = []
                    for kc in range(DC):
                        rhs = zb.t[:, kc - 8, :n] if 8 <= kc < 12 else hb.t[:, kc, :n]
                        prs.append((wt.t[:, kc * 256 + j * 128:kc * 256 + (j + 1) * 128], rhs))
                    k.mm(p, p.t[:, :n], prs, [wt, hb, zb])
                    k.op('dve', [p], [of], lambda e, p=p, mi=mi: e.tensor_copy(of.t[:, mi, :n], p.t[:, :n]))
                    sq_now = sumsq_sq(of, of.t[:, mi, :n], n)
                    if pend is not None:
                        sumsq_mm(pend[0], pend[1], n)
                    pend = (sq_now, mi)
            if pend is not None:
                sumsq_mm(pend[0], pend[1], n)
            k.dma('sp', xs, xs.t[:, :, :n], None, xT[:, :, c0:c0 + n])
            if dbg not in (3, 4, 5):
                make_rstd(n)
                resid_update(cG1 + m * DC, n)
            if dbg == 0:
                norm_mod(cA2 + m * DC, cB2 + m * DC, n)
            for mi in range(FC if dbg == 0 else 0):
                wt = wload(wu_s, mi, DC * 256)
                pgi = nxt('pg', 2)
                pgt, pvt = pg[pgi], pv[pgi]
                k.mm(pgt, pgt.t[:, :n], [(wt.t[:, kc * 256:kc * 256 + 128], hb.t[:, kc, :n]) for kc in range(DC)], [wt, hb])
                if kind == 'halo':
                    k.op('dve', [pgt, prm], [ghalo], lambda e, pgt=pgt, mi=mi: e.tensor_tensor(
                        ghalo.t[:, mi, :], pgt.t[:, :n], prm.t[:, o_hm:o_hm + NH], ALU.mult))
                    continue
                k.mm(pvt, pvt.t[:, :n], [(wt.t[:, kc * 256 + 128:kc * 256 + 256], hb.t[:, kc, :n]) for kc in range(DC)], [wt, hb])
                gi = nxt('g', NR)
                gb, ca, cbb, ge = gbuf[gi], cva[gi], cvb[gi], geb[gi]
                k.op('act', [pgt], [gb], lambda e, gb=gb, pgt=pgt: e.activation(gb.t[:, 1:n + 1], pgt.t[:, :n], AF.Identity))
                if kind == 'lat':
                    k.op('pool', [ghalo], [gb], lambda e, gb=gb, mi=mi: e.tensor_copy(gb.t[:, 0:n + 2:n + 1], ghalo.t[:, mi, 2 * ti:2 * ti + 2]))
                else:
                    k.op('pool', [], [gb], lambda e, gb=gb: e.memset(gb.t[:, 0:n + 2:n + 1], 0.0))
                k.op('dve', [gb, prm], [ca], lambda e, gb=gb, ca=ca, mi=mi: e.tensor_scalar(
                    ca.t[:, :n], gb.t[:, 0:n], _col(prm.t, o_cw + mi), _col(prm.t, o_cb + mi), ALU.mult, ALU.add))
                k.op('dve', [gb, prm, ca], [cbb], lambda e, gb=gb, ca=ca, cbb=cbb, mi=mi: e.scalar_tensor_tensor(
                    cbb.t[:, :n], gb.t[:, 1:n + 1], _col(prm.t, o_cw + FC + mi), ca.t[:, :n], ALU.mult, ALU.add))
                k.op('dve', [gb, prm, cbb], [ca], lambda e, gb=gb, ca=ca, cbb=cbb, mi=mi: e.scalar_tensor_tensor(
                    ca.t[:, :n], gb.t[:, 2:n + 2], _col(prm.t, o_cw + 2 * FC + mi), cbb.t[:, :n], ALU.mult, ALU.add))
                k.op('act', [ca], [ge], lambda e, ca=ca, ge=ge: e.activation(ge.t[:, :n], ca.t[:, :n], AF.Gelu))
                k.op('dve', [ge, pvt], [ab], lambda e, ge=ge, pvt=pvt, mi=mi: e.tensor_tensor(ab.t[:, mi, :n], ge.t[:, :n], pvt.t[:, :n], ALU.mult))
            if kind == 'halo':
                continue
            pend = None
            for g in range(DC if dbg == 0 else 0):
                wt = wload(wd_s, g, FC * 128)
                p = pm[nxt('pm', 2)]
                k.mm(p, p.t[:, :n], [(wt.t[:, mi * 128:(mi + 1) * 128], ab.t[:, mi, :n]) for mi in range(FC)], [wt, ab])
                k.op('dve', [p], [of], lambda e, p=p, g=g: e.tensor_copy(of.t[:, g, :n], p.t[:, :n]))
                sq_now = sumsq_sq(of, of.t[:, g, :n], n)
                if pend is not None:
                    sumsq_mm(pend[0], pend[1], n)
                pend = (sq_now, g)
            if pend is not None:
                sumsq_mm(pend[0], pend[1], n)
            if dbg == 0:
                make_rstd(n)
                resid_update(cG2 + m * DC, n)
            k.dma('pool', None, xo[:, :, c0 - NH:c0 - NH + n], xs, xs.t[:, :, :n], final=True)
        if do_A:
            norm_mod(cA1 + m * DC, cB1 + m * DC, n)
            for g in range(14):
                wt = wload(wi_s, g, DC * 256)
                for j in range(2):
                    mi = 2 * g + j
                    p = pm[nxt('pm', 2)]
                    k.mm(p, p.t[:, :n], [(wt.t[:, kc * 256 + j * 128:kc * 256 + (j + 1) * 128], hb.t[:, kc, :n]) for kc in range(DC)], [wt, hb])
                    pb = pbo[nxt('pb', NR)]
                    k.op('act', [p], [pb], lambda e, p=p, pb=pb: e.activation(pb.t[:, :n], p.t[:, :n], AF.Identity))
                    k.dma('pool', None, po[mi * 128:(mi + 1) * 128, c0 - NH:c0 - NH + n], pb, pb.t[:, :n], final=True)
    k.finish()
    return nc


def blk(W, cols):
    Kd, M = W.shape
    KC, G = Kd // 128, M // cols
    return np.ascontiguousarray(W.reshape(KC, 128, G, cols).transpose(2, 1, 0, 3).reshape(G, 128, KC * cols))


def blk_up(W):
    Wg = W[:, :DFF].reshape(DC, 128, FC, 1, 128)
    Wv = W[:, DFF:].reshape(DC, 128, FC, 1, 128)
    Wc = np.concatenate([Wg, Wv], axis=3)
    return np.ascontiguousarray(Wc.transpose(2, 1, 0, 3, 4).reshape(FC, 128, DC * 256))


def colvec(v):
    return np.ascontiguousarray(v.reshape(-1, 128).T)


def cols(*vs):
    return np.ascontiguousarray(np.concatenate([colvec(v) for v in vs], axis=1).astype(np.float32))


def build_M(L, CT=256, TT=512):
    nc = bass.Bass("TRN2", target_bir_lowering=False)
    k = K(nc)
    SEQ = [('c', CT), ('l', L)]

    def din(name, shape, dt=F32):
        return nc.dram_tensor(name, shape, dt, kind="ExternalInput").ap()

    I = {}
    for s, Ls in SEQ:
        I['ux' + s] = din('ux' + s, [128, Ls + 3], BF16)
        I['ug' + s] = din('ug' + s, [128, Ls], BF16)
        I['up' + s] = din('up' + s, [128, Ls + 16], BF16)
        I['uf' + s] = din('uf' + s, [128, Ls], BF16)
        I['q' + s] = din('q' + s, [128, Ls], BF16)
        I['k' + s] = din('k' + s, [128, Ls], BF16)
        I['v' + s] = din('v' + s, [Ls, 128], BF16)
        I['icnt' + s] = din('icnt' + s, [128, Ls])
        L1 = Ls // 128
        I['F1' + s] = din('F1' + s, [L1, 4 * L1])
        I['Tw' + s] = din('Tw' + s, [128, 2 * L1])
        I['yo' + s] = nc.dram_tensor('yo' + s, [4, 128, Ls], BF16, kind="ExternalOutput").ap()
    lruw_in = din('lruw', [128, 512]); lrup_in = din('lrup', [128, 11])
    poolw_in = din('poolw', [128, 128]); poolp_in = din('poolp', [128, 5])
    f128_in = din('f128', [128, 384]); rm_in = din('rm', [128, 128])
    cs_in = din('cossin', [2, 128, L]); dlam_in = din('dlam', [128, 256]); attp_in = din('attp', [128, 3])

    wq = k.sb('wq', [128, 512 + 128 + 384 + 128], BF16)
    oLW, oPW, oF, oRM = 0, 512, 640, 1024
    k.dma('pool', wq, wq.t[:, oLW:oLW + 512], None, lruw_in)
    k.dma('pool', wq, wq.t[:, oPW:oPW + 128], None, poolw_in)
    k.dma('pool', wq, wq.t[:, oF:oF + 384], None, f128_in)
    k.dma('pool', wq, wq.t[:, oRM:oRM + 128], None, rm_in)
    pp = k.sb('pp', [128, 11 + 5 + 3 + 8], F32)
    oLP, oPP, oAP, oDV = 0, 11, 16, 19
    k.dma('sp', pp, pp.t[:, oLP:oLP + 11], None, lrup_in)
    k.dma('sp', pp, pp.t[:, oPP:oPP + 5], None, poolp_in)
    k.dma('sp', pp, pp.t[:, oAP:oAP + 3], None, attp_in)
    ones = k.sb("ones", [128, 128], BF16)
    k.op('dve', [], [ones], lambda e: e.memset(ones.t[:], 1.0))
    onec = k.sb("onec", [128, 2], F32)
    k.op('dve', [], [onec], lambda e: e.memset(onec.t[:, 0:1], 1.0))
    k.op('dve', [onec], [onec], lambda e: e.memset(onec.t[:, 1:2], EPS))
    sc0 = k.sb('sc0', [128, 256], F32)
    k.op('act', [pp], [sc0], lambda e: e.activation(sc0.t[:, 0:2], pp.t[:, oLP + 9:oLP + 11], AF.Exp, scale=-1.0))
    k.op('act', [sc0, onec], [sc0], lambda e: e.activation(sc0.t[:, 2:4], sc0.t[:, 0:2], AF.Ln, bias=onec.t[:, 0:1], scale=1.0))
    k.op('dve', [sc0], [pp], lambda e: e.tensor_scalar(pp.t[:, oDV:oDV + 2], sc0.t[:, 2:4], -8.0, None, ALU.mult))
    k.op('dve', [sc0], [pp], lambda e: e.tensor_scalar(pp.t[:, oDV + 2:oDV + 4], sc0.t[:, 2:4], -16.0, None, ALU.mult))
    dl = k.sb('dl', [128, 256], F32)
    k.dma('sp', dl, dl.t[:, :], None, dlam_in)
    k.op('dve', [dl], [sc0], lambda e: e.tensor_tensor(sc0.t[:, 0:64], dl.t[:, 0:64], dl.t[:, 64:128], ALU.mult))
    k.op('dve', [dl, sc0], [sc0], lambda e: e.tensor_tensor(sc0.t[:, 64:128], dl.t[:, 128:192], dl.t[:, 192:256], ALU.mult))
    k.op('dve', [sc0], [sc0], lambda e: e.reduce_sum(sc0.t[:, 128:130], sc0.t[:, 0:128].rearrange("p (a b) -> p a b", a=2), mybir.AxisListType.X))
    k.op('act', [sc0], [sc0], lambda e: e.activation(sc0.t[:, 130:132], sc0.t[:, 128:130], AF.Exp))
    k.op('dve', [sc0], [sc0], lambda e: e.tensor_tensor(sc0.t[:, 132:133], sc0.t[:, 131:132], sc0.t[:, 130:131], ALU.subtract))
    k.op('dve', [sc0, pp], [pp], lambda e: e.tensor_tensor(pp.t[:, oDV + 4:oDV + 5], sc0.t[:, 132:133], pp.t[:, oAP + 1:oAP + 2], ALU.subtract))

    k.op('dve', [pp], [pp], lambda e: e.tensor_tensor(pp.t[:, oDV + 5:oDV + 6], pp.t[:, oAP:oAP + 1], pp.t[:, oAP + 2:oAP + 3], ALU.mult))
    NR = 3
    rot = {}

    def nxt(key, n=NR):
        v = rot.get(key, 0)
        rot[key] = v + 1
        return v % n

    in_att = [False]
    ps = [k.ps("ps%d" % i, [128, 512]) if i in (0, 1, 4, 5) else None for i in range(8)]
    f32t = [k.sb("f32t%d" % i, [128, TT + 16], F32) for i in range(20)]
    bft = [k.sb("bft%d" % i, [128, TT + 16], BF16) for i in range(12)]
    outb = [k.sb("outb%d" % i, [128, TT], BF16) for i in range(NR)]

    def F():
        return f32t[nxt('f', 20)]

    def B():
        return bft[nxt('b', 12)]

    def P():
        if in_att[0]:
            return ps[nxt('p', 2)]
        return ps[(0, 1, 4, 5)[nxt('p4', 4)]]

    def store(y_ap, src_tk, src_ap):
        k.dma('pool', None, y_ap, src_tk, src_ap, final=True)

    hstate = k.sb('hstate', [128, 4], F32)
    k.push_scope()
    hf_all = k.sb('hf_all', [128, L], F32)

    def lru(s, Ls):
        yo = I['yo' + s]
        nt = (Ls + TT - 1) // TT
        tiles = [(i * TT, min(TT, Ls - i * TT)) for i in range(nt)]

        def gates(t0, n, d):
            xt = B()
            k.dma('sp', xt, xt.t[:, :n + 3], None, I['ux' + s][:, t0:t0 + n + 3])
            u = F(); u2 = F()
            k.op('dve', [xt, pp], [u], lambda e: e.tensor_scalar(u.t[:, :n], xt.t[:, 0:n], _col(pp.t, oLP + 0), _col(pp.t, oLP + 4), ALU.mult, ALU.add))
            k.op('dve', [xt, pp, u], [u2], lambda e: e.scalar_tensor_tensor(u2.t[:, :n], xt.t[:, 1:n + 1], _col(pp.t, oLP + 1), u.t[:, :n], ALU.mult, ALU.add))
            k.op('dve', [xt, pp, u2], [u], lambda e: e.scalar_tensor_tensor(u.t[:, :n], xt.t[:, 2:n + 2], _col(pp.t, oLP + 2), u2.t[:, :n], ALU.mult, ALU.add))
            k.op('dve', [xt, pp, u], [u2], lambda e: e.scalar_tensor_tensor(u2.t[:, :n], xt.t[:, 3:n + 3], _col(pp.t, oLP + 3), u.t[:, :n], ALU.mult, ALU.add))
            ub = B()
            k.op('act', [u2], [ub], lambda e: e.activation(ub.t[:, :n], u2.t[:, :n], AF.Identity))
            pr, pi = P(), P()
            k.mm(pr, pr.t[:, :n], [(wq.t[:, oLW + (2 * d) * 128:oLW + (2 * d + 1) * 128], ub.t[:, :n])], [wq, ub])
            k.mm(pi, pi.t[:, :n], [(wq.t[:, oLW + (2 * d + 1) * 128:oLW + (2 * d + 2) * 128], ub.t[:, :n])], [wq, ub])
            r = F(); ig = F(); a = F(); sq = F(); bb = F()
            k.op('act', [pr, pp], [r], lambda e: e.activation(r.t[:, :n], pr.t[:, :n], AF.Sigmoid, bias=_col(pp.t, oLP + 5 + d), scale=1.0))
            k.op('act', [pi, pp], [ig], lambda e: e.activation(ig.t[:, :n], pi.t[:, :n], AF.Sigmoid, bias=_col(pp.t, oLP + 7 + d), scale=1.0))
            k.op('act', [r, pp], [a], lambda e: e.activation(a.t[:, :n], r.t[:, :n], AF.Exp, scale=_col(pp.t, oDV + d)))
            k.op('act', [r, pp], [sq], lambda e: e.activation(sq.t[:, :n], r.t[:, :n], AF.Exp, scale=_col(pp.t, oDV + 2 + d)))
            k.op('act', [sq, onec], [sq], lambda e: e.activation(sq.t[:, :n], sq.t[:, :n], AF.Sqrt, bias=onec.t[:, 0:1], scale=-1.0))
            k.op('dve', [sq, ig], [bb], lambda e: e.tensor_tensor(bb.t[:, :n], sq.t[:, :n], ig.t[:, :n], ALU.mult))
            k.op('dve', [bb, u2], [bb], lambda e: e.tensor_tensor(bb.t[:, :n], bb.t[:, :n], u2.t[:, :n], ALU.mult))
            return a, bb

        for ti, (t0, n) in enumerate(tiles):
            a, bb = gates(t0, n, 0)
            if ti == 0:
                if s == 'c':
                    k.op('dve', [a, bb], [hf_all], lambda e: e.tensor_tensor_scan(hf_all.t[:, t0:t0 + n], a.t[:, :n], bb.t[:, :n], 0.0, ALU.mult, ALU.add))
                else:
                    k.op('dve', [a, bb, hstate], [hf_all], lambda e: e.tensor_tensor_scan(hf_all.t[:, t0:t0 + n], a.t[:, :n], bb.t[:, :n], hstate.t[:, 0:1], ALU.mult, ALU.add))
            else:
                k.op('dve', [a, bb, hf_all], [hf_all], lambda e: e.tensor_tensor_scan(hf_all.t[:, t0:t0 + n], a.t[:, :n], bb.t[:, :n], hf_all.t[:, t0 - 1:t0], ALU.mult, ALU.add))
        if s == 'c':
            k.op('dve', [hf_all], [hstate], lambda e: e.tensor_copy(hstate.t[:, 0:1], hf_all.t[:, Ls - 1:Ls]))
        prev = None
        for ti in range(nt - 1, -1, -1):
            t0, n = tiles[ti]
            a, bb = gates(t0, n, 1)
            hb_ = F()
            if prev is None:
                init = 0.0 if s == 'c' else hstate.t[:, 1:2]
                rd = [a, bb] if s == 'c' else [a, bb, hstate]
            else:
                init = prev.t[:, 0:1]
                rd = [a, bb, prev]
            k.op('dve', rd, [hb_], lambda e: e.tensor_tensor_scan(hb_.t[:, n - 1::-1] if False else hb_.t[:, 0:n][:, ::-1], a.t[:, 0:n][:, ::-1], bb.t[:, 0:n][:, ::-1], init, ALU.mult, ALU.add))
            prev = hb_
            gt = B(); gg = F(); hs = F(); ob = outb[nxt('o')]
            k.dma('sp', gt, gt.t[:, :n], None, I['ug' + s][:, t0:t0 + n])
            k.op('act', [gt], [gg], lambda e: e.activation(gg.t[:, :n], gt.t[:, :n], AF.Gelu))
            k.op('pool', [hb_, hf_all], [hs], lambda e: e.tensor_tensor(hs.t[:, :n], hb_.t[:, :n], hf_all.t[:, t0:t0 + n], ALU.add))
            k.op('dve', [hs, gg], [ob], lambda e: e.tensor_tensor(ob.t[:, :n], hs.t[:, :n], gg.t[:, :n], ALU.mult))
            store(yo[0, :, t0:t0 + n], ob, ob.t[:, :n])
        if s == 'c':
            k.op('dve', [prev], [hstate], lambda e: e.tensor_copy(hstate.t[:, 1:2], prev.t[:, 0:1]))

    def pool(s, Ls):
        yo = I['yo' + s]
        for t0 in range(0, Ls, TT):
            n = min(TT, Ls - t0)
            ut = B(); ic = F()
            k.dma('sp', ut, ut.t[:, :n + 16], None, I['up' + s][:, t0:t0 + n + 16])
            k.dma('sp', ic, ic.t[:, :n], None, I['icnt' + s][:, t0:t0 + n])
            w1 = F(); w2 = F(); w4 = F(); w8 = F(); ws = F(); ws2 = F()
            m = n + 16
            k.op('pool', [ut], [w1], lambda e: e.tensor_tensor(w1.t[:, 1:m], ut.t[:, 0:m - 1], ut.t[:, 1:m], ALU.add))
            k.op('pool', [w1], [w2], lambda e: e.tensor_tensor(w2.t[:, 2:m - 1], w1.t[:, 1:m - 2], w1.t[:, 3:m], ALU.add))
            k.op('pool', [w2], [w4], lambda e: e.tensor_tensor(w4.t[:, 4:m - 3], w2.t[:, 2:m - 5], w2.t[:, 6:m - 1], ALU.add))
            k.op('pool', [w4], [w8], lambda e: e.tensor_tensor(w8.t[:, 8:m - 7], w4.t[:, 4:m - 11], w4.t[:, 12:m - 3], ALU.add))
            k.op('dve', [w1, pp], [ws], lambda e: e.tensor_scalar(ws.t[:, :n], w1.t[:, 8:8 + n], _col(pp.t, oPP + 1), None, ALU.mult))
            k.op('dve', [w2, pp, ws], [ws2], lambda e: e.scalar_tensor_tensor(ws2.t[:, :n], w2.t[:, 8:8 + n], _col(pp.t, oPP + 2), ws.t[:, :n], ALU.mult, ALU.add))
            k.op('dve', [w4, pp, ws2], [ws], lambda e: e.scalar_tensor_tensor(ws.t[:, :n], w4.t[:, 8:8 + n], _col(pp.t, oPP + 3), ws2.t[:, :n], ALU.mult, ALU.add))
            k.op('dve', [w8, pp, ws], [ws2], lambda e: e.scalar_tensor_tensor(ws2.t[:, :n], w8.t[:, 8:8 + n], _col(pp.t, oPP + 4), ws.t[:, :n], ALU.mult, ALU.add))
            k.op('dve', [ws2, ic], [ws], lambda e: e.tensor_tensor(ws.t[:, :n], ws2.t[:, :n], ic.t[:, :n], ALU.mult))
            db = B()
            k.op('dve', [ws, ut], [db], lambda e: e.tensor_tensor(db.t[:, :n], ws.t[:, :n], ut.t[:, 8:8 + n], ALU.subtract))
            p = P(); ob = outb[nxt('o')]
            k.mm(p, p.t[:, :n], [(wq.t[:, oPW:oPW + 128], db.t[:, :n])], [wq, db])
            k.op('act', [p, pp], [ob], lambda e: e.activation(ob.t[:, :n], p.t[:, :n], AF.Identity, scale=_col(pp.t, oPP + 0)))
            store(yo[1, :, t0:t0 + n], ob, ob.t[:, :n])

    for s, Ls in SEQ:
        lru(s, Ls)
    k.pop_scope()
    for s, Ls in SEQ:
        pool(s, Ls)

    def fft(s, Ls):
        yo = I['yo' + s]
        L1 = Ls // 128
        scale = 1.0 / math.sqrt(Ls * 128.0)
        k.push_scope()
        ufs = k.sb('ufs' + s, [128, Ls], BF16)
        k.dma('sp', ufs, ufs.t[:, :], None, I['uf' + s])
        f1 = k.sb('f1' + s, [128, 4 * L1], BF16)
        k.dma('pool', f1, f1.t[0:L1, :], None, I['F1' + s])
        tw = k.sb('tw' + s, [128, 2 * L1], F32)
        k.dma('sp', tw, tw.t[:, :], None, I['Tw' + s])
        Ab = k.sb('Ab' + s, [128, 2 * 64 * 128], BF16)
        Bp = k.sb('Bp' + s, [128, 2 * 64 * L1], BF16)
        Yh = k.sb('Yh' + s, [128, 128 * L1], BF16)
        Av = Ab.t[:, :].rearrange("p (r c l) -> p r c l", r=2, c=64)
        Bv = Bp.t[:, :].rearrange("p (r c l) -> p r c l", r=2, c=64)
        Yv = Yh.t[:, :].rearrange("p (a b) -> p a b", b=L1)
        for h in range(2):
            for l2 in range(128):
                p = P()
                k.mm(p, p.t[0:L1, 0:64], [(ufs.t[:, l2 * L1:(l2 + 1) * L1], wq.t[:, oF + h * 64:oF + h * 64 + 64])], [ufs, wq])
                k.mm(p, p.t[0:L1, 64:128], [(ufs.t[:, l2 * L1:(l2 + 1) * L1], wq.t[:, oF + 128 + h * 64:oF + 128 + h * 64 + 64])], [ufs, wq])
                k.op('act' if l2 % 2 else 'dve', [p], [Ab], (lambda e, p=p, l2=l2: e.activation(Av[0:L1, :, :, l2], p.t[0:L1, 0:128].rearrange("p (r c) -> p r c", r=2), AF.Identity)) if l2 % 2 else
                     (lambda e, p=p, l2=l2: e.tensor_copy(Av[0:L1, :, :, l2], p.t[0:L1, 0:128].rearrange("p (r c) -> p r c", r=2))))
            for c in range(64):
                p = P()
                k.mm(p, p.t[:, 0:2 * L1], [(Av[0:L1, 0, c, :], f1.t[0:L1, 0:2 * L1]), (Av[0:L1, 1, c, :], f1.t[0:L1, 2 * L1:4 * L1])], [Ab, f1])
                br, bi = p.t[:, 0:L1], p.t[:, L1:2 * L1]
                tc, ts = tw.t[:, 0:L1], tw.t[:, L1:2 * L1]
                t1 = F(); t2 = F(); t3 = F(); t4 = F()
                k.op('dve', [p, tw], [t1], lambda e, t1=t1, br=br, tc=tc: e.tensor_tensor(t1.t[:, :L1], br, tc, ALU.mult))
                k.op('dve', [p, tw], [t2], lambda e, t2=t2, bi=bi, ts=ts: e.tensor_tensor(t2.t[:, :L1], bi, ts, ALU.mult))
                k.op('dve', [p, tw], [t3], lambda e, t3=t3, bi=bi, tc=tc: e.tensor_tensor(t3.t[:, :L1], bi, tc, ALU.mult))
                k.op('dve', [p, tw], [t4], lambda e, t4=t4, br=br, ts=ts: e.tensor_tensor(t4.t[:, :L1], br, ts, ALU.mult))
                k.op('pool', [t1, t2], [Bp], lambda e, t1=t1, t2=t2, c=c: e.tensor_tensor(Bv[:, 0, c, :], t1.t[:, :L1], t2.t[:, :L1], ALU.add))
                k.op('pool', [t3, t4], [Bp], lambda e, t3=t3, t4=t4, c=c: e.tensor_tensor(Bv[:, 1, c, :], t3.t[:, :L1], t4.t[:, :L1], ALU.subtract))
            for l1 in range(L1):
                p = P()
                k.mm(p, p.t[0:64, 0:128], [(Bv[:, 0, :, l1], wq.t[:, oF:oF + 128]), (Bv[:, 1, :, l1], wq.t[:, oF + 256:oF + 384])], [Bp, wq])
                k.op('act', [p], [Yh], lambda e, p=p, l1=l1: e.activation(Yv[h * 64:h * 64 + 64, :, l1], p.t[0:64, 0:128], AF.Identity, scale=scale))
        store(yo[2, :, :].rearrange("p (a b) -> p a b", b=L1), Yh, Yv)
        k.pop_scope()

    for s, Ls in SEQ:
        fft(s, Ls)

    k.push_scope()
    LK = CT + L
    KT = k.sb('KT', [128, LK], BF16)
    VV = k.sb('VV', [128, LK], BF16)
    QT = k.sb('QT', [128, L], BF16)
    QC = k.sb('QC', [128, CT], BF16)
    k.dma('sp', KT, KT.t[:, 0:CT], None, I['kc'])
    k.dma('sp', QC, QC.t[:, :], None, I['qc'])
    Vv = VV.t[:, :].rearrange("p (c e) -> p c e", e=128)
    k.dma('sp', VV, Vv[:, 0:CT // 128, :], None, I['vc'].rearrange("(c p) e -> p c e", p=128))
    k.dma('sp', VV, Vv[:, CT // 128:, :], None, I['vl'].rearrange("(c p) e -> p c e", p=128))
    for nm, dst, off in (('ql', QT, 0), ('kl', KT, CT)):
        for t0 in range(0, L, TT):
            n = min(TT, L - t0)
            xt = B(); ct = F(); st = F()
            k.dma('sp', xt, xt.t[:, :n], None, I[nm][:, t0:t0 + n])
            k.dma('sp', ct, ct.t[:, :n], None, cs_in[0, :, t0:t0 + n])
            k.dma('sp', st, st.t[:, :n], None, cs_in[1, :, t0:t0 + n])
            p = P()
            k.mm(p, p.t[:, :n], [(wq.t[:, oRM:oRM + 128], xt.t[:, :n])], [wq, xt])
            t1 = F(); t2 = F()
            k.op('dve', [xt, ct], [t1], lambda e, t1=t1, xt=xt, ct=ct: e.tensor_tensor(t1.t[:, :n], xt.t[:, :n], ct.t[:, :n], ALU.mult))
            k.op('dve', [p, st], [t2], lambda e, t2=t2, p=p, st=st: e.tensor_tensor(t2.t[:, :n], p.t[:, :n], st.t[:, :n], ALU.mult))
            k.op('pool', [t1, t2], [dst], lambda e, t1=t1, t2=t2, dst=dst, a=off + t0: e.tensor_tensor(dst.t[:, a:a + n], t1.t[:, :n], t2.t[:, :n], ALU.add))
    pT = [k.sb('pT%d' % i, [128, 2 * TT], BF16) for i in range(4)]
    pS = [k.ps("pS%d" % i, [128, 2 * TT]) for i in range(2)]
    zacc = [k.sb('zacc%d' % i, [128, TT], F32) for i in range(2)]
    onef = k.sb('onef', [128, 128], F32)
    k.op('dve', [], [onef], lambda e: e.memset(onef.t[:], 1.0))

    sel = k.sb('sel', [64, 256], F32)
    k.op('dve', [], [sel], lambda e: e.memset(sel.t[:, :], 0.0))
    k.op('dve', [sel], [sel], lambda e: e.memset(sel.t[0:1, 0:128], 1.0))
    k.op('dve', [sel], [sel], lambda e: e.memset(sel.t[32:33, 128:256], 1.0))

    def attend(s, Ls, Qtile, nk):
        yo = I['yo' + s]
        nch = nk // 128
        for q0 in range(0, Ls, TT):
            n = min(TT, Ls - q0)
            po0, po1, pz0, pz1 = ps[4], ps[5], ps[0], ps[1]
            pts = {}

            def emit_S(c):
                pt = pT[nxt('pt', 4)]
                p = pS[nxt('pS', 2)]
                for j in range(2):
                    k.mm(p, p.t[:, j * TT:j * TT + n], [(KT.t[64 * j:64 * j + 64, c * 128:(c + 1) * 128], Qtile.t[64 * j:64 * j + 64, q0:q0 + n])], [KT, Qtile])
                if n == TT:
                    k.op('act', [p], [pt], lambda e, p=p, pt=pt: e.activation(pt.t[:, :], p.t[:, :], AF.Exp, scale=0.125))
                else:
                    for j in range(2):
                        k.op('act', [p], [pt], lambda e, p=p, pt=pt, j=j: e.activation(pt.t[:, j * TT:j * TT + n], p.t[:, j * TT:j * TT + n], AF.Exp, scale=0.125))
                pts[c] = pt

            def emit_PV(c):
                pt = pts.pop(c)
                tgts = [(po0, po0.t[:, :n], Vv[:, c, :], 0), (po1, po1.t[:, :n], Vv[:, c, :], 1),
                        (pz0, pz0.t[0:32, :n], ones.t[:, 0:32], 0), (pz0, pz0.t[32:64, :n], ones.t[:, 0:32], 1)]
                for tgt, oap, lhs, j in tgts:
                    k._wait('pe', k._deps([pt, VV, ones], [tgt]))
                    ins = nc.tensor.matmul(oap, lhs, pt.t[:, j * TT:j * TT + n], start=(c == 0), stop=(c == nch - 1))
                    k.cnt['pe'] += 1
                    ins.then_inc(k.sem['pe'], 1)
                    k._mark((k.sem['pe'], k.cnt['pe']), [pt, VV, ones], [tgt])

            emit_S(0)
            for c in range(nch):
                if c + 1 < nch:
                    emit_S(c + 1)
                emit_PV(c)
            rzs = F(); r0 = F(); r1 = F(); o0 = F(); o1 = F(); oo = F()
            k.op('dve', [pz0], [rzs], lambda e: e.reciprocal(rzs.t[0:64, :n], pz0.t[0:64, :n]))
            for j, rj in enumerate((r0, r1)):
                pb = P()
                k.mm(pb, pb.t[:, :n], [(sel.t[0:64, j * 128:(j + 1) * 128], rzs.t[0:64, :n])], [sel, rzs])
                k.op('act', [pb], [rj], lambda e, pb=pb, rj=rj: e.activation(rj.t[:, :n], pb.t[:, :n], AF.Identity))
            k.op('dve', [po0, r0], [o0], lambda e: e.tensor_tensor(o0.t[:, :n], po0.t[:, :n], r0.t[:, :n], ALU.mult))
            k.op('dve', [po1, r1], [o1], lambda e: e.tensor_tensor(o1.t[:, :n], po1.t[:, :n], r1.t[:, :n], ALU.mult))
            k.op('dve', [o0, o1, pp], [oo], lambda e: e.scalar_tensor_tensor(oo.t[:, :n], o1.t[:, :n], _col(pp.t, oDV + 4), o0.t[:, :n], ALU.mult, ALU.add))
            sq = B(); p = P(); rs = F(); t3 = F(); ob = outb[nxt('o')]
            k.op('act', [oo], [sq], lambda e: e.activation(sq.t[:, :n], oo.t[:, :n], AF.Square))
            k.mm(p, p.t[:, :n], [(ones.t[:, :], sq.t[:, :n])], [ones, sq])
            k.op('act', [p, onec], [rs], lambda e: e.activation(rs.t[:, :n], p.t[:, :n], AF.Sqrt, bias=onec.t[:, 1:2], scale=1.0 / 128))
            k.op('dve', [rs], [t3], lambda e: e.reciprocal(t3.t[:, :n], rs.t[:, :n]))
            k.op('dve', [oo, t3, pp], [ob], lambda e: e.scalar_tensor_tensor(ob.t[:, :n], oo.t[:, :n], _col(pp.t, oDV + 5), t3.t[:, :n], ALU.mult, ALU.mult))
            store(yo[3, :, q0:q0 + n], ob, ob.t[:, :n])

    in_att[0] = True
    attend('c', CT, QC, CT)
    attend('l', L, QT, LK)
    k.pop_scope()
    k.finish()
    return nc


POOL_HALF = (1, 2, 4, 8)


def _dft(n):
    i = np.arange(n)
    ang = 2.0 * np.pi * np.outer(i, i) / n
    return np.cos(ang), np.sin(ang)


def m_consts(L, CT, j, lam_init):
    c = {}
    C, S = _dft(128)
    c['f128'] = np.concatenate([C, -S, S], 1).astype(np.float32)
    rm = np.zeros((128, 128), np.float32)
    for d in range(128):
        if d % 32 < 16:
            rm[d + 16, d] = -1.0
        else:
            rm[d - 16, d] = 1.0
    c['rm'] = rm
    t = np.arange(L)
    inv = (10000.0 ** (-np.arange(16, dtype=np.float32) / 16)).astype(np.float32)
    ang_r = (t // 64).astype(np.float32)[:, None] * inv
    ang_c = (t % 64).astype(np.float32)[:, None] * inv
    ang = np.zeros((128, L), np.float32)
    for p in range(128):
        d = p % 64
        ang[p] = ang_r[:, d % 16] if d < 32 else ang_c[:, (d - 32) % 16]
    c['cossin'] = np.stack([np.cos(ang), np.sin(ang)]).astype(np.float32)
    for s, Ls in (('c', CT), ('l', L)):
        L1 = Ls // 128
        C1, S1 = _dft(L1)
        c['F1' + s] = np.concatenate([C1, -S1, S1, C1], 1).astype(np.float32)
        a = 2.0 * np.pi * np.outer(np.arange(128), np.arange(L1)) / Ls
        c['Tw' + s] = np.concatenate([np.cos(a), np.sin(a)], 1).astype(np.float32)
        tt = np.arange(Ls)
        half = POOL_HALF[j]
        lo = np.clip(tt - half, 0, Ls - 1)
        hi = np.clip(tt + half - 1, 0, Ls - 1)
        c['icnt' + s] = np.ascontiguousarray(np.broadcast_to((1.0 / (hi - lo + 1)).astype(np.float32)[None], (128, Ls)))
    sel = np.zeros((128, 4), np.float32)
    sel[:, j] = 1.0
    c['sel'] = sel
    c['lam_init'] = np.full((128, 1), lam_init, np.float32)
    c['one_m_lam_init'] = np.full((128, 1), 1.0 - lam_init, np.float32)
    return c


def m_inputs(pT, pcT, j, L, CT, P, consts):
    inp = {}
    for s, Ls, src in (('c', CT, pcT), ('l', L, pT)):
        sl = lambda base: src[base + 128 * j: base + 128 * (j + 1)]
        z = lambda n: np.zeros((128, n), src.dtype)
        inp['ux' + s] = np.concatenate([z(2), sl(0), z(1)], 1)
        inp['ug' + s] = np.ascontiguousarray(sl(512))
        inp['up' + s] = np.concatenate([z(8), sl(1024), z(8)], 1)
        L1 = Ls // 128
        inp['uf' + s] = np.ascontiguousarray(sl(1536).reshape(128, L1, 128).transpose(0, 2, 1).reshape(128, Ls))
        inp['q' + s] = np.ascontiguousarray(sl(2048))
        inp['k' + s] = np.ascontiguousarray(sl(2560))
        inp['v' + s] = np.ascontiguousarray(sl(3072).T)
        for nm in ('icnt', 'F1', 'Tw'):
            inp[nm + s] = consts[nm + s]
    lw = np.zeros((128, 512), np.float32)
    for d in range(2):
        for gi, W in enumerate((P['lru_wa'], P['lru_wi'])):
            for a in range(2):
                lw[64 * a:64 * a + 64, (2 * d + gi) * 128 + 64 * a:(2 * d + gi) * 128 + 64 * a + 64] = W[d, 2 * j + a]
    inp['lruw'] = lw
    cs = slice(128 * j, 128 * (j + 1))
    inp['lrup'] = np.ascontiguousarray(np.stack(
        [P['lru_conv_w'][0, cs], P['lru_conv_w'][1, cs], P['lru_conv_w'][2, cs], P['lru_conv_w'][3, cs], P['lru_conv_b'][cs],
         P['lru_ba'][0, cs], P['lru_ba'][1, cs], P['lru_bi'][0, cs], P['lru_bi'][1, cs], P['lru_lam'][0, cs], P['lru_lam'][1, cs]], 1).astype(np.float32))
    inp['poolw'] = np.ascontiguousarray(P['pool_w'][j])
    inp['poolp'] = np.ascontiguousarray(np.concatenate([P['pool_scale'][cs][:, None], consts['sel']], 1).astype(np.float32))
    inp['f128'] = consts['f128']
    inp['rm'] = consts['rm']
    inp['cossin'] = consts['cossin']
    inp['dlam'] = np.ascontiguousarray(np.broadcast_to(P['diff_lam'].reshape(1, 256), (128, 256)))
    inp['attp'] = np.ascontiguousarray(np.concatenate([P['diff_subln_g'][:, None], consts['lam_init'], consts['one_m_lam_init']], 1).astype(np.float32))
    return inp


ACOLS = 6 * D // NCORES


def build_ADA(depth=4):
    nc = bass.Bass("TRN2", target_bir_lowering=False)
    k = K(nc)
    cT = nc.dram_tensor("cT", [128, DC * 3], F32, kind="ExternalInput").ap()
    aw = nc.dram_tensor("aw", [depth * DC, 128, ACOLS], F32, kind="ExternalInput").ap()
    ab_in = nc.dram_tensor("ab", [3, depth * ACOLS], F32, kind="ExternalInput").ap()
    mo = nc.dram_tensor("mo", [3, depth * ACOLS], F32, kind="ExternalOutput").ap()
    ct = k.sb("ct", [128, DC * 3], F32)
    sc = k.sb("sc", [128, DC * 3], F32)
    bias = k.sb("bias", [3, depth * ACOLS], F32)
    res = k.sb("res", [3, depth * ACOLS], F32)
    k.dma('sp', ct, ct.t[:, :], None, cT)
    k.dma('sp', bias, bias.t[:, :], None, ab_in)
    k.op('act', [ct], [sc], lambda e: e.activation(sc.t[:, :], ct.t[:, :], AF.Silu))
    wb = [k.sb("awb%d" % i, [128, ACOLS], F32) for i in range(4)]
    pss = [k.ps("aps%d" % i, [128, 512]) for i in range(3)]
    for l in range(depth):
        for kc in range(DC):
            w = wb[(l * DC + kc) % 4]
            k.dma('sp', w, w.t[:, :], None, aw[l * DC + kc])
            for g in range(3):
                k._wait('pe', k._deps([w, sc], [pss[g]]))
                ins = nc.tensor.matmul(pss[g].t[0:3, :], sc.t[:, kc * 3:kc * 3 + 3], w.t[:, g * 512:(g + 1) * 512], start=(kc == 0), stop=(kc == DC - 1))
                k.cnt['pe'] += 1
                ins.then_inc(k.sem['pe'], 1)
                k._mark((k.sem['pe'], k.cnt['pe']), [w, sc], [pss[g]])
        for g in range(3):
            o = l * ACOLS + g * 512
            k.op('dve', [pss[g], bias], [res], lambda e, g=g, o=o: e.tensor_tensor(res.t[0:3, o:o + 512], pss[g].t[0:3, :], bias.t[0:3, o:o + 512], ALU.add))
    k.dma('pool', None, mo, res, res.t[:, :], final=True)
    k.finish()
    return nc


_CACHE = {}


def _prog(key, fn):
    if key not in _CACHE:
        _CACHE[key] = fn()
    return _CACHE[key]


def kernel(x, c, ctx, c_ctx, ada_w, ada_b, norm_g, w_in, lru_conv_w, lru_conv_b, lru_wa, lru_ba, lru_wi, lru_bi,
           lru_lam, pool_w, pool_scale, fourier_w, diff_lam, diff_subln_g, w_out, ffn_w_up, ffn_conv_w, ffn_conv_b,
           ffn_w_down):
    A = lambda a: np.asarray(a)
    x, c, ctx, c_ctx = A(x), A(c), A(ctx), A(c_ctx)
    ada_w, ada_b, norm_g, w_in = A(ada_w), A(ada_b), A(norm_g), A(w_in)
    Bn, L, _ = x.shape
    CT = ctx.shape[1]
    depth = w_in.shape[0]
    NS = NCORES // Bn
    NT = L // NS
    TT = 512
    ntile = NT // TT
    cores = list(range(NCORES))
    PL = [dict(lru_conv_w=A(lru_conv_w)[l], lru_conv_b=A(lru_conv_b)[l], lru_wa=A(lru_wa)[l], lru_ba=A(lru_ba)[l],
               lru_wi=A(lru_wi)[l], lru_bi=A(lru_bi)[l], lru_lam=A(lru_lam)[l], pool_w=A(pool_w)[l],
               pool_scale=A(pool_scale)[l], diff_lam=A(diff_lam)[l], diff_subln_g=A(diff_subln_g)[l]) for l in range(depth)]

    c3 = np.concatenate([c, c_ctx[None]], 0).astype(np.float32)
    cT = np.ascontiguousarray(c3.reshape(3, DC, 128).transpose(2, 1, 0).reshape(128, DC * 3))
    maps = []
    for r in cores:
        cs = slice(r * ACOLS, (r + 1) * ACOLS)
        aw = np.ascontiguousarray(ada_w[:, :, cs].reshape(depth * DC, 128, ACOLS))
        ab = np.ascontiguousarray(np.broadcast_to(ada_b[:, cs].reshape(1, depth * ACOLS), (3, depth * ACOLS)))
        maps.append(dict(cT=cT, aw=aw, ab=ab))
    res = run_bass_kernel_spmd(_prog('ada', lambda: build_ADA(depth)), maps, core_ids=cores)
    mod = np.concatenate([res.results[r]['mo'].reshape(3, depth, ACOLS) for r in cores], axis=2)
    mod = mod.reshape(3, depth, 6, D)
    del maps, res

    xT = [np.ascontiguousarray(x[b].T) for b in range(Bn)]
    xcT = [np.ascontiguousarray(ctx[b].T) for b in range(Bn)]

    def t_launch(lC, lA, yT, ycT):
        do_C, do_A = lC is not None, lA is not None
        NH = 2 * ntile if do_C else 0
        common = {}
        if do_C:
            common.update(ngC=cols(norm_g[lC, 1], norm_g[lC, 2], norm_g[lC, 3]),
                          cw=np.ascontiguousarray(A(ffn_conv_w)[lC].reshape(3, FC, 128).transpose(2, 0, 1).reshape(128, 3 * FC)),
                          cb=colvec(A(ffn_conv_b)[lC]), wf=blk(A(fourier_w)[lC], 512), wo=blk(A(w_out)[lC], 256),
                          wu=blk_up(A(ffn_w_up)[lC]), wd=blk(A(ffn_w_down)[lC], 128))
        if do_A:
            common.update(ngA=cols(norm_g[lA, 0]), wi=blk(w_in[lA], 256))
        maps = []
        for r in cores:
            b, s = divmod(r, NS)
            t0 = s * NT
            m = dict(common)
            hm = np.zeros(max(NH, 1), np.float32)
            hidx = []
            for i in range(ntile if do_C else 0):
                for hi, tpos in enumerate((t0 + i * TT - 1, t0 + (i + 1) * TT)):
                    ok = 0 <= tpos < L
                    hm[2 * i + hi] = 1.0 if ok else 0.0
                    hidx.append(tpos if ok else 0)
            xh = xT[b][:, hidx] if do_C else np.zeros((D, 0), np.float32)
            m['xT'] = np.ascontiguousarray(np.concatenate([xh, xT[b][:, t0:t0 + NT], xcT[b]], 1))
            if do_C:
                m['yT'] = np.ascontiguousarray(np.concatenate([yT[b][:, hidx], yT[b][:, t0:t0 + NT], ycT[b]], 1))
                m['hmask'] = np.ascontiguousarray(np.broadcast_to(hm[None, :NH], (128, NH)))
                m['modC'] = cols(*[mod[row, lC, i] for row in (b, 2) for i in (2, 3, 4, 5)])
            if do_A:
                m['modA'] = cols(*[mod[row, lA, i] for row in (b, 2) for i in (0, 1)])
            maps.append(m)
        res = run_bass_kernel_spmd(_prog(('T', NT, do_C, do_A, CT), lambda: build_T(NT, do_C, do_A, CT, TT)), maps, core_ids=cores)
        pT = pcT = None
        if do_C:
            for b in range(Bn):
                xT[b] = np.ascontiguousarray(np.concatenate([res.results[b * NS + s]['xo'][:, :NT] for s in range(NS)], 1))
                xcT[b] = np.ascontiguousarray(res.results[b * NS]['xo'][:, NT:])
        if do_A:
            pT = [np.concatenate([res.results[b * NS + s]['po'][:, :NT] for s in range(NS)], 1) for b in range(Bn)]
            pcT = [res.results[b * NS]['po'][:, NT:] for b in range(Bn)]
        return pT, pcT

    pT, pcT = t_launch(None, 0, None, None)
    for l in range(depth):
        lam_init = 0.8 - 0.6 * math.exp(-0.3 * l)
        maps = []
        for r in cores:
            b, j = divmod(r, NS)
            maps.append(m_inputs(pT[b], pcT[b], j, L, CT, PL[l], m_consts(L, CT, j, lam_init)))
        res = run_bass_kernel_spmd(_prog(('M', L, CT), lambda: build_M(L, CT, TT)), maps, core_ids=cores)
        yT, ycT = [], []
        for b in range(Bn):
            yl = np.stack([res.results[b * NS + j]['yol'] for j in range(NS)], 1)
            yc = np.stack([res.results[b * NS + j]['yoc'] for j in range(NS)], 1)
            yT.append(yl.reshape(D, L))
            ycT.append(yc.reshape(D, CT))
        del maps, res
        pT, pcT = t_launch(l, l + 1 if l + 1 < depth else None, yT, ycT)
    out = np.stack([xT[b].T for b in range(Bn)], 0).astype(np.float32)
    return np.ascontiguousarray(out)
```

```python
import contextlib
import math
import numpy as np
import ml_dtypes
import concourse.bass as bass
import concourse.mybir as mybir
from concourse.bass_utils import run_bass_kernel_spmd

F32 = mybir.dt.float32
BF16 = mybir.dt.bfloat16
AF = mybir.ActivationFunctionType
ALU = mybir.AluOpType
NPBF = ml_dtypes.bfloat16

D = 2048
DC = D // 128
DFF = 5632
FC = DFF // 128
DIN = 3584
EPS = 1e-6
NCORES = 8


class Tk:
    def __init__(self, t, name):
        self.t = t
        self.name = name
        self.w = None
        self.r = {}
        self.dsem = {}
        self.dcnt = {}
        self.psum = False

    def __getitem__(self, idx):
        return self.t[idx]


class K:
    def __init__(self, nc):
        self.nc = nc
        self.es = contextlib.ExitStack()
        self.root = self.es
        self.E = {'pe': nc.tensor, 'act': nc.scalar, 'dve': nc.vector, 'pool': nc.gpsimd, 'sp': nc.sync}
        self.sem = {e: self.es.enter_context(nc.semaphore('s_' + e)) for e in ('pe', 'act', 'dve', 'pool')}
        self.cnt = dict.fromkeys(self.sem, 0)
        self.known = {e: {} for e in self.E}
        self.final = []
        self.nd = 0

    def sb(self, name, shape, dt):
        t = Tk(self.es.enter_context(self.nc.sbuf_tensor(name, shape, dt)), name)
        if getattr(self, 'scope_tiles', None) is not None:
            self.scope_tiles.append(t)
        return t

    def push_scope(self):
        self.saved_es = self.es
        self.es = contextlib.ExitStack()
        self.scope_tiles = []

    def pop_scope(self):
        deps = [(self.sem[e], self.cnt[e]) for e in self.sem if self.cnt[e]]
        for t in self.scope_tiles:
            if t.w:
                deps.append(t.w)
            deps.extend(t.r.values())
        for e in self.E:
            self._wait(e, deps)
        self.es.close()
        self.es = self.saved_es
        self.scope_tiles = None

    def ps(self, name, shape, dt=F32):
        t = Tk(self.es.enter_context(self.nc.psum_tensor(name, shape, dt)), name)
        t.psum = True
        return t

    def dram(self, name, shape, dt, kind):
        return Tk(self.nc.dram_tensor(name, shape, dt, kind=kind).ap(), name)

    def _wait(self, e, deps):
        kn = self.known[e]
        for sem, val in deps:
            if e == 'pe' and sem is self.sem['pe']:
                continue
            if kn.get(id(sem), 0) < val:
                self.E[e].wait_ge(sem, val)
                kn[id(sem)] = val

    @staticmethod
    def _deps(reads, writes):
        d = []
        for t in reads:
            if t.w:
                d.append(t.w)
        for t in writes:
            if t.w:
                d.append(t.w)
            d.extend(t.r.values())
        return d

    @staticmethod
    def _mark(tok, reads, writes):
        for t in reads:
            t.r[id(tok[0])] = tok
        for t in writes:
            t.w = tok
            t.r = {}

    def op(self, e, reads, writes, fn):
        writes = list(writes) + [t for t in reads if t.psum]
        self._wait(e, self._deps(reads, writes))
        ins = fn(self.E[e])
        self.cnt[e] += 1
        ins.then_inc(self.sem[e], 1)
        self._mark((self.sem[e], self.cnt[e]), reads, writes)

    def mm(self, out, out_ap, pairs, reads, **kw):
        self._wait('pe', self._deps(reads, [out]))
        n = len(pairs)
        for i, (l, r) in enumerate(pairs):
            ins = self.nc.tensor.matmul(out_ap, l, r, start=(i == 0), stop=(i == n - 1), **kw)
        self.cnt['pe'] += 1
        ins.then_inc(self.sem['pe'], 1)
        self._mark((self.sem['pe'], self.cnt['pe']), reads, [out])

    def dma(self, q, out_tk, out_ap, in_tk, in_ap, final=False, **kw):
        reads = [in_tk] if in_tk is not None else []
        writes = [out_tk] if out_tk is not None else []
        self._wait(q, self._deps(reads, writes))
        own = out_tk if out_tk is not None else in_tk
        if q not in own.dsem:
            self.nd += 1
            own.dsem[q] = self.root.enter_context(self.nc.semaphore('d%d' % self.nd))
            own.dcnt[q] = 0
        ins = self.E[q].dma_start(out=out_ap, in_=in_ap, **kw)
        own.dcnt[q] += 16
        ins.then_inc(own.dsem[q], 16)
        tok = (own.dsem[q], own.dcnt[q])
        self._mark(tok, reads, writes)
        if final:
            self.final.append(tok)

    def finish(self):
        self._wait('sp', self.final)
        self._wait('sp', [(self.sem[e], self.cnt[e]) for e in self.sem if self.cnt[e]])
        self.es.close()


def _col(t, i):
    return t[:, i:i + 1]


def build_T(NT, do_C, do_A, CT=256, TT=512, dbg=0):
    nc = bass.Bass("TRN2", target_bir_lowering=False)
    k = K(nc)
    ntile = NT // TT
    NH = 2 * ntile if do_C else 0
    NTOT = NH + NT + CT
    NOUT = NT + CT

    def din(name, shape, dt=F32):
        return nc.dram_tensor(name, shape, dt, kind="ExternalInput").ap()

    xT = din("xT", [D, NTOT]).rearrange("(c p) n -> p c n", p=128)
    if do_C:
        yT = din("yT", [D, NTOT], BF16).rearrange("(c p) n -> p c n", p=128)
        hmask = din("hmask", [128, NH])
        modC = din("modC", [128, 2 * 4 * DC])
        ngC = din("ngC", [128, 3 * DC])
        cw = din("cw", [128, 3 * FC])
        cb = din("cb", [128, FC])
        wf_in = din("wf", [1, 128, 4 * 512])
        wo_in = din("wo", [8, 128, DC * 256])
        wu_in = din("wu", [FC, 128, DC * 256])
        wd_in = din("wd", [DC, 128, FC * 128])
        xo = nc.dram_tensor("xo", [D, NOUT], F32, kind="ExternalOutput").ap().rearrange("(c p) n -> p c n", p=128)
    if do_A:
        modA = din("modA", [128, 2 * 2 * DC])
        ngA = din("ngA", [128, DC])
        wi_in = din("wi", [14, 128, DC * 256])
        po = nc.dram_tensor("po", [DIN, NOUT], BF16, kind="ExternalOutput").ap()

    def scratch(name, src):
        G, _, Fw = src.shape
        step = max(1, (1 << 20) // (128 * Fw))
        mld = max(d for d in range(1, 2049) if Fw % d == 0)
        pieces = []
        for g0 in range(0, G, step):
            g1 = min(G, g0 + step)
            tk = k.dram("%s_bf%d" % (name, g0), [g1 - g0, 128, Fw], BF16, "Internal")
            k.dma('pool', tk, tk.t[0:g1 - g0], None, src[g0:g1], max_dma_last_dim=mld)
            for g in range(g0, g1):
                pieces.append((tk, g - g0))
        return pieces

    WB = 3
    wbuf = [k.sb("wbuf%d" % i, [128, FC * 128], BF16) for i in range(WB)]
    wctr = [0]

    def wload(pieces, g, width):
        b = wbuf[wctr[0] % WB]
        wctr[0] += 1
        tk, gi = pieces[g]
        k.dma('sp', b, b.t[:, 0:width], tk, tk.t[gi])
        return b

    if do_C:
        wf_s = scratch("wf", wf_in)
        wo_s = scratch("wo", wo_in)
        wu_s = scratch("wu", wu_in)
        wd_s = scratch("wd", wd_in)
    if do_A:
        wi_s = scratch("wi", wi_in)

    ones = k.sb("ones", [128, 128], BF16)
    k.op('dve', [], [ones], lambda e: e.memset(ones.t[:], 1.0))
    epsc = k.sb("epsc", [128, 1], F32)
    k.op('dve', [], [epsc], lambda e: e.memset(epsc.t[:], EPS))
    prm = k.sb("prm", [128, 16 * DC + 4 * FC + NH], F32)
    cst = k.sb("cst", [128, 12 * DC], F32)
    o_modC, o_ngC, o_modA, o_ngA, o_cw, o_cb, o_hm = 0, 8 * DC, 11 * DC, 15 * DC, 16 * DC, 16 * DC + 3 * FC, 16 * DC + 4 * FC
    cG1, cA2, cB2, cG2, cA1, cB1 = 0, 2 * DC, 4 * DC, 6 * DC, 8 * DC, 10 * DC
    if do_C:
        k.dma('sp', prm, prm.t[:, o_modC:o_modC + 8 * DC], None, modC)
        k.dma('sp', prm, prm.t[:, o_ngC:o_ngC + 3 * DC], None, ngC)
        k.dma('sp', prm, prm.t[:, o_cw:o_cw + 3 * FC], None, cw)
        k.dma('sp', prm, prm.t[:, o_cb:o_cb + FC], None, cb)
        k.dma('sp', prm, prm.t[:, o_hm:o_hm + NH], None, hmask)
        for m in range(2):
            mo = o_modC + m * 4 * DC
            k.op('dve', [prm], [cst], lambda e, m=m, mo=mo: e.tensor_tensor(
                cst.t[:, cG1 + m * DC:cG1 + (m + 1) * DC], prm.t[:, mo:mo + DC], prm.t[:, o_ngC:o_ngC + DC], ALU.mult))
            k.op('dve', [prm], [cst], lambda e, m=m, mo=mo: e.scalar_tensor_tensor(
                cst.t[:, cA2 + m * DC:cA2 + (m + 1) * DC], prm.t[:, mo + 2 * DC:mo + 3 * DC], 1.0,
                prm.t[:, o_ngC + DC:o_ngC + 2 * DC], ALU.add, ALU.mult))
            k.op('dve', [prm], [cst], lambda e, m=m, mo=mo: e.tensor_copy(
                cst.t[:, cB2 + m * DC:cB2 + (m + 1) * DC], prm.t[:, mo + DC:mo + 2 * DC]))
            k.op('dve', [prm], [cst], lambda e, m=m, mo=mo: e.tensor_tensor(
                cst.t[:, cG2 + m * DC:cG2 + (m + 1) * DC], prm.t[:, mo + 3 * DC:mo + 4 * DC],
                prm.t[:, o_ngC + 2 * DC:o_ngC + 3 * DC], ALU.mult))
    if do_A:
        k.dma('sp', prm, prm.t[:, o_modA:o_modA + 4 * DC], None, modA)
        k.dma('sp', prm, prm.t[:, o_ngA:o_ngA + DC], None, ngA)
        for m in range(2):
            mo = o_modA + m * 2 * DC
            k.op('dve', [prm], [cst], lambda e, m=m, mo=mo: e.scalar_tensor_tensor(
                cst.t[:, cA1 + m * DC:cA1 + (m + 1) * DC], prm.t[:, mo + DC:mo + 2 * DC], 1.0,
                prm.t[:, o_ngA:o_ngA + DC], ALU.add, ALU.mult))
            k.op('dve', [prm], [cst], lambda e, m=m, mo=mo: e.tensor_copy(
                cst.t[:, cB1 + m * DC:cB1 + (m + 1) * DC], prm.t[:, mo:mo + DC]))

    xs = k.sb("xs", [128, DC, TT], F32)
    of = k.sb("of", [128, DC, TT], F32)
    hb = k.sb("hb", [128, DC, TT], BF16)
    zb = k.sb("zb", [128, 4, TT], BF16)
    ab = k.sb("ab", [128, FC, TT], BF16)
    ghalo = k.sb("ghalo", [128, FC, max(NH, 2)], F32)
    rstd = k.sb("rstd", [128, TT], F32)
    NR = 3
    sqb = [k.sb("sqb%d" % i, [128, TT], BF16) for i in range(NR)]
    tmp = [k.sb("tmp%d" % i, [128, TT], F32) for i in range(NR)]
    gbuf = [k.sb("gbuf%d" % i, [128, TT + 2], F32) for i in range(NR)]
    cva = [k.sb("cva%d" % i, [128, TT], F32) for i in range(NR)]
    cvb = [k.sb("cvb%d" % i, [128, TT], F32) for i in range(NR)]
    geb = [k.sb("geb%d" % i, [128, TT], F32) for i in range(NR)]
    pbo = [k.sb("pbo%d" % i, [128, TT], BF16) for i in range(NR)]
    pg = [k.ps("pg%d" % i, [128, TT]) for i in range(2)]
    pv = [k.ps("pv%d" % i, [128, TT]) for i in range(2)]
    pm = [k.ps("pm%d" % i, [128, TT]) for i in range(2)]
    pst = k.ps("pst", [128, TT])
    rot = {'sq': 0, 'tmp': 0, 'g': 0, 'pg': 0, 'pm': 0, 'pb': 0}

    def nxt(key, n):
        v = rot[key] % n
        rot[key] += 1
        return v

    def sumsq_sq(src_tk, src_ap, n):
        s = sqb[nxt('sq', NR)]
        k.op('act', [src_tk], [s], lambda e: e.activation(s.t[:, :n], src_ap, AF.Square))
        return s

    def sumsq_mm(s, c, n):
        k._wait('pe', k._deps([s, ones], [pst]))
        ins = nc.tensor.matmul(pst.t[:, :n], ones.t[:, :], s.t[:, :n], start=(c == 0), stop=(c == DC - 1))
        k.cnt['pe'] += 1
        ins.then_inc(k.sem['pe'], 1)
        k._mark((k.sem['pe'], k.cnt['pe']), [s, ones], [pst])

    def sumsq_chunk(src_tk, src_ap, c, n):
        sumsq_mm(sumsq_sq(src_tk, src_ap, n), c, n)

    def make_rstd(n):
        t = tmp[nxt('tmp', NR)]
        k.op('act', [pst, epsc], [t], lambda e: e.activation(t.t[:, :n], pst.t[:, :n], AF.Sqrt, bias=epsc.t[:, 0:1], scale=1.0 / D))
        k.op('dve', [t], [rstd], lambda e: e.reciprocal(rstd.t[:, :n], t.t[:, :n]))

    def resid_update(gcol0, n):
        for c in range(DC):
            t = tmp[nxt('tmp', NR)]
            k.op('dve', [of, cst, rstd], [t], lambda e, c=c, t=t: e.scalar_tensor_tensor(
                t.t[:, :n], of.t[:, c, :n], _col(cst.t, gcol0 + c), rstd.t[:, :n], ALU.mult, ALU.mult))
            k.op('pool', [t, xs], [xs], lambda e, c=c, t=t: e.tensor_tensor(xs.t[:, c, :n], xs.t[:, c, :n], t.t[:, :n], ALU.add))

    def norm_mod(acol0, bcol0, n):
        for c in range(DC):
            sumsq_chunk(xs, xs.t[:, c, :n], c, n)
        make_rstd(n)
        for c in range(DC):
            t = tmp[nxt('tmp', NR)]
            k.op('dve', [xs, cst, rstd], [t], lambda e, c=c, t=t: e.scalar_tensor_tensor(
                t.t[:, :n], xs.t[:, c, :n], _col(cst.t, acol0 + c), rstd.t[:, :n], ALU.mult, ALU.mult))
            k.op('act', [t, cst], [hb], lambda e, c=c, t=t: e.activation(
                hb.t[:, c, :n], t.t[:, :n], AF.Identity, bias=_col(cst.t, bcol0 + c), scale=1.0))

    segs = []
    if do_C and dbg < 2:
        segs.append(('halo', 0, NH, 0, -1))
    for i in range(ntile):
        segs.append(('lat', NH + i * TT, TT, 0, i))
    segs.append(('ctx', NH + NT, CT, 1, -1))

    for kind, c0, n, m, ti in segs:
        if not do_C:
            k.dma('sp', xs, xs.t[:, :, :n], None, xT[:, :, c0:c0 + n])
        if do_C:
            k.dma('sp', hb, hb.t[:, :, :n], None, yT[:, :, c0:c0 + n])
            wt = wload(wf_s, 0, 4 * 512)
            for mj in range(4 if dbg not in (3,) else 0):
                p = pm[nxt('pm', 2)]
                k.mm(p, p.t[:, :n], [(wt.t[:, kc * 512 + mj * 128:kc * 512 + (mj + 1) * 128], hb.t[:, 8 + kc, :n]) for kc in range(4)], [wt, hb])
                k.op('act', [p], [zb], lambda e, p=p, mj=mj: e.activation(zb.t[:, mj, :n], p.t[:, :n], AF.Identity))
            pend = None
            for g in range(8 if dbg not in (3, 4) else 0):
                wt = wload(wo_s, g, DC * 256)
                for j in range(2):
                    mi = 2 * g + j
                    p = pm[nxt('pm', 2)]
                    prs = []
                    for kc in range(DC):
                        rhs = zb.t[:, kc - 8, :n] if 8 <= kc < 12 else hb.t[:, kc, :n]
                        prs.append((wt.t[:, kc * 256 + j * 128:kc * 256 + (j + 1) * 128], rhs))
                    k.mm(p, p.t[:, :n], prs, [wt, hb, zb])
                    k.op('dve', [p], [of], lambda e, p=p, mi=mi: e.tensor_copy(of.t[:, mi, :n], p.t[:, :n]))
                    sq_now = sumsq_sq(of, of.t[:, mi, :n], n)
                    if pend is not None:
                        sumsq_mm(pend[0], pend[1], n)
                    pend = (sq_now, mi)
            if pend is not None:
                sumsq_mm(pend[0], pend[1], n)
            k.dma('sp', xs, xs.t[:, :, :n], None, xT[:, :, c0:c0 + n])
            if dbg not in (3, 4, 5):
                make_rstd(n)
                resid_update(cG1 + m * DC, n)
            if dbg == 0:
                norm_mod(cA2 + m * DC, cB2 + m * DC, n)
            for mi in range(FC if dbg == 0 else 0):
                wt = wload(wu_s, mi, DC * 256)
                pgi = nxt('pg', 2)
                pgt, pvt = pg[pgi], pv[pgi]
                k.mm(pgt, pgt.t[:, :n], [(wt.t[:, kc * 256:kc * 256 + 128], hb.t[:, kc, :n]) for kc in range(DC)], [wt, hb])
                if kind == 'halo':
                    k.op('dve', [pgt, prm], [ghalo], lambda e, pgt=pgt, mi=mi: e.tensor_tensor(
                        ghalo.t[:, mi, :], pgt.t[:, :n], prm.t[:, o_hm:o_hm + NH], ALU.mult))
                    continue
                k.mm(pvt, pvt.t[:, :n], [(wt.t[:, kc * 256 + 128:kc * 256 + 256], hb.t[:, kc, :n]) for kc in range(DC)], [wt, hb])
                gi = nxt('g', NR)
                gb, ca, cbb, ge = gbuf[gi], cva[gi], cvb[gi], geb[gi]
                k.op('act', [pgt], [gb], lambda e, gb=gb, pgt=pgt: e.activation(gb.t[:, 1:n + 1], pgt.t[:, :n], AF.Identity))
                if kind == 'lat':
                    k.op('pool', [ghalo], [gb], lambda e, gb=gb, mi=mi: e.tensor_copy(gb.t[:, 0:n + 2:n + 1], ghalo.t[:, mi, 2 * ti:2 * ti + 2]))
                else:
                    k.op('pool', [], [gb], lambda e, gb=gb: e.memset(gb.t[:, 0:n + 2:n + 1], 0.0))
                k.op('dve', [gb, prm], [ca], lambda e, gb=gb, ca=ca, mi=mi: e.tensor_scalar(
                    ca.t[:, :n], gb.t[:, 0:n], _col(prm.t, o_cw + mi), _col(prm.t, o_cb + mi), ALU.mult, ALU.add))
                k.op('dve', [gb, prm, ca], [cbb], lambda e, gb=gb, ca=ca, cbb=cbb, mi=mi: e.scalar_tensor_tensor(
                    cbb.t[:, :n], gb.t[:, 1:n + 1], _col(prm.t, o_cw + FC + mi), ca.t[:, :n], ALU.mult, ALU.add))
                k.op('dve', [gb, prm, cbb], [ca], lambda e, gb=gb, ca=ca, cbb=cbb, mi=mi: e.scalar_tensor_tensor(
                    ca.t[:, :n], gb.t[:, 2:n + 2], _col(prm.t, o_cw + 2 * FC + mi), cbb.t[:, :n], ALU.mult, ALU.add))
                k.op('act', [ca], [ge], lambda e, ca=ca, ge=ge: e.activation(ge.t[:, :n], ca.t[:, :n], AF.Gelu))
                k.op('dve', [ge, pvt], [ab], lambda e, ge=ge, pvt=pvt, mi=mi: e.tensor_tensor(ab.t[:, mi, :n], ge.t[:, :n], pvt.t[:, :n], ALU.mult))
            if kind == 'halo':
                continue
            pend = None
            for g in range(DC if dbg == 0 else 0):
                wt = wload(wd_s, g, FC * 128)
                p = pm[nxt('pm', 2)]
                k.mm(p, p.t[:, :n], [(wt.t[:, mi * 128:(mi + 1) * 128], ab.t[:, mi, :n]) for mi in range(FC)], [wt, ab])
                k.op('dve', [p], [of], lambda e, p=p, g=g: e.tensor_copy(of.t[:, g, :n], p.t[:, :n]))
                sq_now = sumsq_sq(of, of.t[:, g, :n], n)
                if pend is not None:
                    sumsq_mm(pend[0], pend[1], n)
                pend = (sq_now, g)
            if pend is not None:
                sumsq_mm(pend[0], pend[1], n)
            if dbg == 0:
                make_rstd(n)
                resid_update(cG2 + m * DC, n)
            k.dma('pool', None, xo[:, :, c0 - NH:c0 - NH + n], xs, xs.t[:, :, :n], final=True)
        if do_A:
            norm_mod(cA1 + m * DC, cB1 + m * DC, n)
            for g in range(14):
                wt = wload(wi_s, g, DC * 256)
                for j in range(2):
                    mi = 2 * g + j
                    p = pm[nxt('pm', 2)]
                    k.mm(p, p.t[:, :n], [(wt.t[:, kc * 256 + j * 128:kc * 256 + (j + 1) * 128], hb.t[:, kc, :n]) for kc in range(DC)], [wt, hb])
                    pb = pbo[nxt('pb', NR)]
                    k.op('act', [p], [pb], lambda e, p=p, pb=pb: e.activation(pb.t[:, :n], p.t[:, :n], AF.Identity))
                    k.dma('pool', None, po[mi * 128:(mi + 1) * 128, c0 - NH:c0 - NH + n], pb, pb.t[:, :n], final=True)
    k.finish()
    return nc


def blk(W, cols):
    Kd, M = W.shape
    KC, G = Kd // 128, M // cols
    return np.ascontiguousarray(W.reshape(KC, 128, G, cols).transpose(2, 1, 0, 3).reshape(G, 128, KC * cols))


def blk_up(W):
    Wg = W[:, :DFF].reshape(DC, 128, FC, 1, 128)
    Wv = W[:, DFF:].reshape(DC, 128, FC, 1, 128)
    Wc = np.concatenate([Wg, Wv], axis=3)
    return np.ascontiguousarray(Wc.transpose(2, 1, 0, 3, 4).reshape(FC, 128, DC * 256))


def colvec(v):
    return np.ascontiguousarray(v.reshape(-1, 128).T)


def cols(*vs):
    return np.ascontiguousarray(np.concatenate([colvec(v) for v in vs], axis=1).astype(np.float32))


def build_M(L, CT=256, TT=512):
    nc = bass.Bass("TRN2", target_bir_lowering=False)
    k = K(nc)
    SEQ = [('c', CT), ('l', L)]

    def din(name, shape, dt=F32):
        return nc.dram_tensor(name, shape, dt, kind="ExternalInput").ap()

    I = {}
    for s, Ls in SEQ:
        I['ux' + s] = din('ux' + s, [128, Ls + 3], BF16)
        I['ug' + s] = din('ug' + s, [128, Ls], BF16)
        I['up' + s] = din('up' + s, [128, Ls + 16], BF16)
        I['uf' + s] = din('uf' + s, [128, Ls], BF16)
        I['q' + s] = din('q' + s, [128, Ls], BF16)
        I['k' + s] = din('k' + s, [128, Ls], BF16)
        I['v' + s] = din('v' + s, [Ls, 128], BF16)
        I['icnt' + s] = din('icnt' + s, [128, Ls])
        L1 = Ls // 128
        I['F1' + s] = din('F1' + s, [L1, 4 * L1])
        I['Tw' + s] = din('Tw' + s, [128, 2 * L1])
        I['yo' + s] = nc.dram_tensor('yo' + s, [4, 128, Ls], BF16, kind="ExternalOutput").ap()
    lruw_in = din('lruw', [128, 512]); lrup_in = din('lrup', [128, 11])
    poolw_in = din('poolw', [128, 128]); poolp_in = din('poolp', [128, 5])
    f128_in = din('f128', [128, 384]); rm_in = din('rm', [128, 128])
    cs_in = din('cossin', [2, 128, L]); dlam_in = din('dlam', [128, 256]); attp_in = din('attp', [128, 3])

    wq = k.sb('wq', [128, 512 + 128 + 384 + 128], BF16)
    oLW, oPW, oF, oRM = 0, 512, 640, 1024
    k.dma('pool', wq, wq.t[:, oLW:oLW + 512], None, lruw_in)
    k.dma('pool', wq, wq.t[:, oPW:oPW + 128], None, poolw_in)
    k.dma('pool', wq, wq.t[:, oF:oF + 384], None, f128_in)
    k.dma('pool', wq, wq.t[:, oRM:oRM + 128], None, rm_in)
    pp = k.sb('pp', [128, 11 + 5 + 3 + 8], F32)
    oLP, oPP, oAP, oDV = 0, 11, 16, 19
    k.dma('sp', pp, pp.t[:, oLP:oLP + 11], None, lrup_in)
    k.dma('sp', pp, pp.t[:, oPP:oPP + 5], None, poolp_in)
    k.dma('sp', pp, pp.t[:, oAP:oAP + 3], None, attp_in)
    ones = k.sb("ones", [128, 128], BF16)
    k.op('dve', [], [ones], lambda e: e.memset(ones.t[:], 1.0))
    onec = k.sb("onec", [128, 2], F32)
    k.op('dve', [], [onec], lambda e: e.memset(onec.t[:, 0:1], 1.0))
    k.op('dve', [onec], [onec], lambda e: e.memset(onec.t[:, 1:2], EPS))
    sc0 = k.sb('sc0', [128, 256], F32)
    k.op('act', [pp], [sc0], lambda e: e.activation(sc0.t[:, 0:2], pp.t[:, oLP + 9:oLP + 11], AF.Exp, scale=-1.0))
    k.op('act', [sc0, onec], [sc0], lambda e: e.activation(sc0.t[:, 2:4], sc0.t[:, 0:2], AF.Ln, bias=onec.t[:, 0:1], scale=1.0))
    k.op('dve', [sc0], [pp], lambda e: e.tensor_scalar(pp.t[:, oDV:oDV + 2], sc0.t[:, 2:4], -8.0, None, ALU.mult))
    k.op('dve', [sc0], [pp], lambda e: e.tensor_scalar(pp.t[:, oDV + 2:oDV + 4], sc0.t[:, 2:4], -16.0, None, ALU.mult))
    dl = k.sb('dl', [128, 256], F32)
    k.dma('sp', dl, dl.t[:, :], None, dlam_in)
    k.op('dve', [dl], [sc0], lambda e: e.tensor_tensor(sc0.t[:, 0:64], dl.t[:, 0:64], dl.t[:, 64:128], ALU.mult))
    k.op('dve', [dl, sc0], [sc0], lambda e: e.tensor_tensor(sc0.t[:, 64:128], dl.t[:, 128:192], dl.t[:, 192:256], ALU.mult))
    k.op('dve', [sc0], [sc0], lambda e: e.reduce_sum(sc0.t[:, 128:130], sc0.t[:, 0:128].rearrange("p (a b) -> p a b", a=2), mybir.AxisListType.X))
    k.op('act', [sc0], [sc0], lambda e: e.activation(sc0.t[:, 130:132], sc0.t[:, 128:130], AF.Exp))
    k.op('dve', [sc0], [sc0], lambda e: e.tensor_tensor(sc0.t[:, 132:133], sc0.t[:, 131:132], sc0.t[:, 130:131], ALU.subtract))
    k.op('dve', [sc0, pp], [pp], lambda e: e.tensor_tensor(pp.t[:, oDV + 4:oDV + 5], sc0.t[:, 132:133], pp.t[:, oAP + 1:oAP + 2], ALU.subtract))

    k.op('dve', [pp], [pp], lambda e: e.tensor_tensor(pp.t[:, oDV + 5:oDV + 6], pp.t[:, oAP:oAP + 1], pp.t[:, oAP + 2:oAP + 3], ALU.mult))
    NR = 3
    rot = {}

    def nxt(key, n=NR):
        v = rot.get(key, 0)
        rot[key] = v + 1
        return v % n

    in_att = [False]
    ps = [k.ps("ps%d" % i, [128, 512]) if i in (0, 1, 4, 5) else None for i in range(8)]
    f32t = [k.sb("f32t%d" % i, [128, TT + 16], F32) for i in range(20)]
    bft = [k.sb("bft%d" % i, [128, TT + 16], BF16) for i in range(12)]
    outb = [k.sb("outb%d" % i, [128, TT], BF16) for i in range(NR)]

    def F():
        return f32t[nxt('f', 20)]

    def B():
        return bft[nxt('b', 12)]

    def P():
        if in_att[0]:
            return ps[nxt('p', 2)]
        return ps[(0, 1, 4, 5)[nxt('p4', 4)]]

    def store(y_ap, src_tk, src_ap):
        k.dma('pool', None, y_ap, src_tk, src_ap, final=True)

    hstate = k.sb('hstate', [128, 4], F32)
    k.push_scope()
    hf_all = k.sb('hf_all', [128, L], F32)

    def lru(s, Ls):
        yo = I['yo' + s]
        nt = (Ls + TT - 1) // TT
        tiles = [(i * TT, min(TT, Ls - i * TT)) for i in range(nt)]

        def gates(t0, n, d):
            xt = B()
            k.dma('sp', xt, xt.t[:, :n + 3], None, I['ux' + s][:, t0:t0 + n + 3])
            u = F(); u2 = F()
            k.op('dve', [xt, pp], [u], lambda e: e.tensor_scalar(u.t[:, :n], xt.t[:, 0:n], _col(pp.t, oLP + 0), _col(pp.t, oLP + 4), ALU.mult, ALU.add))
            k.op('dve', [xt, pp, u], [u2], lambda e: e.scalar_tensor_tensor(u2.t[:, :n], xt.t[:, 1:n + 1], _col(pp.t, oLP + 1), u.t[:, :n], ALU.mult, ALU.add))
            k.op('dve', [xt, pp, u2], [u], lambda e: e.scalar_tensor_tensor(u.t[:, :n], xt.t[:, 2:n + 2], _col(pp.t, oLP + 2), u2.t[:, :n], ALU.mult, ALU.add))
            k.op('dve', [xt, pp, u], [u2], lambda e: e.scalar_tensor_tensor(u2.t[:, :n], xt.t[:, 3:n + 3], _col(pp.t, oLP + 3), u.t[:, :n], ALU.mult, ALU.add))
            ub = B()
            k.op('act', [u2], [ub], lambda e: e.activation(ub.t[:, :n], u2.t[:, :n], AF.Identity))
            pr, pi = P(), P()
            k.mm(pr, pr.t[:, :n], [(wq.t[:, oLW + (2 * d) * 128:oLW + (2 * d + 1) * 128], ub.t[:, :n])], [wq, ub])
            k.mm(pi, pi.t[:, :n], [(wq.t[:, oLW + (2 * d + 1) * 128:oLW + (2 * d + 2) * 128], ub.t[:, :n])], [wq, ub])
            r = F(); ig = F(); a = F(); sq = F(); bb = F()
            k.op('act', [pr, pp], [r], lambda e: e.activation(r.t[:, :n], pr.t[:, :n], AF.Sigmoid, bias=_col(pp.t, oLP + 5 + d), scale=1.0))
            k.op('act', [pi, pp], [ig], lambda e: e.activation(ig.t[:, :n], pi.t[:, :n], AF.Sigmoid, bias=_col(pp.t, oLP + 7 + d), scale=1.0))
            k.op('act', [r, pp], [a], lambda e: e.activation(a.t[:, :n], r.t[:, :n], AF.Exp, scale=_col(pp.t, oDV + d)))
            k.op('act', [r, pp], [sq], lambda e: e.activation(sq.t[:, :n], r.t[:, :n], AF.Exp, scale=_col(pp.t, oDV + 2 + d)))
            k.op('act', [sq, onec], [sq], lambda e: e.activation(sq.t[:, :n], sq.t[:, :n], AF.Sqrt, bias=onec.t[:, 0:1], scale=-1.0))
            return a, bb, sq, ig, u2, n

        def gates_B(st):
            a, bb, sq, ig, u2, n = st
            k.op('dve', [sq, ig], [bb], lambda e: e.tensor_tensor(bb.t[:, :n], sq.t[:, :n], ig.t[:, :n], ALU.mult))
            k.op('dve', [bb, u2], [bb], lambda e: e.tensor_tensor(bb.t[:, :n], bb.t[:, :n], u2.t[:, :n], ALU.mult))
            return a, bb

        nxt_ab = gates(tiles[0][0], tiles[0][1], 0)
        for ti, (t0, n) in enumerate(tiles):
            cur = nxt_ab
            if ti + 1 < nt:
                nxt_ab = gates(tiles[ti + 1][0], tiles[ti + 1][1], 0)
            a, bb = gates_B(cur)
            if ti == 0:
                if s == 'c':
                    k.op('dve', [a, bb], [hf_all], lambda e: e.tensor_tensor_scan(hf_all.t[:, t0:t0 + n], a.t[:, :n], bb.t[:, :n], 0.0, ALU.mult, ALU.add))
                else:
                    k.op('dve', [a, bb, hstate], [hf_all], lambda e: e.tensor_tensor_scan(hf_all.t[:, t0:t0 + n], a.t[:, :n], bb.t[:, :n], hstate.t[:, 0:1], ALU.mult, ALU.add))
            else:
                k.op('dve', [a, bb, hf_all], [hf_all], lambda e: e.tensor_tensor_scan(hf_all.t[:, t0:t0 + n], a.t[:, :n], bb.t[:, :n], hf_all.t[:, t0 - 1:t0], ALU.mult, ALU.add))
        if s == 'c':
            k.op('dve', [hf_all], [hstate], lambda e: e.tensor_copy(hstate.t[:, 0:1], hf_all.t[:, Ls - 1:Ls]))
        prev = None
        nxt_ab = gates(tiles[nt - 1][0], tiles[nt - 1][1], 1)
        for ti in range(nt - 1, -1, -1):
            t0, n = tiles[ti]
            cur = nxt_ab
            if ti - 1 >= 0:
                nxt_ab = gates(tiles[ti - 1][0], tiles[ti - 1][1], 1)
            a, bb = gates_B(cur)
            hb_ = F()
            if prev is None:
                init = 0.0 if s == 'c' else hstate.t[:, 1:2]
                rd = [a, bb] if s == 'c' else [a, bb, hstate]
            else:
                init = prev.t[:, 0:1]
                rd = [a, bb, prev]
            k.op('dve', rd, [hb_], lambda e: e.tensor_tensor_scan(hb_.t[:, n - 1::-1] if False else hb_.t[:, 0:n][:, ::-1], a.t[:, 0:n][:, ::-1], bb.t[:, 0:n][:, ::-1], init, ALU.mult, ALU.add))
            prev = hb_
            gt = B(); gg = F(); hs = F(); ob = outb[nxt('o')]
            k.dma('sp', gt, gt.t[:, :n], None, I['ug' + s][:, t0:t0 + n])
            k.op('act', [gt], [gg], lambda e: e.activation(gg.t[:, :n], gt.t[:, :n], AF.Gelu))
            k.op('pool', [hb_, hf_all], [hs], lambda e: e.tensor_tensor(hs.t[:, :n], hb_.t[:, :n], hf_all.t[:, t0:t0 + n], ALU.add))
            k.op('dve', [hs, gg], [ob], lambda e: e.tensor_tensor(ob.t[:, :n], hs.t[:, :n], gg.t[:, :n], ALU.mult))
            store(yo[0, :, t0:t0 + n], ob, ob.t[:, :n])
        if s == 'c':
            k.op('dve', [prev], [hstate], lambda e: e.tensor_copy(hstate.t[:, 1:2], prev.t[:, 0:1]))

    def pool(s, Ls):
        yo = I['yo' + s]
        for t0 in range(0, Ls, TT):
            n = min(TT, Ls - t0)
            ut = B(); ic = F()
            k.dma('sp', ut, ut.t[:, :n + 16], None, I['up' + s][:, t0:t0 + n + 16])
            k.dma('sp', ic, ic.t[:, :n], None, I['icnt' + s][:, t0:t0 + n])
            w1 = F(); w2 = F(); w4 = F(); w8 = F(); ws = F(); ws2 = F()
            m = n + 16
            k.op('pool', [ut], [w1], lambda e: e.tensor_tensor(w1.t[:, 1:m], ut.t[:, 0:m - 1], ut.t[:, 1:m], ALU.add))
            k.op('pool', [w1], [w2], lambda e: e.tensor_tensor(w2.t[:, 2:m - 1], w1.t[:, 1:m - 2], w1.t[:, 3:m], ALU.add))
            k.op('pool', [w2], [w4], lambda e: e.tensor_tensor(w4.t[:, 4:m - 3], w2.t[:, 2:m - 5], w2.t[:, 6:m - 1], ALU.add))
            k.op('pool', [w4], [w8], lambda e: e.tensor_tensor(w8.t[:, 8:m - 7], w4.t[:, 4:m - 11], w4.t[:, 12:m - 3], ALU.add))
            k.op('dve', [w1, pp], [ws], lambda e: e.tensor_scalar(ws.t[:, :n], w1.t[:, 8:8 + n], _col(pp.t, oPP + 1), None, ALU.mult))
            k.op('dve', [w2, pp, ws], [ws2], lambda e: e.scalar_tensor_tensor(ws2.t[:, :n], w2.t[:, 8:8 + n], _col(pp.t, oPP + 2), ws.t[:, :n], ALU.mult, ALU.add))
            k.op('dve', [w4, pp, ws2], [ws], lambda e: e.scalar_tensor_tensor(ws.t[:, :n], w4.t[:, 8:8 + n], _col(pp.t, oPP + 3), ws2.t[:, :n], ALU.mult, ALU.add))
            k.op('dve', [w8, pp, ws], [ws2], lambda e: e.scalar_tensor_tensor(ws2.t[:, :n], w8.t[:, 8:8 + n], _col(pp.t, oPP + 4), ws.t[:, :n], ALU.mult, ALU.add))
            k.op('dve', [ws2, ic], [ws], lambda e: e.tensor_tensor(ws.t[:, :n], ws2.t[:, :n], ic.t[:, :n], ALU.mult))
            db = B()
            k.op('dve', [ws, ut], [db], lambda e: e.tensor_tensor(db.t[:, :n], ws.t[:, :n], ut.t[:, 8:8 + n], ALU.subtract))
            p = P(); ob = outb[nxt('o')]
            k.mm(p, p.t[:, :n], [(wq.t[:, oPW:oPW + 128], db.t[:, :n])], [wq, db])
            k.op('act', [p, pp], [ob], lambda e: e.activation(ob.t[:, :n], p.t[:, :n], AF.Identity, scale=_col(pp.t, oPP + 0)))
            store(yo[1, :, t0:t0 + n], ob, ob.t[:, :n])

    for s, Ls in SEQ:
        lru(s, Ls)
    k.pop_scope()
    for s, Ls in SEQ:
        pool(s, Ls)

    def fft(s, Ls):
        yo = I['yo' + s]
        L1 = Ls // 128
        scale = 1.0 / math.sqrt(Ls * 128.0)
        k.push_scope()
        ufs = k.sb('ufs' + s, [128, Ls], BF16)
        k.dma('sp', ufs, ufs.t[:, :], None, I['uf' + s])
        f1 = k.sb('f1' + s, [128, 4 * L1], BF16)
        k.dma('pool', f1, f1.t[0:L1, :], None, I['F1' + s])
        tw = k.sb('tw' + s, [128, 2 * L1], F32)
        k.dma('sp', tw, tw.t[:, :], None, I['Tw' + s])
        Ab = k.sb('Ab' + s, [128, 2 * 64 * 128], BF16)
        Bp = k.sb('Bp' + s, [128, 2 * 64 * L1], BF16)
        Yh = k.sb('Yh' + s, [128, 128 * L1], BF16)
        Av = Ab.t[:, :].rearrange("p (r c l) -> p r c l", r=2, c=64)
        Bv = Bp.t[:, :].rearrange("p (r c l) -> p r c l", r=2, c=64)
        Yv = Yh.t[:, :].rearrange("p (a b) -> p a b", b=L1)
        for h in range(2):
            for l2 in range(128):
                p = P()
                k.mm(p, p.t[0:L1, 0:64], [(ufs.t[:, l2 * L1:(l2 + 1) * L1], wq.t[:, oF + h * 64:oF + h * 64 + 64])], [ufs, wq])
                k.mm(p, p.t[0:L1, 64:128], [(ufs.t[:, l2 * L1:(l2 + 1) * L1], wq.t[:, oF + 128 + h * 64:oF + 128 + h * 64 + 64])], [ufs, wq])
                k.op('act' if l2 % 2 else 'dve', [p], [Ab], (lambda e, p=p, l2=l2: e.activation(Av[0:L1, :, :, l2], p.t[0:L1, 0:128].rearrange("p (r c) -> p r c", r=2), AF.Identity)) if l2 % 2 else
                     (lambda e, p=p, l2=l2: e.tensor_copy(Av[0:L1, :, :, l2], p.t[0:L1, 0:128].rearrange("p (r c) -> p r c", r=2))))
            for c in range(64):
                p = P()
                k.mm(p, p.t[:, 0:2 * L1], [(Av[0:L1, 0, c, :], f1.t[0:L1, 0:2 * L1]), (Av[0:L1, 1, c, :], f1.t[0:L1, 2 * L1:4 * L1])], [Ab, f1])
                br, bi = p.t[:, 0:L1], p.t[:, L1:2 * L1]
                tc, ts = tw.t[:, 0:L1], tw.t[:, L1:2 * L1]
                t1 = F(); t2 = F(); t3 = F(); t4 = F()
                k.op('dve', [p, tw], [t1], lambda e, t1=t1, br=br, tc=tc: e.tensor_tensor(t1.t[:, :L1], br, tc, ALU.mult))
                k.op('dve', [p, tw], [t2], lambda e, t2=t2, bi=bi, ts=ts: e.tensor_tensor(t2.t[:, :L1], bi, ts, ALU.mult))
                k.op('dve', [p, tw], [t3], lambda e, t3=t3, bi=bi, tc=tc: e.tensor_tensor(t3.t[:, :L1], bi, tc, ALU.mult))
                k.op('dve', [p, tw], [t4], lambda e, t4=t4, br=br, ts=ts: e.tensor_tensor(t4.t[:, :L1], br, ts, ALU.mult))
                k.op('pool', [t1, t2], [Bp], lambda e, t1=t1, t2=t2, c=c: e.tensor_tensor(Bv[:, 0, c, :], t1.t[:, :L1], t2.t[:, :L1], ALU.add))
                k.op('pool', [t3, t4], [Bp], lambda e, t3=t3, t4=t4, c=c: e.tensor_tensor(Bv[:, 1, c, :], t3.t[:, :L1], t4.t[:, :L1], ALU.subtract))
            for l1 in range(L1):
                p = P()
                k.mm(p, p.t[0:64, 0:128], [(Bv[:, 0, :, l1], wq.t[:, oF:oF + 128]), (Bv[:, 1, :, l1], wq.t[:, oF + 256:oF + 384])], [Bp, wq])
                k.op('act', [p], [Yh], lambda e, p=p, l1=l1: e.activation(Yv[h * 64:h * 64 + 64, :, l1], p.t[0:64, 0:128], AF.Identity, scale=scale))
        store(yo[2, :, :].rearrange("p (a b) -> p a b", b=L1), Yh, Yv)
        k.pop_scope()

    for s, Ls in SEQ:
        fft(s, Ls)

    k.push_scope()
    LK = CT + L
    KT = k.sb('KT', [128, LK], BF16)
    VV = k.sb('VV', [128, LK], BF16)
    QT = k.sb('QT', [128, L], BF16)
    QC = k.sb('QC', [128, CT], BF16)
    k.dma('sp', KT, KT.t[:, 0:CT], None, I['kc'])
    k.dma('sp', QC, QC.t[:, :], None, I['qc'])
    Vv = VV.t[:, :].rearrange("p (c e) -> p c e", e=128)
    k.dma('sp', VV, Vv[:, 0:CT // 128, :], None, I['vc'].rearrange("(c p) e -> p c e", p=128))
    k.dma('sp', VV, Vv[:, CT // 128:, :], None, I['vl'].rearrange("(c p) e -> p c e", p=128))
    for nm, dst, off in (('ql', QT, 0), ('kl', KT, CT)):
        for t0 in range(0, L, TT):
            n = min(TT, L - t0)
            xt = B(); ct = F(); st = F()
            k.dma('sp', xt, xt.t[:, :n], None, I[nm][:, t0:t0 + n])
            k.dma('sp', ct, ct.t[:, :n], None, cs_in[0, :, t0:t0 + n])
            k.dma('sp', st, st.t[:, :n], None, cs_in[1, :, t0:t0 + n])
            p = P()
            k.mm(p, p.t[:, :n], [(wq.t[:, oRM:oRM + 128], xt.t[:, :n])], [wq, xt])
            t1 = F(); t2 = F()
            k.op('dve', [xt, ct], [t1], lambda e, t1=t1, xt=xt, ct=ct: e.tensor_tensor(t1.t[:, :n], xt.t[:, :n], ct.t[:, :n], ALU.mult))
            k.op('dve', [p, st], [t2], lambda e, t2=t2, p=p, st=st: e.tensor_tensor(t2.t[:, :n], p.t[:, :n], st.t[:, :n], ALU.mult))
            k.op('pool', [t1, t2], [dst], lambda e, t1=t1, t2=t2, dst=dst, a=off + t0: e.tensor_tensor(dst.t[:, a:a + n], t1.t[:, :n], t2.t[:, :n], ALU.add))
    pT = [k.sb('pT%d' % i, [128, 2 * TT], BF16) for i in range(4)]
    pS = [k.ps("pS%d" % i, [128, 2 * TT]) for i in range(2)]
    zacc = [k.sb('zacc%d' % i, [128, TT], F32) for i in range(2)]
    onef = k.sb('onef', [128, 128], F32)
    k.op('dve', [], [onef], lambda e: e.memset(onef.t[:], 1.0))

    sel = k.sb('sel', [64, 256], F32)
    k.op('dve', [], [sel], lambda e: e.memset(sel.t[:, :], 0.0))
    k.op('dve', [sel], [sel], lambda e: e.memset(sel.t[0:1, 0:128], 1.0))
    k.op('dve', [sel], [sel], lambda e: e.memset(sel.t[32:33, 128:256], 1.0))

    def attend(s, Ls, Qtile, nk):
        yo = I['yo' + s]
        nch = nk // 128
        for q0 in range(0, Ls, TT):
            n = min(TT, Ls - q0)
            po0, po1, pz0, pz1 = ps[4], ps[5], ps[0], ps[1]
            pts = {}

            def emit_S(c):
                pt = pT[nxt('pt', 4)]
                p = pS[nxt('pS', 2)]
                for j in range(2):
                    k.mm(p, p.t[:, j * TT:j * TT + n], [(KT.t[64 * j:64 * j + 64, c * 128:(c + 1) * 128], Qtile.t[64 * j:64 * j + 64, q0:q0 + n])], [KT, Qtile])
                if n == TT:
                    k.op('act', [p], [pt], lambda e, p=p, pt=pt: e.activation(pt.t[:, :], p.t[:, :], AF.Exp, scale=0.125))
                else:
                    for j in range(2):
                        k.op('act', [p], [pt], lambda e, p=p, pt=pt, j=j: e.activation(pt.t[:, j * TT:j * TT + n], p.t[:, j * TT:j * TT + n], AF.Exp, scale=0.125))
                pts[c] = pt

            def emit_PV(c):
                pt = pts.pop(c)
                tgts = [(po0, po0.t[:, :n], Vv[:, c, :], 0), (po1, po1.t[:, :n], Vv[:, c, :], 1),
                        (pz0, pz0.t[0:32, :n], ones.t[:, 0:32], 0), (pz0, pz0.t[32:64, :n], ones.t[:, 0:32], 1)]
                for tgt, oap, lhs, j in tgts:
                    k._wait('pe', k._deps([pt, VV, ones], [tgt]))
                    ins = nc.tensor.matmul(oap, lhs, pt.t[:, j * TT:j * TT + n], start=(c == 0), stop=(c == nch - 1))
                    k.cnt['pe'] += 1
                    ins.then_inc(k.sem['pe'], 1)
                    k._mark((k.sem['pe'], k.cnt['pe']), [pt, VV, ones], [tgt])

            emit_S(0)
            for c in range(nch):
                if c + 1 < nch:
                    emit_S(c + 1)
                emit_PV(c)
            rzs = F(); r0 = F(); r1 = F(); o0 = F(); o1 = F(); oo = F()
            k.op('dve', [pz0], [rzs], lambda e: e.reciprocal(rzs.t[0:64, :n], pz0.t[0:64, :n]))
            for j, rj in enumerate((r0, r1)):
                pb = P()
                k.mm(pb, pb.t[:, :n], [(sel.t[0:64, j * 128:(j + 1) * 128], rzs.t[0:64, :n])], [sel, rzs])
                k.op('act', [pb], [rj], lambda e, pb=pb, rj=rj: e.activation(rj.t[:, :n], pb.t[:, :n], AF.Identity))
            k.op('dve', [po0, r0], [o0], lambda e: e.tensor_tensor(o0.t[:, :n], po0.t[:, :n], r0.t[:, :n], ALU.mult))
            k.op('dve', [po1, r1], [o1], lambda e: e.tensor_tensor(o1.t[:, :n], po1.t[:, :n], r1.t[:, :n], ALU.mult))
            k.op('dve', [o0, o1, pp], [oo], lambda e: e.scalar_tensor_tensor(oo.t[:, :n], o1.t[:, :n], _col(pp.t, oDV + 4), o0.t[:, :n], ALU.mult, ALU.add))
            sq = B(); p = P(); rs = F(); t3 = F(); ob = outb[nxt('o')]
            k.op('act', [oo], [sq], lambda e: e.activation(sq.t[:, :n], oo.t[:, :n], AF.Square))
            k.mm(p, p.t[:, :n], [(ones.t[:, :], sq.t[:, :n])], [ones, sq])
            k.op('act', [p, onec], [rs], lambda e: e.activation(rs.t[:, :n], p.t[:, :n], AF.Sqrt, bias=onec.t[:, 1:2], scale=1.0 / 128))
            k.op('dve', [rs], [t3], lambda e: e.reciprocal(t3.t[:, :n], rs.t[:, :n]))
            k.op('dve', [oo, t3, pp], [ob], lambda e: e.scalar_tensor_tensor(ob.t[:, :n], oo.t[:, :n], _col(pp.t, oDV + 5), t3.t[:, :n], ALU.mult, ALU.mult))
            store(yo[3, :, q0:q0 + n], ob, ob.t[:, :n])

    in_att[0] = True
    attend('c', CT, QC, CT)
    attend('l', L, QT, LK)
    k.pop_scope()
    k.finish()
    return nc


POOL_HALF = (1, 2, 4, 8)


def _dft(n):
    i = np.arange(n)
    ang = 2.0 * np.pi * np.outer(i, i) / n
    return np.cos(ang), np.sin(ang)


def m_consts(L, CT, j, lam_init):
    c = {}
    C, S = _dft(128)
    c['f128'] = np.concatenate([C, -S, S], 1).astype(np.float32)
    rm = np.zeros((128, 128), np.float32)
    for d in range(128):
        if d % 32 < 16:
            rm[d + 16, d] = -1.0
        else:
            rm[d - 16, d] = 1.0
    c['rm'] = rm
    t = np.arange(L)
    inv = (10000.0 ** (-np.arange(16, dtype=np.float32) / 16)).astype(np.float32)
    ang_r = (t // 64).astype(np.float32)[:, None] * inv
    ang_c = (t % 64).astype(np.float32)[:, None] * inv
    ang = np.zeros((128, L), np.float32)
    for p in range(128):
        d = p % 64
        ang[p] = ang_r[:, d % 16] if d < 32 else ang_c[:, (d - 32) % 16]
    c['cossin'] = np.stack([np.cos(ang), np.sin(ang)]).astype(np.float32)
    for s, Ls in (('c', CT), ('l', L)):
        L1 = Ls // 128
        C1, S1 = _dft(L1)
        c['F1' + s] = np.concatenate([C1, -S1, S1, C1], 1).astype(np.float32)
        a = 2.0 * np.pi * np.outer(np.arange(128), np.arange(L1)) / Ls
        c['Tw' + s] = np.concatenate([np.cos(a), np.sin(a)], 1).astype(np.float32)
        tt = np.arange(Ls)
        half = POOL_HALF[j]
        lo = np.clip(tt - half, 0, Ls - 1)
        hi = np.clip(tt + half - 1, 0, Ls - 1)
        c['icnt' + s] = np.ascontiguousarray(np.broadcast_to((1.0 / (hi - lo + 1)).astype(np.float32)[None], (128, Ls)))
    sel = np.zeros((128, 4), np.float32)
    sel[:, j] = 1.0
    c['sel'] = sel
    c['lam_init'] = np.full((128, 1), lam_init, np.float32)
    c['one_m_lam_init'] = np.full((128, 1), 1.0 - lam_init, np.float32)
    return c


def m_inputs(pT, pcT, j, L, CT, P, consts):
    inp = {}
    for s, Ls, src in (('c', CT, pcT), ('l', L, pT)):
        sl = lambda base: src[base + 128 * j: base + 128 * (j + 1)]
        z = lambda n: np.zeros((128, n), src.dtype)
        inp['ux' + s] = np.concatenate([z(2), sl(0), z(1)], 1)
        inp['ug' + s] = np.ascontiguousarray(sl(512))
        inp['up' + s] = np.concatenate([z(8), sl(1024), z(8)], 1)
        L1 = Ls // 128
        inp['uf' + s] = np.ascontiguousarray(sl(1536).reshape(128, L1, 128).transpose(0, 2, 1).reshape(128, Ls))
        inp['q' + s] = np.ascontiguousarray(sl(2048))
        inp['k' + s] = np.ascontiguousarray(sl(2560))
        inp['v' + s] = np.ascontiguousarray(sl(3072).T)
        for nm in ('icnt', 'F1', 'Tw'):
            inp[nm + s] = consts[nm + s]
    lw = np.zeros((128, 512), np.float32)
    for d in range(2):
        for gi, W in enumerate((P['lru_wa'], P['lru_wi'])):
            for a in range(2):
                lw[64 * a:64 * a + 64, (2 * d + gi) * 128 + 64 * a:(2 * d + gi) * 128 + 64 * a + 64] = W[d, 2 * j + a]
    inp['lruw'] = lw
    cs = slice(128 * j, 128 * (j + 1))
    inp['lrup'] = np.ascontiguousarray(np.stack(
        [P['lru_conv_w'][0, cs], P['lru_conv_w'][1, cs], P['lru_conv_w'][2, cs], P['lru_conv_w'][3, cs], P['lru_conv_b'][cs],
         P['lru_ba'][0, cs], P['lru_ba'][1, cs], P['lru_bi'][0, cs], P['lru_bi'][1, cs], P['lru_lam'][0, cs], P['lru_lam'][1, cs]], 1).astype(np.float32))
    inp['poolw'] = np.ascontiguousarray(P['pool_w'][j])
    inp['poolp'] = np.ascontiguousarray(np.concatenate([P['pool_scale'][cs][:, None], consts['sel']], 1).astype(np.float32))
    inp['f128'] = consts['f128']
    inp['rm'] = consts['rm']
    inp['cossin'] = consts['cossin']
    inp['dlam'] = np.ascontiguousarray(np.broadcast_to(P['diff_lam'].reshape(1, 256), (128, 256)))
    inp['attp'] = np.ascontiguousarray(np.concatenate([P['diff_subln_g'][:, None], consts['lam_init'], consts['one_m_lam_init']], 1).astype(np.float32))
    return inp


ACOLS = 6 * D // NCORES


def build_ADA(depth=4):
    nc = bass.Bass("TRN2", target_bir_lowering=False)
    k = K(nc)
    cT = nc.dram_tensor("cT", [128, DC * 3], F32, kind="ExternalInput").ap()
    aw = nc.dram_tensor("aw", [depth * DC, 128, ACOLS], F32, kind="ExternalInput").ap()
    ab_in = nc.dram_tensor("ab", [3, depth * ACOLS], F32, kind="ExternalInput").ap()
    mo = nc.dram_tensor("mo", [3, depth * ACOLS], F32, kind="ExternalOutput").ap()
    ct = k.sb("ct", [128, DC * 3], F32)
    sc = k.sb("sc", [128, DC * 3], F32)
    bias = k.sb("bias", [3, depth * ACOLS], F32)
    res = k.sb("res", [3, depth * ACOLS], F32)
    k.dma('sp', ct, ct.t[:, :], None, cT)
    k.dma('sp', bias, bias.t[:, :], None, ab_in)
    k.op('act', [ct], [sc], lambda e: e.activation(sc.t[:, :], ct.t[:, :], AF.Silu))
    wb = [k.sb("awb%d" % i, [128, ACOLS], F32) for i in range(4)]
    pss = [k.ps("aps%d" % i, [128, 512]) for i in range(3)]
    for l in range(depth):
        for kc in range(DC):
            w = wb[(l * DC + kc) % 4]
            k.dma('sp', w, w.t[:, :], None, aw[l * DC + kc])
            for g in range(3):
                k._wait('pe', k._deps([w, sc], [pss[g]]))
                ins = nc.tensor.matmul(pss[g].t[0:3, :], sc.t[:, kc * 3:kc * 3 + 3], w.t[:, g * 512:(g + 1) * 512], start=(kc == 0), stop=(kc == DC - 1))
                k.cnt['pe'] += 1
                ins.then_inc(k.sem['pe'], 1)
                k._mark((k.sem['pe'], k.cnt['pe']), [w, sc], [pss[g]])
        for g in range(3):
            o = l * ACOLS + g * 512
            k.op('dve', [pss[g], bias], [res], lambda e, g=g, o=o: e.tensor_tensor(res.t[0:3, o:o + 512], pss[g].t[0:3, :], bias.t[0:3, o:o + 512], ALU.add))
    k.dma('pool', None, mo, res, res.t[:, :], final=True)
    k.finish()
    return nc


_CACHE = {}


def _prog(key, fn):
    if key not in _CACHE:
        _CACHE[key] = fn()
    return _CACHE[key]


def kernel(x, c, ctx, c_ctx, ada_w, ada_b, norm_g, w_in, lru_conv_w, lru_conv_b, lru_wa, lru_ba, lru_wi, lru_bi,
           lru_lam, pool_w, pool_scale, fourier_w, diff_lam, diff_subln_g, w_out, ffn_w_up, ffn_conv_w, ffn_conv_b,
           ffn_w_down):
    A = lambda a: np.asarray(a)
    x, c, ctx, c_ctx = A(x), A(c), A(ctx), A(c_ctx)
    ada_w, ada_b, norm_g, w_in = A(ada_w), A(ada_b), A(norm_g), A(w_in)
    Bn, L, _ = x.shape
    CT = ctx.shape[1]
    depth = w_in.shape[0]
    NS = NCORES // Bn
    NT = L // NS
    TT = 512
    ntile = NT // TT
    cores = list(range(NCORES))
    PL = [dict(lru_conv_w=A(lru_conv_w)[l], lru_conv_b=A(lru_conv_b)[l], lru_wa=A(lru_wa)[l], lru_ba=A(lru_ba)[l],
               lru_wi=A(lru_wi)[l], lru_bi=A(lru_bi)[l], lru_lam=A(lru_lam)[l], pool_w=A(pool_w)[l],
               pool_scale=A(pool_scale)[l], diff_lam=A(diff_lam)[l], diff_subln_g=A(diff_subln_g)[l]) for l in range(depth)]

    c3 = np.concatenate([c, c_ctx[None]], 0).astype(np.float32)
    cT = np.ascontiguousarray(c3.reshape(3, DC, 128).transpose(2, 1, 0).reshape(128, DC * 3))
    maps = []
    for r in cores:
        cs = slice(r * ACOLS, (r + 1) * ACOLS)
        aw = np.ascontiguousarray(ada_w[:, :, cs].reshape(depth * DC, 128, ACOLS))
        ab = np.ascontiguousarray(np.broadcast_to(ada_b[:, cs].reshape(1, depth * ACOLS), (3, depth * ACOLS)))
        maps.append(dict(cT=cT, aw=aw, ab=ab))
    res = run_bass_kernel_spmd(_prog('ada', lambda: build_ADA(depth)), maps, core_ids=cores)
    mod = np.concatenate([res.results[r]['mo'].reshape(3, depth, ACOLS) for r in cores], axis=2)
    mod = mod.reshape(3, depth, 6, D)
    del maps, res

    xT = [np.ascontiguousarray(x[b].T) for b in range(Bn)]
    xcT = [np.ascontiguousarray(ctx[b].T) for b in range(Bn)]

    def t_launch(lC, lA, yT, ycT):
        do_C, do_A = lC is not None, lA is not None
        NH = 2 * ntile if do_C else 0
        common = {}
        if do_C:
            common.update(ngC=cols(norm_g[lC, 1], norm_g[lC, 2], norm_g[lC, 3]),
                          cw=np.ascontiguousarray(A(ffn_conv_w)[lC].reshape(3, FC, 128).transpose(2, 0, 1).reshape(128, 3 * FC)),
                          cb=colvec(A(ffn_conv_b)[lC]), wf=blk(A(fourier_w)[lC], 512), wo=blk(A(w_out)[lC], 256),
                          wu=blk_up(A(ffn_w_up)[lC]), wd=blk(A(ffn_w_down)[lC], 128))
        if do_A:
            common.update(ngA=cols(norm_g[lA, 0]), wi=blk(w_in[lA], 256))
        maps = []
        for r in cores:
            b, s = divmod(r, NS)
            t0 = s * NT
            m = dict(common)
            hm = np.zeros(max(NH, 1), np.float32)
            hidx = []
            for i in range(ntile if do_C else 0):
                for hi, tpos in enumerate((t0 + i * TT - 1, t0 + (i + 1) * TT)):
                    ok = 0 <= tpos < L
                    hm[2 * i + hi] = 1.0 if ok else 0.0
                    hidx.append(tpos if ok else 0)
            xh = xT[b][:, hidx] if do_C else np.zeros((D, 0), np.float32)
            m['xT'] = np.ascontiguousarray(np.concatenate([xh, xT[b][:, t0:t0 + NT], xcT[b]], 1))
            if do_C:
                m['yT'] = np.ascontiguousarray(np.concatenate([yT[b][:, hidx], yT[b][:, t0:t0 + NT], ycT[b]], 1))
                m['hmask'] = np.ascontiguousarray(np.broadcast_to(hm[None, :NH], (128, NH)))
                m['modC'] = cols(*[mod[row, lC, i] for row in (b, 2) for i in (2, 3, 4, 5)])
            if do_A:
                m['modA'] = cols(*[mod[row, lA, i] for row in (b, 2) for i in (0, 1)])
            maps.append(m)
        res = run_bass_kernel_spmd(_prog(('T', NT, do_C, do_A, CT), lambda: build_T(NT, do_C, do_A, CT, TT)), maps, core_ids=cores)
        pT = pcT = None
        if do_C:
            for b in range(Bn):
                xT[b] = np.ascontiguousarray(np.concatenate([res.results[b * NS + s]['xo'][:, :NT] for s in range(NS)], 1))
                xcT[b] = np.ascontiguousarray(res.results[b * NS]['xo'][:, NT:])
        if do_A:
            pT = [np.concatenate([res.results[b * NS + s]['po'][:, :NT] for s in range(NS)], 1) for b in range(Bn)]
            pcT = [res.results[b * NS]['po'][:, NT:] for b in range(Bn)]
        return pT, pcT

    pT, pcT = t_launch(None, 0, None, None)
    for l in range(depth):
        lam_init = 0.8 - 0.6 * math.exp(-0.3 * l)
        maps = []
        for r in cores:
            b, j = divmod(r, NS)
            maps.append(m_inputs(pT[b], pcT[b], j, L, CT, PL[l], m_consts(L, CT, j, lam_init)))
        res = run_bass_kernel_spmd(_prog(('M', L, CT), lambda: build_M(L, CT, TT)), maps, core_ids=cores)
        yT, ycT = [], []
        for b in range(Bn):
            yl = np.stack([res.results[b * NS + j]['yol'] for j in range(NS)], 1)
            yc = np.stack([res.results[b * NS + j]['yoc'] for j in range(NS)], 1)
            yT.append(yl.reshape(D, L))
            ycT.append(yc.reshape(D, CT))
        del maps, res
        pT, pcT = t_launch(l, l + 1 if l + 1 < depth else None, yT, ycT)
    out = np.stack([xT[b].T for b in range(Bn)], 0).astype(np.float32)
    return np.ascontiguousarray(out)
```
